# Optimizing a Trainium2 kernel written in Bass

```python
import jax
import jax.numpy as jnp
from jax import lax
import numpy as np

D_MODEL = 4096
BATCH = 4
SEQ = 2048
DEPTH = 2

GRID_W = 64
CTX_LEN = 256
MIX_W = D_MODEL
N_MIXERS = 4
GROUP_W = MIX_W // N_MIXERS
HEAD_DIM = 128
N_GROUP_HEADS = GROUP_W // HEAD_DIM
GQA_KV_HEADS = 2
GQA_KV_W = GQA_KV_HEADS * HEAD_DIM
CONV_WIDTH = 31
CHUNK = 128
Q_BLOCK = 128
NA_WIN_R = 8
NA_WIN_C = 16
ROPE_THETA = 10000.0
NORM_EPS = 1e-6
LN_EPS = 1e-5
ATTN_SCALE = HEAD_DIM ** -0.5
MASK_VALUE = -1e30
RET_BASE_EXP = 5.0

COL_SIZES = (
    2 * GROUP_W, GROUP_W,
    GROUP_W, GQA_KV_W, GQA_KV_W, GROUP_W,
    GROUP_W, GROUP_W, GROUP_W, GROUP_W,
    GROUP_W, GROUP_W, GROUP_W, GROUP_W,
)
P_TOTAL = 13 * GROUP_W + 2 * GQA_KV_W

kernel_name = 'hybrid_conv_gqa_retention_natten_dit'


def _rmsnorm(x, g):
    xf = x.astype(jnp.float32)
    y = xf * lax.rsqrt(jnp.mean(xf * xf, axis=-1, keepdims=True) + NORM_EPS)
    return (y * g.astype(jnp.float32)).astype(x.dtype)


def _layernorm(x, g, b):
    xf = x.astype(jnp.float32)
    mu = jnp.mean(xf, axis=-1, keepdims=True)
    var = jnp.mean(jnp.square(xf - mu), axis=-1, keepdims=True)
    y = (xf - mu) * lax.rsqrt(var + LN_EPS)
    return (y * g.astype(jnp.float32) + b.astype(jnp.float32)).astype(x.dtype)


def _split_cols(p):
    out = []
    start = 0
    for n in COL_SIZES:
        out.append(p[..., start:start + n])
        start += n
    return out


def _to_heads(a, n_heads):
    bsz, n_tok, _ = a.shape
    return a.reshape(bsz, n_tok, n_heads, HEAD_DIM).transpose(0, 2, 1, 3)


def _from_heads(a):
    bsz, n_heads, n_tok, dh = a.shape
    return a.transpose(0, 2, 1, 3).reshape(bsz, n_tok, n_heads * dh)


def _axial_rope_tables(n_tok):
    t = jnp.arange(n_tok)
    row = (t // GRID_W).astype(jnp.float32)
    col = (t % GRID_W).astype(jnp.float32)
    half = HEAD_DIM // 2
    inv_freq = ROPE_THETA ** (-jnp.arange(0, half, 2, dtype=jnp.float32) / half)
    ang_r = row[:, None] * inv_freq[None, :]
    ang_c = col[:, None] * inv_freq[None, :]
    ang = jnp.concatenate([ang_r, ang_r, ang_c, ang_c], axis=-1)
    return jnp.cos(ang), jnp.sin(ang)


def _apply_rope(x, cos, sin):
    x1, x2, x3, x4 = jnp.split(x, 4, axis=-1)
    rot = jnp.concatenate([-x2, x1, -x4, x3], axis=-1)
    return x * cos.astype(x.dtype) + rot * sin.astype(x.dtype)


def _softmax_attend(q, k, v):
    s = jnp.einsum('bkgqd,bksd->bkgqs', q, k).astype(jnp.float32) * ATTN_SCALE
    p = jax.nn.softmax(s, axis=-1).astype(v.dtype)
    return jnp.einsum('bkgqs,bksd->bkgqd', p, v)


def _conv_mixer(glu, gate, w, b, ln_g, ln_b):
    a, g = jnp.split(glu, 2, axis=-1)
    u = a * jax.nn.sigmoid(g)
    u = lax.conv_general_dilated(
        u, w[:, None, :], window_strides=(1,),
        padding=[(CONV_WIDTH // 2, CONV_WIDTH // 2)],
        dimension_numbers=('NWC', 'WIO', 'NWC'),
        feature_group_count=GROUP_W) + b
    u = _layernorm(u, ln_g, ln_b)
    return jax.nn.silu(u) * jax.nn.silu(gate)


def _gqa_mixer(lat, ctx, qn_g, kn_g, cos, sin, with_ctx):
    ql, kl, vl, gl = lat
    qc, kc, vc, gc = ctx
    bsz, n_tok, _ = ql.shape
    n_grp = N_GROUP_HEADS // GQA_KV_HEADS
    ql_h = _apply_rope(_rmsnorm(_to_heads(ql, N_GROUP_HEADS), qn_g), cos, sin)
    kl_h = _apply_rope(_rmsnorm(_to_heads(kl, GQA_KV_HEADS), kn_g), cos, sin)
    kc_h = _rmsnorm(_to_heads(kc, GQA_KV_HEADS), kn_g)
    vl_h = _to_heads(vl, GQA_KV_HEADS)
    vc_h = _to_heads(vc, GQA_KV_HEADS)
    k_all = jnp.concatenate([kc_h, kl_h], axis=2)
    v_all = jnp.concatenate([vc_h, vl_h], axis=2)
    n_blk = n_tok // Q_BLOCK
    qb = ql_h.reshape(bsz, GQA_KV_HEADS, n_grp, n_blk, Q_BLOCK, HEAD_DIM).transpose(3, 0, 1, 2, 4, 5)
    ob = lax.map(lambda q_blk: _softmax_attend(q_blk, k_all, v_all), qb)
    ol = ob.transpose(1, 2, 3, 0, 4, 5).reshape(bsz, N_GROUP_HEADS, n_tok, HEAD_DIM)
    y_lat = _from_heads(ol) * jax.nn.silu(gl)
    y_ctx = None
    if with_ctx:
        n_ctx = qc.shape[1]
        qc_h = _rmsnorm(_to_heads(qc, N_GROUP_HEADS), qn_g)
        oc = _softmax_attend(qc_h.reshape(bsz, GQA_KV_HEADS, n_grp, n_ctx, HEAD_DIM), kc_h, vc_h)
        y_ctx = _from_heads(oc.reshape(bsz, N_GROUP_HEADS, n_ctx, HEAD_DIM)) * jax.nn.silu(gc)
    return y_lat, y_ctx


def _chunk_retention(q, k, v, log_g, s0):
    bsz, n_heads, n_tok, dh = q.shape
    n_chunk = n_tok // CHUNK
    idx = jnp.arange(CHUNK, dtype=jnp.float32)
    diff = idx[:, None] - idx[None, :]
    decay_in = jnp.where(diff[None] >= 0,
                         jnp.exp(jnp.maximum(diff, 0.0)[None] * log_g[:, None, None]), 0.0)
    q_dec = jnp.exp((idx + 1.0)[None, :] * log_g[:, None])[None, :, :, None]
    k_dec = jnp.exp((CHUNK - 1.0 - idx)[None, :] * log_g[:, None])[None, :, :, None]
    c_dec = jnp.exp(CHUNK * log_g)[None, :, None, None]

    def chunks(a):
        return a.reshape(bsz, n_heads, n_chunk, CHUNK, dh).transpose(2, 0, 1, 3, 4)

    def step(s, qkv):
        qc, kc, vc = qkv
        att = jnp.einsum('bhid,bhjd->bhij', qc, kc) * decay_in
        o = jnp.einsum('bhij,bhjd->bhid', att, vc) + jnp.einsum('bhid,bhde->bhie', qc * q_dec, s)
        s = s * c_dec + jnp.einsum('bhjd,bhje->bhde', kc * k_dec, vc)
        return s, o

    s_fin, o = lax.scan(step, s0, (chunks(q), chunks(k), chunks(v)))
    o = o.transpose(1, 2, 0, 3, 4).reshape(bsz, n_heads, n_tok, dh)
    return o, s_fin


def _retention_groupnorm(o, g, b, dtype):
    mu = jnp.mean(o, axis=-1, keepdims=True)
    var = jnp.mean(jnp.square(o - mu), axis=-1, keepdims=True)
    on = _from_heads((o - mu) * lax.rsqrt(var + LN_EPS))
    return (on * g.astype(jnp.float32) + b.astype(jnp.float32)).astype(dtype)


def _retention_mixer(lat, ctx, dec_f, dec_b, gn_g, gn_b, cos, sin, with_ctx):
    ql, kl, vl, gl = lat
    qc, kc, vc, gc = ctx
    f32 = jnp.float32
    ql_h = _apply_rope(_to_heads(ql, N_GROUP_HEADS), cos, sin).astype(f32)
    kl_h = (_apply_rope(_to_heads(kl, N_GROUP_HEADS), cos, sin) * ATTN_SCALE).astype(f32)
    vl_h = _to_heads(vl, N_GROUP_HEADS).astype(f32)
    qc_h = _to_heads(qc, N_GROUP_HEADS).astype(f32)
    kc_h = (_to_heads(kc, N_GROUP_HEADS) * ATTN_SCALE).astype(f32)
    vc_h = _to_heads(vc, N_GROUP_HEADS).astype(f32)
    lg_f = jax.nn.log_sigmoid(dec_f.astype(f32))
    lg_b = jax.nn.log_sigmoid(dec_b.astype(f32))
    zero = jnp.zeros((ql.shape[0], N_GROUP_HEADS, HEAD_DIM, HEAD_DIM), f32)

    def rev(a):
        return jnp.flip(a, axis=2)

    oc_f, sc_f = _chunk_retention(qc_h, kc_h, vc_h, lg_f, zero)
    oc_b, sc_b = _chunk_retention(rev(qc_h), rev(kc_h), rev(vc_h), lg_b, zero)
    ol_f, _ = _chunk_retention(ql_h, kl_h, vl_h, lg_f, sc_f)
    ol_b, _ = _chunk_retention(rev(ql_h), rev(kl_h), rev(vl_h), lg_b, sc_b)
    y_lat = _retention_groupnorm(ol_f + rev(ol_b), gn_g, gn_b, gl.dtype) * jax.nn.silu(gl)
    y_ctx = None
    if with_ctx:
        y_ctx = _retention_groupnorm(oc_f + rev(oc_b), gn_g, gn_b, gc.dtype) * jax.nn.silu(gc)
    return y_lat, y_ctx


def _na_mixer(lat, ctx, bias_tab, with_ctx):
    ql, kl, vl, gl = lat
    qc, kc, vc, gc = ctx
    bsz, n_tok, _ = ql.shape
    rows = n_tok // GRID_W
    kr = min(NA_WIN_R, rows)
    q_g = _to_heads(ql, N_GROUP_HEADS).reshape(bsz, N_GROUP_HEADS, rows, GRID_W, HEAD_DIM)
    k_g = _to_heads(kl, N_GROUP_HEADS).reshape(bsz, N_GROUP_HEADS, rows, GRID_W, HEAD_DIM)
    v_g = _to_heads(vl, N_GROUP_HEADS).reshape(bsz, N_GROUP_HEADS, rows, GRID_W, HEAD_DIM)
    kc_h = _to_heads(kc, N_GROUP_HEADS)
    vc_h = _to_heads(vc, N_GROUP_HEADS)
    r = jnp.arange(rows)
    r_start = jnp.clip(r - kr // 2, 0, rows - kr)
    key_rows = r_start[:, None] + jnp.arange(kr)[None, :]
    k_nb = k_g[:, :, key_rows]
    v_nb = v_g[:, :, key_rows]
    s_nb = jnp.einsum('bhrqd,bhrjkd->bhrqjk', q_g, k_nb).astype(jnp.float32) * ATTN_SCALE
    col = jnp.arange(GRID_W)
    c_start = jnp.clip(col - NA_WIN_C // 2, 0, GRID_W - NA_WIN_C)
    in_win = (col[None, :] >= c_start[:, None]) & (col[None, :] < c_start[:, None] + NA_WIN_C)
    off_r = key_rows - r[:, None] + (NA_WIN_R - 1)
    off_c = jnp.clip(col[None, :] - col[:, None], -(NA_WIN_C - 1), NA_WIN_C - 1) + (NA_WIN_C - 1)
    bias = bias_tab[:, off_r[:, None, :, None], off_c[None, :, None, :]]
    s_nb = s_nb + bias[None].astype(jnp.float32)
    s_nb = jnp.where(in_win[None, None, None, :, None, :], s_nb, MASK_VALUE)
    s_ctx = jnp.einsum('bhrqd,bhsd->bhrqs', q_g, kc_h).astype(jnp.float32) * ATTN_SCALE
    n_nb = kr * GRID_W
    s = jnp.concatenate([s_nb.reshape(bsz, N_GROUP_HEADS, rows, GRID_W, n_nb), s_ctx], axis=-1)
    p = jax.nn.softmax(s, axis=-1).astype(vl.dtype)
    p_nb = p[..., :n_nb].reshape(bsz, N_GROUP_HEADS, rows, GRID_W, kr, GRID_W)
    p_ctx = p[..., n_nb:]
    o = (jnp.einsum('bhrqjk,bhrjkd->bhrqd', p_nb, v_nb)
         + jnp.einsum('bhrqs,bhsd->bhrqd', p_ctx, vc_h))
    o = o.reshape(bsz, N_GROUP_HEADS, n_tok, HEAD_DIM)
    y_lat = _from_heads(o) * jax.nn.silu(gl)
    y_ctx = None
    if with_ctx:
        n_ctx = qc.shape[1]
        qc_h = _to_heads(qc, N_GROUP_HEADS).reshape(bsz, N_GROUP_HEADS, 1, n_ctx, HEAD_DIM)
        oc = _softmax_attend(qc_h, kc_h, vc_h).reshape(bsz, N_GROUP_HEADS, n_ctx, HEAD_DIM)
        y_ctx = _from_heads(oc) * jax.nn.silu(gc)
    return y_lat, y_ctx


def setup_inputs(seed: int = 0) -> dict:
    key = jax.random.key(seed)
    ks = jax.random.split(key, 21)
    f32 = jnp.float32

    def nrm(k, shape, s):
        return s * jax.random.normal(k, shape, f32)

    decay_base = jnp.log(2.0 ** (RET_BASE_EXP + jnp.arange(N_GROUP_HEADS, dtype=f32)) - 1.0)
    return {
        'x': nrm(ks[0], (BATCH, SEQ, D_MODEL), 1.0),
        'c': nrm(ks[1], (BATCH, D_MODEL), 1.0),
        'ctx': nrm(ks[2], (BATCH, CTX_LEN, D_MODEL), 1.0),
        'c_ctx': nrm(ks[3], (D_MODEL,), 1.0),
        'ada_w': nrm(ks[4], (DEPTH, D_MODEL, 3 * D_MODEL), 0.5 * D_MODEL ** -0.5),
        'ada_b': nrm(ks[5], (DEPTH, 3 * D_MODEL), 0.01),
        'norm_g': 1.0 + nrm(ks[6], (DEPTH, D_MODEL), 0.02),
        'w_in': nrm(ks[7], (DEPTH, D_MODEL, P_TOTAL), D_MODEL ** -0.5),
        'conv_w': nrm(ks[8], (DEPTH, CONV_WIDTH, GROUP_W), CONV_WIDTH ** -0.5),
        'conv_b': nrm(ks[9], (DEPTH, GROUP_W), 0.01),
        'conv_ln_g': 1.0 + nrm(ks[10], (DEPTH, GROUP_W), 0.02),
        'conv_ln_b': nrm(ks[11], (DEPTH, GROUP_W), 0.01),
        'gqa_qn_g': 1.0 + nrm(ks[12], (DEPTH, HEAD_DIM), 0.02),
        'gqa_kn_g': 1.0 + nrm(ks[13], (DEPTH, HEAD_DIM), 0.02),
        'ret_decay_fwd': decay_base[None, :] + nrm(ks[14], (DEPTH, N_GROUP_HEADS), 0.1),
        'ret_decay_bwd': decay_base[None, :] + nrm(ks[15], (DEPTH, N_GROUP_HEADS), 0.1),
        'ret_gn_g': 1.0 + nrm(ks[16], (DEPTH, GROUP_W), 0.02),
        'ret_gn_b': nrm(ks[17], (DEPTH, GROUP_W), 0.01),
        'na_bias': nrm(ks[18], (DEPTH, N_GROUP_HEADS, 2 * NA_WIN_R - 1, 2 * NA_WIN_C - 1), 0.02),
        'w_out': nrm(ks[19], (DEPTH, MIX_W, D_MODEL), MIX_W ** -0.5),
        'final_g': 1.0 + nrm(ks[20], (D_MODEL,), 0.02),
    }


def reference(x, c, ctx, c_ctx, ada_w, ada_b, norm_g, w_in, conv_w, conv_b, conv_ln_g, conv_ln_b,
              gqa_qn_g, gqa_kn_g, ret_decay_fwd, ret_decay_bwd, ret_gn_g, ret_gn_b, na_bias,
              w_out, final_g):
    n_tok = x.shape[1]
    cos, sin = _axial_rope_tables(n_tok)
    h_ctx = ctx
    for l in range(DEPTH):
        with_ctx = l < DEPTH - 1
        shift, scale, gate = jnp.split(jax.nn.silu(c) @ ada_w[l] + ada_b[l], 3, axis=-1)
        shift_c, scale_c, gate_c = jnp.split(jax.nn.silu(c_ctx) @ ada_w[l] + ada_b[l], 3, axis=-1)
        hl = _rmsnorm(x, norm_g[l]) * (1.0 + scale[:, None, :]) + shift[:, None, :]
        hc = _rmsnorm(h_ctx, norm_g[l]) * (1.0 + scale_c) + shift_c
        pl = _split_cols(hl @ w_in[l])
        pc = _split_cols(hc @ w_in[l])
        ya_l = _conv_mixer(pl[0], pl[1], conv_w[l], conv_b[l], conv_ln_g[l], conv_ln_b[l])
        yb_l, yb_c = _gqa_mixer(pl[2:6], pc[2:6], gqa_qn_g[l], gqa_kn_g[l], cos, sin, with_ctx)
        yc_l, yc_c = _retention_mixer(pl[6:10], pc[6:10], ret_decay_fwd[l], ret_decay_bwd[l],
                                      ret_gn_g[l], ret_gn_b[l], cos, sin, with_ctx)
        yd_l, yd_c = _na_mixer(pl[10:14], pc[10:14], na_bias[l], with_ctx)
        y_l = jnp.concatenate([ya_l, yb_l, yc_l, yd_l], axis=-1)
        x = x + gate[:, None, :] * (y_l @ w_out[l])
        if with_ctx:
            ya_c = _conv_mixer(pc[0], pc[1], conv_w[l], conv_b[l], conv_ln_g[l], conv_ln_b[l])
            y_c = jnp.concatenate([ya_c, yb_c, yc_c, yd_c], axis=-1)
            h_ctx = h_ctx + gate_c * (y_c @ w_out[l])
    return _rmsnorm(x, final_g)
```

```python
import contextlib
import numpy as np
import ml_dtypes
import concourse.bass as bass
import concourse.mybir as mybir
from concourse.bass_utils import run_bass_kernel_spmd

F32 = mybir.dt.float32
BF16 = mybir.dt.bfloat16
AF = mybir.ActivationFunctionType
ALU = mybir.AluOpType
AX = mybir.AxisListType

D = 4096
SEQ = 2048
CTX = 256
TALL = SEQ + CTX
DEPTH = 2
GW = 1024
HD = 128
PT = 13824
GRID_W = 64
SCALE = HD ** -0.5
NEPS = 1e-6
LEPS = 1e-5
SAME_ENG_SYNC = True

OFF = dict(A_a=0, A_g=1024, A_gate=2048, B_q=3072, B_k=4096, B_v=4352, B_gate=4608,
           C_q=5632, C_k=6656, C_v=7680, C_gate=8704, D_q=9728, D_k=10752, D_v=11776, D_gate=12800)


class _Op:
    __slots__ = ("eng", "fn", "deps", "dsem", "cnt", "sig", "idx")


class _Rec:
    def __init__(self):
        self.call = None

    def __getattr__(self, name):
        def f(*a, **k):
            self.call = (name, a, k)
            return self
        return f


class Prog:
    def __init__(self, nc):
        self.nc = nc
        self.ops = []
        self.state = {}
        self.dcnt = {}
        self.sb_off = 20480
        self.sb_marks = []
        self.ntens = 0
        self.debug = False
        self.last_dma = {}

    def sb(self, shape, dtype, name=None):
        esz = 4 if dtype == F32 else 2
        n = 1
        for s in shape[1:]:
            n *= s
        nbytes = (n * esz + 31) // 32 * 32
        self.ntens += 1
        t = self.nc.alloc_sbuf_tensor_at(f"{name or 't'}_{self.ntens}", list(shape), dtype, offset=self.sb_off)
        self.sb_off += nbytes
        assert self.sb_off <= 229376, f"SBUF overflow {self.sb_off} at {name}"
        return t

    def mark(self):
        self.sb_marks.append(self.sb_off)

    def release(self):
        self.sb_off = self.sb_marks.pop()

    def _add(self, eng, fn, reads, writes, dsem=None):
        op = _Op()
        if fn is not None:
            rec = _Rec()
            fn(rec)
            c = rec.call
            fn = lambda e, c=c: getattr(e, c[0])(*c[1], **c[2])
        op.eng = eng; op.fn = fn; op.dsem = dsem; op.idx = len(self.ops); op.sig = None
        deps = []

        def push(lst, o):
            lst[:] = [w for w in lst if not ((w.dsem is None and o.dsem is None and w.eng == o.eng)
                                             or (w.dsem is not None and w.dsem == o.dsem))]
            lst.append(o)
        for k in reads:
            st = self.state.setdefault(k, [[], [], []])
            deps.extend(st[1])
            push(st[2], op)
        for k in writes:
            st = self.state.setdefault(k, [[], [], []])
            if st[2]:
                st[0] = st[2]; old_w = st[1]; st[1] = [op]; st[2] = []
                deps.extend(st[0]); deps.extend(old_w)
            else:
                deps.extend(st[0]); deps.extend(st[1])
                push(st[1], op)
        if dsem is not None:
            prev = self.last_dma.get(dsem)
            if prev is not None:
                deps.append(prev)
            self.last_dma[dsem] = op
        op.deps = [d for d in deps if d is not op]
        if dsem is not None:
            self.dcnt[dsem] = self.dcnt.get(dsem, 0) + 16
            op.cnt = self.dcnt[dsem]
        self.ops.append(op)
        return op

    def pe(self, fn, reads=(), writes=()):
        return self._add("pe", fn, reads, writes)

    def act(self, fn, reads=(), writes=()):
        return self._add("act", fn, reads, writes)

    def dve(self, fn, reads=(), writes=()):
        return self._add("dve", fn, reads, writes)

    def pool(self, fn, reads=(), writes=()):
        return self._add("pool", fn, reads, writes)

    def dma(self, q, dsem, out, in_, reads=(), writes=(), **kw):
        return self._add(q, lambda e: e.dma_start(out=out, in_=in_, **kw), reads, writes, dsem=dsem)

    def barrier(self):
        alld = {}
        for st in self.state.values():
            for lst in st:
                for o in lst:
                    alld[o.idx] = o
        alld = list(alld.values())
        for eng in ("pe", "act", "dve", "pool", "sp"):
            op = self._add(eng, None, (), ())
            op.deps = list(alld)
        self.state = {}

    def wait_all(self, eng, keys):
        return self._add(eng, None, list(keys), ())

    def emit(self):
        nc = self.nc
        ops = self.ops
        need = set()
        for op in ops:
            for d in op.deps:
                if d.dsem is None:
                    if d.eng == op.eng and op.dsem is None and (d.eng == "pe" or not SAME_ENG_SYNC):
                        continue
                    need.add(d.idx)
        cnt = {e: 0 for e in ("pe", "act", "dve", "pool", "sp")}
        for op in ops:
            if op.dsem is None:
                if op.fn is not None and op.idx in need:
                    cnt[op.eng] += 1
                    op.sig = True
                    op.cnt = cnt[op.eng]
                else:
                    op.sig = False
                    op.cnt = None
        dsems = sorted(self.dcnt.keys())
        import bisect
        dlist = {d: ([], []) for d in dsems}
        for op in ops:
            if op.dsem is not None:
                dlist[op.dsem][0].append(op.idx)
                dlist[op.dsem][1].append(op.cnt)

        def dma_wait_val(d, idx):
            ii, cc = dlist[d]
            j = bisect.bisect_left(ii, idx)
            return cc[j - 1] if j > 0 else 0
        with contextlib.ExitStack() as es:
            esem = {e: es.enter_context(nc.semaphore(f"s_{e}")) for e in ("pe", "act", "dve", "pool")}
            dsem = {d: es.enter_context(nc.semaphore(f"d_{d}")) for d in dsems}
            block = es.enter_context(nc.Block())

            def stream(engname):
                def body(eng):
                    waited = {}
                    for op in ops:
                        if op.eng != engname:
                            continue
                        w = {}
                        for d in op.deps:
                            if d.dsem is not None:
                                key = ("d", d.dsem); val = max(d.cnt, dma_wait_val(d.dsem, op.idx))
                            else:
                                if d.fn is None:
                                    continue
                                if d.eng == engname and op.dsem is None and (engname == "pe" or not SAME_ENG_SYNC):
                                    continue
                                key = ("e", d.eng); val = d.cnt
                            if val > w.get(key, 0):
                                w[key] = val
                        for key, val in w.items():
                            if waited.get(key, 0) >= val:
                                continue
                            waited[key] = val
                            s = dsem[key[1]] if key[0] == "d" else esem[key[1]]
                            eng.wait_ge(s, val)
                        if op.fn is None:
                            continue
                        ins = op.fn(eng)
                        if op.dsem is not None:
                            ins.then_inc(dsem[op.dsem], 16)
                        elif op.sig:
                            ins.then_inc(esem[engname], 1)
                return body

            block.tensor(stream("pe"))
            block.scalar(stream("act"))
            block.vector(stream("dve"))
            block.gpsimd(stream("pool"))
            block.sync(stream("sp"))
        return cnt, {d: self.dcnt[d] for d in dsems}


class Rot:
    def __init__(self, bufs, name):
        self.bufs = bufs
        self.name = name
        self.i = 0

    def next(self):
        b = self.bufs[self.i % len(self.bufs)]
        k = (self.name, self.i % len(self.bufs))
        self.i += 1
        return b, k


def _r_start(r, rows=32, kr=8):
    return min(max(r - kr // 2, 0), rows - kr)


NA_PAIRS = []
for _qb in range(4):
    for _j in range(16):
        lo = min(_r_start(r) for r in range(8 * _qb, 8 * _qb + 8))
        hi = max(_r_start(r) + 7 for r in range(8 * _qb, 8 * _qb + 8))
        if 2 * _j + 1 >= lo and 2 * _j <= hi:
            NA_PAIRS.append((_qb, _j))


def host_consts():
    c = {}
    bf = ml_dtypes.bfloat16
    c["ident"] = np.eye(128, dtype=np.float32).astype(bf)
    c["identf"] = np.eye(128, dtype=np.float32)
    c["onesf"] = np.ones((128, 128), np.float32)
    c["onesb"] = np.ones((128, 128), np.float32).astype(bf)
    t = np.arange(SEQ)
    row = (t // GRID_W).astype(np.float32)
    col = (t % GRID_W).astype(np.float32)
    half = HD // 2
    inv_freq = (np.float32(10000.0) ** (-np.arange(0, half, 2, dtype=np.float32) / np.float32(half))).astype(np.float32)
    ang_r = row[:, None] * inv_freq[None, :]
    ang_c = col[:, None] * inv_freq[None, :]
    ang = np.concatenate([ang_r, ang_r, ang_c, ang_c], axis=-1).astype(np.float32)
    cosT = np.ones((128, TALL), np.float32)
    sinT = np.zeros((128, TALL), np.float32)
    cosT[:, CTX:] = np.cos(ang).T
    sinT[:, CTX:] = np.sin(ang).T
    c["cosT"] = cosT
    c["sinT"] = sinT
    R = np.zeros((128, 128), np.float32)
    for i in range(32):
        R[i, 32 + i] = -1.0
        R[32 + i, i] = 1.0
        R[64 + i, 96 + i] = -1.0
        R[96 + i, 64 + i] = 1.0
    c["permT"] = R.T.copy().astype(bf)
    a = np.arange(128, dtype=np.float32)
    diff = a[None, :] - a[:, None]
    c["ret_tabs"] = np.stack([
        diff, -diff,
        (diff >= 0).astype(np.float32), (diff <= 0).astype(np.float32),
        np.broadcast_to(a[None, :] + 1.0, (128, 128)), np.broadcast_to(128.0 - a[None, :], (128, 128)),
    ], axis=1).astype(np.float32)
    c["ret_pidx"] = np.stack([127.0 - a, a, np.full(128, 128.0, np.float32)], axis=1).astype(np.float32)
    rm = np.zeros((128, len(NA_PAIRS), 512), np.float32)
    for pi, (qb, j) in enumerate(NA_PAIRS):
        for i in range(2):
            for jj in range(8):
                kr = 2 * j + i
                qr = 8 * qb + jj
                rs = _r_start(qr)
                if rs <= kr <= rs + 7:
                    rm[64 * i:64 * i + 64, pi, 64 * jj:64 * jj + 64] = 1.0
    c["na_rmask"] = rm.astype(bf)
    cv = np.zeros((64, 64), np.float32)
    for qc in range(64):
        cs = min(max(qc - 8, 0), 48)
        cv[cs:cs + 16, qc] = 1.0
    c["na_cv"] = np.concatenate([cv, cv], 0)
    z = np.zeros((31, 64, 128), np.float32)
    for qc in range(64):
        for p in range(128):
            j = (p % 64) - qc + 15
            if 0 <= j < 31:
                z[j, qc, p] = 1.0
    c["na_z"] = z.astype(bf)
    return c


CONST_SPECS = None


def build_program(stop_after=None, dbg=False):
    nc = bass.Bass("TRN2", target_bir_lowering=False)
    p = Prog(nc)

    def din(name, shape, dt=F32):
        return nc.dram_tensor(name, list(shape), dt, kind="ExternalInput").ap()

    def dscr(name, shape, dt):
        if dbg:
            return nc.dram_tensor(name, list(shape), dt, kind="ExternalOutput").ap()
        return nc.dram_tensor(name, list(shape), dt).ap()

    x_in = din("x", [SEQ, D])
    ctx_in = din("ctx", [CTX, D])
    c2_in = din("c2", [2, D])
    ada_w = din("ada_w", [DEPTH, D, 3 * D])
    ada_b = din("ada_b", [DEPTH, 3 * D])
    norm_g = din("norm_g", [DEPTH, D])
    w_in = din("w_in", [DEPTH, D, PT])
    conv_w = din("conv_w", [DEPTH, 31, GW])
    conv_b = din("conv_b", [DEPTH, GW])
    conv_ln_g = din("conv_ln_g", [DEPTH, GW])
    conv_ln_b = din("conv_ln_b", [DEPTH, GW])
    gqa_qn_g = din("gqa_qn_g", [DEPTH, HD])
    gqa_kn_g = din("gqa_kn_g", [DEPTH, HD])
    ret_dec = din("ret_dec", [DEPTH, 16])
    ret_gn_g = din("ret_gn_g", [DEPTH, GW])
    ret_gn_b = din("ret_gn_b", [DEPTH, GW])
    na_bT = din("na_bT", [DEPTH, 31, 8, 15])
    w_out = din("w_out", [DEPTH, D, D])
    final_g = din("final_g", [D])
    hc = {}
    cs = host_consts()
    for k, v in cs.items():
        hc[k] = din("k_" + k, v.shape, BF16 if v.dtype == ml_dtypes.bfloat16 else F32)
    out = nc.dram_tensor("out", [SEQ, D], F32, kind="ExternalOutput").ap()

    mod = dscr("mod", [DEPTH, 2, 3 * D], F32)
    xres = dscr("xres", [TALL, D], F32)
    uT = dscr("uT", [GW, TALL], BF16)
    sgT = dscr("sgT", [D, TALL], BF16)
    qT = {m: dscr("qT" + m, [GW, TALL], BF16) for m in "BCD"}
    kT = {"B": dscr("kTB", [256, TALL], BF16), "C": dscr("kTC", [GW, TALL], BF16), "D": dscr("kTD", [GW, TALL], BF16)}
    vS = {"B": dscr("vB", [TALL, 256], BF16), "C": dscr("vC", [TALL, GW], BF16), "D": dscr("vD", [TALL, GW], BF16)}
    yT = dscr("yT", [D, TALL], BF16)
    dbg_outs = {}

    psA = [nc.alloc_psum_tensor(f"psA{i}", [128, 512], F32) for i in range(4)]
    psB = [nc.alloc_psum_tensor(f"psB{i}", [128, 512], F32) for i in range(2)]
    psT = [nc.alloc_psum_tensor(f"psT{i}", [128, 8, 128], BF16) for i in range(2)]
    rA = Rot(psA, "psA")
    rB = Rot(psB, "psB")
    rT = Rot(psT, "psT")

    ident = p.sb([128, 128], BF16, "ident")
    identf = p.sb([128, 128], F32, "identf")
    onesf = p.sb([128, 128], F32, "onesf")
    onesb = p.sb([128, 128], BF16, "onesb")
    permT = p.sb([128, 128], BF16, "permT")
    for t, n in ((ident, "ident"), (identf, "identf"), (onesf, "onesf"), (onesb, "onesb"), (permT, "permT")):
        p.dma("sp", "const", t[:], hc[n], writes=[n])
    small = p.sb([128, 64], F32, "small")
    nc_ctx = nc.allow_non_contiguous_dma(reason="small per-channel vectors")
    nc_ctx.__enter__()

    def wload(q, dsem, dst, src, wkey):
        p.dma(q, dsem, dst, src, writes=[wkey])

    p.mark()
    cT = p.sb([128, 2, 32], F32, "cT")
    scT = p.sb([128, 2, 32], BF16, "scT")
    modsb = p.sb([2, 3 * D], F32, "modsb")
    tmp2 = p.sb([2, 3 * D], F32, "tmp2")
    wbuf = [p.sb([128, 32, 256], BF16, f"adaw{i}") for i in range(3)]
    rW = Rot(wbuf, "adaw")
    for r in range(2):
        p.dma("sp", "c2", cT[:, r, :], c2_in[r].rearrange("(k p) -> p k", p=128), writes=["cT"])
    p.act(lambda e: e.activation(out=scT[:], in_=cT[:], func=AF.Silu), reads=["cT"], writes=["scT"])
    for l in range(DEPTH):
        for nb in range(3 * D // 256):
            wb, wk = rW.next()
            p.dma("pool", "w%d" % wk[1], wb[:], ada_w[l, :, nb * 256:(nb + 1) * 256].rearrange("(k p) n -> p k n", p=128), writes=[wk])
            ps, pk = rA.next()
            for k in range(32):
                p.pe(lambda e, ps=ps, wb=wb, k=k: e.matmul(ps[0:2, 0:256], lhsT=scT[:, :, k], rhs=wb[:, k, :], start=(k == 0), stop=(k == 31)),
                     reads=["scT", wk], writes=[pk])
            p.dve(lambda e, ps=ps, nb=nb: e.tensor_copy(out=modsb[:, nb * 256:(nb + 1) * 256], in_=ps[0:2, 0:256]), reads=[pk], writes=["modsb"])
        p.dma("sp", "c2", tmp2[:], ada_b[l:l + 1, :].broadcast_to([2, 3 * D]) if False else ada_b[l].partition_broadcast(2), writes=["tmp2"])
        p.dve(lambda e: e.tensor_tensor(out=modsb[:], in0=modsb[:], in1=tmp2[:], op=ALU.add), reads=["modsb", "tmp2"], writes=["modsb"])
        p.dma("sp", "c2", tmp2[:, 0:D], norm_g[l].partition_broadcast(2), reads=[], writes=["tmp2"])
        p.dve(lambda e: e.scalar_tensor_tensor(out=modsb[:, D:2 * D], in0=modsb[:, D:2 * D], scalar=1.0, in1=tmp2[:, 0:D], op0=ALU.add, op1=ALU.mult),
              reads=["modsb", "tmp2"], writes=["modsb"])
        p.dma("sp", "modst", mod[l], modsb[:], reads=["modsb"], writes=["mod"])
    p.barrier()
    p.release()

    def load_small(l):
        def col(dst_c, vec, nb):
            p.dma("sp", "small", small[:, dst_c:dst_c + nb], vec.rearrange("(b p) -> p b", p=128), writes=["small"])
        col(0, conv_b[l], 8); col(8, conv_ln_g[l], 8); col(16, conv_ln_b[l], 8)
        col(24, ret_gn_g[l], 8); col(32, ret_gn_b[l], 8)
        col(40, gqa_qn_g[l], 1); col(41, gqa_kn_g[l], 1)

    groups = [(0, 1280), (1280, 1024)]

    for l in range(DEPTH):
        last = (l == DEPTH - 1)
        load_small(l)
        for (tok0, T) in groups:
            p.mark()
            hlT = p.sb([128, 32, T], BF16, "hlT")
            p.mark()
            xt = [p.sb([128, D], F32, f"xt{i}") for i in range(3)]
            rX = Rot(xt, "xt")
            junk = p.sb([128, D], BF16, "junk")
            hlbs = [p.sb([128, D], BF16, f"hlb{i}") for i in range(2)]
            rH = Rot(hlbs, "hlb")
            gsb = p.sb([128, D], F32, "gsb")
            shb = p.sb([128, D], F32, "shb")
            st = p.sb([128, 8], F32, "st")
            cur_row = None
            for tt in range(T // 128):
                g0 = tok0 + tt * 128
                is_ctx = g0 < CTX
                row = 1 if is_ctx else 0
                if row != cur_row:
                    p.dma("sp", "modg", gsb[:], mod[l, row, D:2 * D].partition_broadcast(128), reads=["mod"], writes=["gsb"])
                    p.dma("sp", "mods", shb[:], mod[l, row, 0:D].partition_broadcast(128), reads=["mod"], writes=["shb"])
                    cur_row = row
                if l == 0:
                    src = ctx_in[g0:g0 + 128, :] if is_ctx else x_in[g0 - CTX:g0 - CTX + 128, :]
                else:
                    src = xres[g0:g0 + 128, :]
                xb_, xk = rX.next()
                p.dma("sp", "x%d" % xk[1], xb_[:], src, reads=["xres"], writes=[xk])
                p.act(lambda e, xb_=xb_: e.activation(out=junk[:], in_=xb_[:], func=AF.Square, accum_out=st[:, 0:1]), reads=[xk], writes=["junk", "st0"])
                p.act(lambda e: e.activation(out=st[:, 1:2], in_=st[:, 0:1], func=AF.Sqrt, bias=NEPS, scale=1.0 / D), reads=["st0"], writes=["st1"])
                p.dve(lambda e: e.reciprocal(out=st[:, 2:3], in_=st[:, 1:2]), reads=["st1"], writes=["st2"])
                p.dve(lambda e, xb_=xb_: e.scalar_tensor_tensor(out=xb_[:], in0=xb_[:], scalar=st[:, 2:3], in1=gsb[:], op0=ALU.mult, op1=ALU.mult),
                      reads=[xk, "st2", "gsb"], writes=[xk])
                hlb, hlbk = rH.next()
                p.pool(lambda e, xb_=xb_, hlb=hlb: e.tensor_tensor(out=hlb[:], in0=xb_[:], in1=shb[:], op=ALU.add), reads=[xk, "shb"], writes=[hlbk])
                for k8 in range(4):
                    pt, ptk = rT.next()
                    for kk in range(8):
                        k = k8 * 8 + kk
                        p.pe(lambda e, pt=pt, kk=kk, k=k, hlb=hlb: e.transpose(out=pt[:, kk, :], in_=hlb[:, k * 128:(k + 1) * 128], identity=ident[:]),
                             reads=[hlbk, "ident"], writes=[ptk])
                    p.dve(lambda e, pt=pt, k8=k8, tt=tt: e.tensor_copy(out=hlT[:, k8 * 8:(k8 + 1) * 8, tt * 128:(tt + 1) * 128], in_=pt[:]),
                          reads=[ptk], writes=["hlT"])
            p.barrier()
            p.release()
            if stop_after == "norm":
                break
            p.mark()
            wbuf = [p.sb([128, 32, 256], BF16, f"winw{i}") for i in range(3)]
            rW = Rot(wbuf, "winw")
            cosT = p.sb([128, T], F32, "cosT")
            sinT = p.sb([128, T], F32, "sinT")
            p.dma("sp", "const", cosT[:], hc["cosT"][:, tok0:tok0 + T], writes=["cosT"])
            p.dma("sp", "const", sinT[:], hc["sinT"][:, tok0:tok0 + T], writes=["sinT"])
            ef = [p.sb([128, 512], F32, f"ef{i}") for i in range(10)]
            rE = Rot(ef, "ef")
            eb = [p.sb([128, 512], BF16, f"eb{i}") for i in range(4)]
            rEb = Rot(eb, "eb")
            ob = [p.sb([128, 512], BF16, f"ob{i}") for i in range(4)]
            rO = Rot(ob, "ob")
            chunks = [(c0, min(512, T - c0)) for c0 in range(0, T, 512)]
            if last and tok0 == 0:
                chunks_q = [(c0, min(512, T - c0)) for c0 in range(CTX, T, 512)]
            else:
                chunks_q = chunks
            pending = []

            def flush():
                cur = pending[:]
                del pending[:]
                for cb_ in cur:
                    cb_()

            def load_w(pieces):
                wb, wk = rW.next()
                o = 0
                for (c0, wd) in pieces:
                    p.dma("pool", "w%d" % wk[1], wb[:, :, o:o + wd], w_in[l, :, c0:c0 + wd].rearrange("(k p) n -> p k n", p=128), writes=[wk])
                    o += wd
                return wb, wk

            def mm_fm(wb, wk, j, c0, n):
                ps, pk = rA.next()
                for k in range(32):
                    p.pe(lambda e, ps=ps, k=k: e.matmul(ps[:, 0:n], lhsT=wb[:, k, j * 128:(j + 1) * 128], rhs=hlT[:, k, c0:c0 + n], start=(k == 0), stop=(k == 31)),
                         reads=["hlT", wk], writes=[pk])
                return ps, pk

            def store_fm(src, sk, dst, r0, c0, n):
                p.dma("sp", "st_%s%d" % sk, dst[r0:r0 + 128, tok0 + c0:tok0 + c0 + n], src[:, 0:n], reads=[sk], writes=["scr"])

            def rope_tail(qn, qnk, c0, n, dst, r0):
                qb, qbk = rEb.next()
                p.dve(lambda e: e.tensor_copy(out=qb[:, 0:n], in_=qn[:, 0:n]), reads=[qnk], writes=[qbk])
                pending.append(lambda: rope_tail2(qn, qnk, qb, qbk, c0, n, dst, r0))

            def rope_tail2(qn, qnk, qb, qbk, c0, n, dst, r0):
                ps2, p2k = rB.next()
                p.pe(lambda e: e.matmul(ps2[:, 0:n], lhsT=permT[:], rhs=qb[:, 0:n], start=True, stop=True), reads=["permT", qbk], writes=[p2k])
                t2, t2k = rE.next()
                p.dve(lambda e: e.tensor_tensor(out=t2[:, 0:n], in0=ps2[:, 0:n], in1=sinT[:, c0:c0 + n], op=ALU.mult), reads=[p2k, "sinT"], writes=[t2k])
                p.dve(lambda e: e.tensor_tensor(out=qn[:, 0:n], in0=qn[:, 0:n], in1=cosT[:, c0:c0 + n], op=ALU.mult), reads=[qnk, "cosT"], writes=[qnk])
                o, ok = rO.next()
                p.dve(lambda e: e.tensor_tensor(out=o[:, 0:n], in0=qn[:, 0:n], in1=t2[:, 0:n], op=ALU.add), reads=[qnk, t2k], writes=[ok])
                store_fm(o, ok, dst, r0, c0, n)

            for jb in range(8):
                wb, wk = load_w([(OFF["A_a"] + jb * 128, 128), (OFF["A_g"] + jb * 128, 128)])
                for (c0, n) in chunks_q:
                    pa, pak = mm_fm(wb, wk, 0, c0, n)
                    pg, pgk = mm_fm(wb, wk, 1, c0, n)
                    sg, sgk = rE.next()
                    p.act(lambda e, sg=sg, pg=pg, n=n: e.activation(out=sg[:, 0:n], in_=pg[:, 0:n], func=AF.Sigmoid), reads=[pgk], writes=[sgk])
                    o, ok = rO.next()
                    p.dve(lambda e, o=o, pa=pa, sg=sg, n=n: e.tensor_tensor(out=o[:, 0:n], in0=pa[:, 0:n], in1=sg[:, 0:n], op=ALU.mult), reads=[pak, sgk], writes=[ok])
                    store_fm(o, ok, uT, jb * 128, c0, n)
            for mi, m in enumerate("ABCD"):
                for jb in range(4):
                    wb, wk = load_w([(OFF[m + "_gate"] + jb * 256, 256)])
                    for j in range(2):
                        for (c0, n) in chunks_q:
                            ps, pk = mm_fm(wb, wk, j, c0, n)
                            o, ok = rO.next()
                            p.act(lambda e, o=o, ps=ps, n=n: e.activation(out=o[:, 0:n], in_=ps[:, 0:n], func=AF.Silu), reads=[pk], writes=[ok])
                            store_fm(o, ok, sgT, mi * GW + jb * 256 + j * 128, c0, n)
            fm_list = [("B", "q", "normrope", 1.0, 1024, 40), ("B", "k", "normrope", 1.0, 256, 41),
                       ("C", "q", "rope", 1.0, 1024, None), ("C", "k", "rope", SCALE, 1024, None),
                       ("D", "q", "plain", 1.0, 1024, None), ("D", "k", "plain", 1.0, 1024, None)]
            for (m, which, kind, scl, width, gcol) in fm_list:
                dst = qT[m] if which == "q" else kT[m]
                for jb in range(width // 256):
                    wb, wk = load_w([(OFF[m + "_" + which] + jb * 256, 256)])
                    for j in range(2):
                        r0 = jb * 256 + j * 128
                        for (c0, n) in (chunks_q if which == "q" else chunks):
                            ps, pk = mm_fm(wb, wk, j, c0, n)
                            flush()
                            if kind == "plain":
                                o, ok = rO.next()
                                p.dve(lambda e, o=o, ps=ps, n=n: e.tensor_copy(out=o[:, 0:n], in_=ps[:, 0:n]), reads=[pk], writes=[ok])
                                store_fm(o, ok, dst, r0, c0, n)
                            elif kind == "rope":
                                qn, qnk = rE.next()
                                p.act(lambda e, qn=qn, ps=ps, n=n, scl=scl: e.activation(out=qn[:, 0:n], in_=ps[:, 0:n], func=AF.Copy, scale=scl), reads=[pk], writes=[qnk])
                                rope_tail(qn, qnk, c0, n, dst, r0)
                            else:
                                sq, sqk = rE.next()
                                p.act(lambda e, sq=sq, ps=ps, n=n: e.activation(out=sq[:, 0:n], in_=ps[:, 0:n], func=AF.Square), reads=[pk], writes=[sqk])

                                def stage_b(sq=sq, sqk=sqk, ps=ps, pk=pk, c0=c0, n=n, gcol=gcol, dst=dst, r0=r0):
                                    ps3, p3k = rB.next()
                                    p.pe(lambda e: e.matmul(ps3[:, 0:n], lhsT=onesf[:], rhs=sq[:, 0:n], start=True, stop=True), reads=["onesf", sqk], writes=[p3k])
                                    rs, rsk = rE.next()
                                    p.act(lambda e: e.activation(out=rs[:, 0:n], in_=ps3[:, 0:n], func=AF.Sqrt, bias=NEPS, scale=1.0 / HD), reads=[p3k], writes=[rsk])
                                    p.dve(lambda e: e.reciprocal(out=rs[:, 0:n], in_=rs[:, 0:n]), reads=[rsk], writes=[rsk])
                                    qn, qnk = rE.next()
                                    p.dve(lambda e: e.scalar_tensor_tensor(out=qn[:, 0:n], in0=ps[:, 0:n], scalar=small[:, gcol:gcol + 1], in1=rs[:, 0:n], op0=ALU.mult, op1=ALU.mult),
                                          reads=[pk, rsk, "small"], writes=[qnk])
                                    rope_tail(qn, qnk, c0, n, dst, r0)
                                pending.append(stage_b)
            flush(); flush(); flush()
            for m, width in (("B", 256), ("C", 1024), ("D", 1024)):
                for jb in range(width // 256):
                    wb, wk = load_w([(OFF[m + "_v"] + jb * 256, 256)])
                    for tt in range(T // 128):
                        ps, pk = rA.next()
                        for k in range(32):
                            p.pe(lambda e, ps=ps, k=k, tt=tt, wb=wb: e.matmul(ps[:, 0:256], lhsT=hlT[:, k, tt * 128:(tt + 1) * 128], rhs=wb[:, k, :], start=(k == 0), stop=(k == 31)),
                                 reads=["hlT", wk], writes=[pk])
                        o, ok = rO.next()
                        p.act(lambda e, o=o, ps=ps: e.activation(out=o[:, 0:256], in_=ps[:, 0:256], func=AF.Copy), reads=[pk], writes=[ok])
                        p.dma("sp", "st_%s%d" % ok, vS[m][tok0 + tt * 128:tok0 + (tt + 1) * 128, jb * 256:(jb + 1) * 256], o[:, 0:256], reads=[ok], writes=["scr"])
            p.barrier()
            p.release()
            p.release()
        if stop_after in ("norm", "proj"):
            break

        p.mark()
        cw31 = p.sb([31, GW], F32, "cw31")
        cwT = p.sb([128, 8, 31], F32, "cwT")
        p.dma("sp", "cw", cw31[:], conv_w[l], writes=["cw31"])
        for cb in range(8):
            ps, pk = rA.next()
            p.pe(lambda e, ps=ps, cb=cb: e.transpose(out=ps[:, 0:31], in_=cw31[:, cb * 128:(cb + 1) * 128], identity=identf[0:31, 0:31]), reads=["cw31", "identf"], writes=[pk])
            p.dve(lambda e, ps=ps, cb=cb: e.tensor_copy(out=cwT[:, cb, :], in_=ps[:, 0:31]), reads=[pk], writes=["cwT"])
        dg = p.sb([128, 8 * 31, 128], BF16, "dg")
        for cb in range(8):
            p.dve(lambda e, cb=cb: e.tensor_tensor(out=dg[:, cb * 31:(cb + 1) * 31, :], in0=identf[:].unsqueeze(1).to_broadcast([128, 31, 128]),
                                                   in1=cwT[:, cb, :].unsqueeze(2).to_broadcast([128, 31, 128]), op=ALU.mult),
                  reads=["identf", "cwT"], writes=["dg"])
        ub = [p.sb([128, 512 + 30], BF16, f"ub{i}") for i in range(3)]
        rU = Rot(ub, "ub")
        vb = [p.sb([128, 512], F32, f"vb{i}") for i in range(8)]
        sqb = [p.sb([128, 512], F32, f"sqb{i}") for i in range(2)]
        rSq = Rot(sqb, "sqb")
        mean = p.sb([128, 512], F32, "mean")
        rstd = p.sb([128, 512], F32, "rstd")
        msq = p.sb([128, 512], F32, "msq")
        sgl = [p.sb([128, 512], BF16, f"sgl{i}") for i in range(2)]
        rSg = Rot(sgl, "sgl")
        tb = [p.sb([128, 512], F32, f"tb{i}") for i in range(2)]
        rTb = Rot(tb, "tb")
        yo = [p.sb([128, 512], BF16, f"yo{i}") for i in range(3)]
        rY = Rot(yo, "yo")
        segs = [(CTX, SEQ)] if last else [(0, CTX), (CTX, SEQ)]
        for (s0, slen) in segs:
            for c0 in range(0, slen, 512):
                n = min(512, slen - c0)
                ps_s, pssk = rB.next()
                ps_q, psqk = rB.next()
                for cb in range(8):
                    u, uk = rU.next()
                    lo = max(c0 - 15, 0)
                    hi = min(c0 + n + 15, slen)
                    p.pool(lambda e, u=u: e.memset(u[:], 0.0), writes=[uk])
                    p.dma("sp", "u%d" % uk[1], u[:, lo - (c0 - 15):hi - (c0 - 15)], uT[cb * 128:(cb + 1) * 128, s0 + lo:s0 + hi], reads=["scr"], writes=[uk])
                    v, vk = vb[cb], ("vb", cb)
                    pc, pck = rA.next()
                    for j in range(31):
                        p.pe(lambda e, pc=pc, u=u, cb=cb, n=n, j=j: e.matmul(pc[:, 0:n], lhsT=dg[:, cb * 31 + j, :], rhs=u[:, j:j + n], start=(j == 0), stop=(j == 30)),
                             reads=["dg", uk], writes=[pck])
                    p.dve(lambda e, v=v, pc=pc, cb=cb, n=n: e.tensor_scalar(out=v[:, 0:n], in0=pc[:, 0:n], scalar1=small[:, cb:cb + 1], scalar2=None, op0=ALU.add),
                          reads=[pck, "small"], writes=[vk])
                    sq, sqk = rSq.next()
                    p.act(lambda e, sq=sq, v=v, n=n: e.activation(out=sq[:, 0:n], in_=v[:, 0:n], func=AF.Square), reads=[vk], writes=[sqk])
                    p.pe(lambda e, v=v, cb=cb, n=n, ps_s=ps_s: e.matmul(ps_s[:, 0:n], lhsT=onesf[:], rhs=v[:, 0:n], start=(cb == 0), stop=(cb == 7)), reads=["onesf", vk], writes=[pssk])
                    p.pe(lambda e, sq=sq, cb=cb, n=n, ps_q=ps_q: e.matmul(ps_q[:, 0:n], lhsT=onesf[:], rhs=sq[:, 0:n], start=(cb == 0), stop=(cb == 7)), reads=["onesf", sqk], writes=[psqk])
                p.dve(lambda e, n=n, ps_s=ps_s: e.tensor_scalar(out=mean[:, 0:n], in0=ps_s[:, 0:n], scalar1=1.0 / GW, scalar2=None, op0=ALU.mult), reads=[pssk], writes=["mean"])
                p.dve(lambda e, n=n: e.tensor_tensor(out=msq[:, 0:n], in0=mean[:, 0:n], in1=mean[:, 0:n], op=ALU.mult), reads=["mean"], writes=["msq"])
                p.dve(lambda e, n=n, ps_q=ps_q: e.scalar_tensor_tensor(out=rstd[:, 0:n], in0=ps_q[:, 0:n], scalar=1.0 / GW, in1=msq[:, 0:n], op0=ALU.mult, op1=ALU.subtract),
                      reads=[psqk, "msq"], writes=["rstd"])
                p.act(lambda e, n=n: e.activation(out=rstd[:, 0:n], in_=rstd[:, 0:n], func=AF.Sqrt, bias=LEPS, scale=1.0), reads=["rstd"], writes=["rstd"])
                p.dve(lambda e, n=n: e.reciprocal(out=rstd[:, 0:n], in_=rstd[:, 0:n]), reads=["rstd"], writes=["rstd"])
                for cb in range(8):
                    v, vk = vb[cb], ("vb", cb)
                    sg, sgk = rSg.next()
                    p.dma("sp", "sg%d" % sgk[1], sg[:, 0:n], sgT[cb * 128:(cb + 1) * 128, s0 + c0:s0 + c0 + n], reads=["scr"], writes=[sgk])
                    p.dve(lambda e, v=v, n=n: e.tensor_tensor(out=v[:, 0:n], in0=v[:, 0:n], in1=mean[:, 0:n], op=ALU.subtract), reads=[vk, "mean"], writes=[vk])
                    p.pool(lambda e, v=v, n=n: e.tensor_tensor(out=v[:, 0:n], in0=v[:, 0:n], in1=rstd[:, 0:n], op=ALU.mult), reads=[vk, "rstd"], writes=[vk])
                    t_, tk = rTb.next()
                    p.act(lambda e, t_=t_, v=v, cb=cb, n=n: e.activation(out=t_[:, 0:n], in_=v[:, 0:n], func=AF.Silu, scale=small[:, 8 + cb:9 + cb], bias=small[:, 16 + cb:17 + cb]),
                          reads=[vk, "small"], writes=[tk])
                    y, yk = rY.next()
                    p.dve(lambda e, y=y, t_=t_, sg=sg, n=n: e.tensor_tensor(out=y[:, 0:n], in0=t_[:, 0:n], in1=sg[:, 0:n], op=ALU.mult), reads=[tk, sgk], writes=[yk])
                    p.dma("sp", "yst%d" % yk[1], yT[cb * 128:(cb + 1) * 128, s0 + c0:s0 + c0 + n], y[:, 0:n], reads=[yk], writes=["yT"])
        p.barrier()
        p.release()
        if stop_after == "conv":
            break

        def attention(m):
            p.mark()
            nkv = 2 if m == "B" else 8
            KT = [p.sb([128, TALL], BF16, f"KT{i}") for i in range(2)]
            rK = Rot(KT, "KT")
            VV = [p.sb([128, 18, 128], BF16, f"VV{i}") for i in range(2)]
            rV = Rot(VV, "VV")
            QT = [p.sb([128, TALL], BF16, f"QT{i}") for i in range(2)]
            rQ = Rot(QT, "QT")
            SG = [p.sb([128, TALL], BF16, f"SG{i}") for i in range(2)]
            rS = Rot(SG, "SG")
            pT = [p.sb([128, 512], BF16, f"pT{i}") for i in range(4)]
            rP = Rot(pT, "pT")
            pM = [p.sb([128, 512], BF16, f"pM{i}") for i in range(3)]
            rM = Rot(pM, "pM")
            rden = p.sb([128, 512], F32, "rden")
            of = p.sb([128, 512], F32, "of")
            yo = [p.sb([128, 512], BF16, f"yo{i}") for i in range(3)]
            rY = Rot(yo, "yo")
            mrow = {"B": GW, "D": 3 * GW}[m]
            if m == "D":
                rmask = p.sb([128, len(NA_PAIRS), 512], BF16, "rmask")
                p.dma("sp", "const", rmask[:], hc["na_rmask"], writes=["rmask"])
                cv = p.sb([128, 64], F32, "cv")
                p.dma("sp", "const", cv[:], hc["na_cv"], writes=["cv"])
                zd = p.sb([31, 64, 128], BF16, "zd")
                p.dma("sp", "const", zd[:], hc["na_z"], writes=["zd"])
                nb = p.sb([31, 8, 15], BF16, "nb")
                p.dma("pool", "nbld", nb[:], na_bT[l], writes=["nb"])
                G2 = [p.sb([128, 29 * 64], BF16, f"G2{i}") for i in range(2)]
                for g2 in G2:
                    p.pool(lambda e, g2=g2: e.memset(g2[:], 0.0), writes=[("G2", G2.index(g2))])
                rG = Rot(G2, "G2")
                gex = p.sb([128, 15, 64], F32, "gex")
            kvl = {}
            qsl = {}
            allheads = [(kv, h) for kv in range(nkv) for h in ([kv * 4 + g for g in range(4)] if m == "B" else [kv])]

            def load_kv(kv):
                K_, Kk = rK.next()
                p.dma("sp", "K%d" % Kk[1], K_[:], kT[m][kv * 128:(kv + 1) * 128, :], reads=["scr"], writes=[Kk])
                V_, Vk = rV.next()
                p.dma("sp", "V%d" % Vk[1], V_[:], vS[m][:, kv * 128:(kv + 1) * 128].rearrange("(t p) d -> p t d", p=128), reads=["scr"], writes=[Vk])
                kvl[kv] = (K_, Kk, V_, Vk)

            def load_qs(i):
                h = allheads[i][1]
                Q_, Qk = rQ.next()
                p.dma("sp", "Q%d" % Qk[1], Q_[:], qT[m][h * 128:(h + 1) * 128, :], reads=["scr"], writes=[Qk])
                S_, Sk = rS.next()
                p.dma("sp", "S%d" % Sk[1], S_[:], sgT[mrow + h * 128:mrow + (h + 1) * 128, :], reads=["scr"], writes=[Sk])
                qsl[i] = (Q_, Qk, S_, Sk)
            load_kv(0)
            load_qs(0)
            hidx = 0
            for kv in range(nkv):
                K_, Kk, V_, Vk = kvl[kv]
                if kv + 1 < nkv:
                    load_kv(kv + 1)
                heads = [kv * 4 + g for g in range(4)] if m == "B" else [kv]
                if m == "D":
                    h = kv
                    g2, g2k = rG.next()
                    for half in range(2):
                        e0, e1 = (0, 8) if half == 0 else (8, 15)
                        ps, pk = rB.next()
                        psv = ps[:, 0:(e1 - e0) * 64].rearrange("p (e q) -> p e q", q=64)
                        for qc in range(64):
                            p.pe(lambda e, psv=psv, qc=qc, e0=e0, e1=e1, h=h: e.matmul(psv[:, :, qc], lhsT=zd[:, qc, :], rhs=nb[:, h, e0:e1], start=True, stop=True),
                                 reads=["zd", "nb"], writes=[pk])
                        p.act(lambda e, psv=psv, e0=e0, e1=e1: e.activation(out=gex[:, e0:e1, :], in_=psv, func=AF.Exp), reads=[pk], writes=["gex"])
                    g2v = g2[:].rearrange("p (e q) -> p e q", q=64)
                    p.dve(lambda e, g2v=g2v: e.tensor_tensor(out=g2v[0:64, 7:22, :], in0=gex[0:64, :, :], in1=cv[0:64, :].unsqueeze(1).to_broadcast([64, 15, 64]), op=ALU.mult),
                          reads=["gex", "cv"], writes=[g2k])
                    p.dve(lambda e, g2v=g2v: e.tensor_tensor(out=g2v[64:128, 8:23, :], in0=gex[64:128, :, :], in1=cv[64:128, :].unsqueeze(1).to_broadcast([64, 15, 64]), op=ALU.mult),
                          reads=["gex", "cv"], writes=[g2k])
                for h in heads:
                    Q_, Qk, S_, Sk = qsl[hidx]
                    if hidx + 1 < len(allheads):
                        load_qs(hidx + 1)
                    hidx += 1
                    qchunks = []
                    if not last:
                        qchunks.append((0, 256, [(0, None), (1, None)]))
                    for qb in range(4):
                        if m == "B":
                            kts = [(t, None) for t in range(18)]
                        else:
                            kts = [(0, None), (1, None)] + [(2 + j, NA_PAIRS.index((qb, j))) for j in range(16) if (qb, j) in NA_PAIRS]
                        qchunks.append((CTX + qb * 512, 512, kts))
                    for (q0, n, kts) in qchunks:
                        ps_o, pok = rB.next()
                        ps_d, pdk = rB.next()

                        def score(i):
                            kt, _ = kts[i]
                            ps, pk = rA.next()
                            p.pe(lambda e, ps=ps, kt=kt: e.matmul(ps[:, 0:n], lhsT=K_[:, kt * 128:(kt + 1) * 128], rhs=Q_[:, q0:q0 + n], start=True, stop=True),
                                 reads=[Kk, Qk], writes=[pk])
                            return ps, pk
                        nxtq = [score(0)]
                        if len(kts) > 1:
                            nxtq.append(score(1))
                        for i, (kt, pi) in enumerate(kts):
                            ps, pk = nxtq.pop(0)
                            if i + 2 < len(kts):
                                nxtq.append(score(i + 2))
                            pt, ptk = rP.next()
                            p.act(lambda e, pt=pt, ps=ps: e.activation(out=pt[:, 0:n], in_=ps[:, 0:n], func=AF.Exp, scale=SCALE), reads=[pk], writes=[ptk])
                            if pi is not None:
                                qb_, j_ = NA_PAIRS[pi]
                                e0 = 8 * qb_ - 2 * j_ + 7
                                pm, pmk = rM.next()
                                p.dve(lambda e, pm=pm, pt=pt, e0=e0: e.tensor_tensor(out=pm[:, 0:n], in0=pt[:, 0:n], in1=g2[:, (e0 + 7) * 64:(e0 + 7) * 64 + 512], op=ALU.mult),
                                      reads=[ptk, g2k], writes=[pmk])
                                p.pool(lambda e, pm=pm, pi=pi: e.tensor_tensor(out=pm[:, 0:n], in0=pm[:, 0:n], in1=rmask[:, pi, :], op=ALU.mult), reads=[pmk, "rmask"], writes=[pmk])
                                pt, ptk = pm, pmk
                            first = (i == 0)
                            lastk = (i == len(kts) - 1)
                            p.pe(lambda e, pt=pt, kt=kt, first=first, lastk=lastk: e.matmul(ps_o[:, 0:n], lhsT=V_[:, kt, :], rhs=pt[:, 0:n], start=first, stop=lastk),
                                 reads=[Vk, ptk], writes=[pok])
                            p.pe(lambda e, pt=pt, first=first, lastk=lastk: e.matmul(ps_d[:, 0:n], lhsT=onesb[:], rhs=pt[:, 0:n], start=first, stop=lastk),
                                 reads=["onesb", ptk], writes=[pdk])
                        p.dve(lambda e: e.reciprocal(out=rden[:, 0:n], in_=ps_d[:, 0:n]), reads=[pdk], writes=["rden"])
                        p.dve(lambda e: e.tensor_tensor(out=of[:, 0:n], in0=ps_o[:, 0:n], in1=rden[:, 0:n], op=ALU.mult), reads=[pok, "rden"], writes=["of"])
                        y, yk = rY.next()
                        p.pool(lambda e, y=y: e.tensor_tensor(out=y[:, 0:n], in0=of[:, 0:n], in1=S_[:, q0:q0 + n], op=ALU.mult), reads=["of", Sk], writes=[yk])
                        p.dma("sp", "yst%d" % yk[1], yT[mrow + h * 128:mrow + (h + 1) * 128, q0:q0 + n], y[:, 0:n], reads=[yk], writes=["yT"])
            p.barrier()
            p.release()

        attention("B")
        if stop_after == "gqa":
            break
        attention("D")
        if stop_after == "na":
            break

        p.mark()
        rtab = p.sb([128, 6, 128], F32, "rtab")
        p.dma("sp", "const", rtab[:], hc["ret_tabs"], writes=["rtab"])
        pidx = p.sb([128, 3], F32, "pidx")
        p.dma("sp", "const", pidx[:], hc["ret_pidx"], writes=["pidx"])
        LG = p.sb([128, 16], F32, "LG")
        p.dma("sp", "const", LG[:], ret_dec[l].partition_broadcast(128), writes=["LG"])
        p.act(lambda e: e.activation(out=LG[:], in_=LG[:], func=AF.Exp, scale=-1.0), reads=["LG"], writes=["LG"])
        p.act(lambda e: e.activation(out=LG[:], in_=LG[:], func=AF.Ln, bias=1.0, scale=1.0), reads=["LG"], writes=["LG"])
        p.dve(lambda e: e.tensor_scalar(out=LG[:], in0=LG[:], scalar1=-1.0, scalar2=None, op0=ALU.mult), reads=["LG"], writes=["LG"])
        Dm = p.sb([128, 16, 128], F32, "Dm")
        qdec = p.sb([128, 16, 128], F32, "qdec")
        kdec = p.sb([128, 16], F32, "kdec")
        cdec = p.sb([128, 16], F32, "cdec")
        for hd in range(16):
            d = hd // 8
            p.act(lambda e, hd=hd, d=d: e.activation(out=Dm[:, hd, :], in_=rtab[:, d, :], func=AF.Exp, scale=LG[:, hd:hd + 1]), reads=["rtab", "LG"], writes=["Dm"])
            p.dve(lambda e, hd=hd, d=d: e.tensor_tensor(out=Dm[:, hd, :], in0=Dm[:, hd, :], in1=rtab[:, 2 + d, :], op=ALU.mult), reads=["Dm", "rtab"], writes=["Dm"])
            p.act(lambda e, hd=hd, d=d: e.activation(out=qdec[:, hd, :], in_=rtab[:, 4 + d, :], func=AF.Exp, scale=LG[:, hd:hd + 1]), reads=["rtab", "LG"], writes=["qdec"])
            p.act(lambda e, hd=hd, d=d: e.activation(out=kdec[:, hd:hd + 1], in_=pidx[:, d:d + 1], func=AF.Exp, scale=LG[:, hd:hd + 1]), reads=["pidx", "LG"], writes=["kdec"])
            p.act(lambda e, hd=hd: e.activation(out=cdec[:, hd:hd + 1], in_=pidx[:, 2:3], func=AF.Exp, scale=LG[:, hd:hd + 1]), reads=["pidx", "LG"], writes=["cdec"])
        KT = [p.sb([128, TALL], BF16, f"rKT{i}") for i in range(2)]
        rK = Rot(KT, "rKT")
        QT = [p.sb([128, TALL], BF16, f"rQT{i}") for i in range(2)]
        rQ = Rot(QT, "rQT")
        VV = [p.sb([128, 18, 128], BF16, f"rVV{i}") for i in range(2)]
        rV = Rot(VV, "rVV")
        SG = [p.sb([128, TALL], BF16, f"rSG{i}") for i in range(2)]
        rS = Rot(SG, "rSG")
        Kd = [p.sb([128, 18, 128], BF16, f"Kd{i}") for i in range(2)]
        oacc = p.sb([128, 18, 128], F32, "oacc")
        osq = p.sb([128, 18, 128], F32, "osq")
        onb = p.sb([128, 18, 128], BF16, "onb")
        gst = p.sb([128, 4, 18], F32, "gst")
        attm = [p.sb([128, 128], BF16, f"attm{i}") for i in range(3)]
        rAt = Rot(attm, "attm")
        qdb = [p.sb([128, 128], BF16, f"qdb{i}") for i in range(3)]
        rQd = Rot(qdb, "qdb")
        S32 = [p.sb([128, 128], F32, f"S32{i}") for i in range(2)]
        Sbf = [p.sb([128, 128], BF16, f"Sbf{i}") for i in range(2)]
        yf = p.sb([128, 512], F32, "yf")
        yo = [p.sb([128, 512], BF16, f"ryo{i}") for i in range(3)]
        rY = Rot(yo, "ryo")
        rl = {}

        def load_ret(h):
            K_, Kk = rK.next()
            p.dma("sp", "K%d" % Kk[1], K_[:], kT["C"][h * 128:(h + 1) * 128, :], reads=["scr"], writes=[Kk])
            Q_, Qk = rQ.next()
            p.dma("sp", "Q%d" % Qk[1], Q_[:], qT["C"][h * 128:(h + 1) * 128, :], reads=["scr"], writes=[Qk])
            V_, Vk = rV.next()
            p.dma("sp", "V%d" % Vk[1], V_[:], vS["C"][:, h * 128:(h + 1) * 128].rearrange("(t p) d -> p t d", p=128), reads=["scr"], writes=[Vk])
            S_, Sk = rS.next()
            p.dma("sp", "S%d" % Sk[1], S_[:], sgT[2 * GW + h * 128:2 * GW + (h + 1) * 128, :], reads=["scr"], writes=[Sk])
            rl[h] = (K_, Kk, Q_, Qk, V_, Vk, S_, Sk)
        load_ret(0)
        oaccB = p.sb([128, 18, 128], F32, "oaccB")
        for h in range(8):
            K_, Kk, Q_, Qk, V_, Vk, S_, Sk = rl[h]
            if h + 1 < 8:
                load_ret(h + 1)
            for c in range(18):
                pt, ptk = rT.next()
                p.pe(lambda e, pt=pt, c=c: e.transpose(out=pt[:, 0, :], in_=K_[:, c * 128:(c + 1) * 128], identity=ident[:]), reads=[Kk, "ident"], writes=[ptk])
                for d in range(2):
                    hd = d * 8 + h
                    p.dve(lambda e, pt=pt, c=c, d=d, hd=hd: e.tensor_scalar(out=Kd[d][:, c, :], in0=pt[:, 0, :], scalar1=kdec[:, hd:hd + 1], scalar2=None, op0=ALU.mult),
                          reads=[ptk, "kdec"], writes=[("Kd", d)])
            orders = [list(range(18)), [1, 0] + list(range(17, 1, -1))]
            for ci in range(18):
              for d in range(2):
                hd = d * 8 + h
                c = orders[d][ci]
                s32, sbf = S32[d], Sbf[d]
                s32k, sbfk = ("S32", d), ("Sbf", d)
                if True:
                    need_o = not (last and c < 2)
                    if need_o:
                        ps, pk = rA.next()
                        p.pe(lambda e, ps=ps, c=c: e.matmul(ps[:, 0:128], lhsT=K_[:, c * 128:(c + 1) * 128], rhs=Q_[:, c * 128:(c + 1) * 128], start=True, stop=True),
                             reads=[Kk, Qk], writes=[pk])
                        at, atk = rAt.next()
                        p.dve(lambda e, at=at, ps=ps, hd=hd: e.tensor_tensor(out=at[:], in0=ps[:, 0:128], in1=Dm[:, hd, :], op=ALU.mult), reads=[pk, "Dm"], writes=[atk])
                        po, pok = rB.next()
                        p.pe(lambda e, po=po, at=at, c=c, ci=ci: e.matmul(po[:, 0:128], lhsT=at[:], rhs=V_[:, c, :], start=True, stop=(ci == 0)), reads=[atk, Vk], writes=[pok])
                        if ci > 0:
                            qd, qdk = rQd.next()
                            p.pool(lambda e, qd=qd, c=c, hd=hd: e.tensor_tensor(out=qd[:], in0=Q_[:, c * 128:(c + 1) * 128], in1=qdec[:, hd, :], op=ALU.mult), reads=[Qk, "qdec"], writes=[qdk])
                            p.pe(lambda e, po=po, qd=qd, sbf=sbf: e.matmul(po[:, 0:128], lhsT=qd[:], rhs=sbf[:], start=False, stop=True), reads=[qdk, sbfk], writes=[pok])
                        if d == 0:
                            p.act(lambda e, po=po, c=c: e.activation(out=oacc[:, c, :], in_=po[:, 0:128], func=AF.Copy), reads=[pok], writes=["oacc"])
                        else:
                            p.act(lambda e, po=po, c=c: e.activation(out=oaccB[:, c, :], in_=po[:, 0:128], func=AF.Copy), reads=[pok], writes=["oaccB"])
                    if ci < 17:
                        pkv, pkvk = rA.next()
                        p.pe(lambda e, pkv=pkv, c=c, d=d: e.matmul(pkv[:, 0:128], lhsT=Kd[d][:, c, :], rhs=V_[:, c, :], start=True, stop=True), reads=[("Kd", d), Vk], writes=[pkvk])
                        if ci == 0:
                            p.dve(lambda e, pkv=pkv, s32=s32: e.tensor_copy(out=s32[:], in_=pkv[:, 0:128]), reads=[pkvk], writes=[s32k])
                        else:
                            p.dve(lambda e, pkv=pkv, s32=s32, hd=hd: e.scalar_tensor_tensor(out=s32[:], in0=s32[:], scalar=cdec[:, hd:hd + 1], in1=pkv[:, 0:128], op0=ALU.mult, op1=ALU.add),
                                  reads=[pkvk, s32k, "cdec"], writes=[s32k])
                        p.act(lambda e, s32=s32, sbf=sbf: e.activation(out=sbf[:], in_=s32[:], func=AF.Copy), reads=[s32k], writes=[sbfk])
            c_lo = 2 if last else 0
            ncn = 18 - c_lo
            ov = oacc[:, c_lo:18, :]
            p.dve(lambda e, ov=ov: e.tensor_tensor(out=ov, in0=ov, in1=oaccB[:, c_lo:18, :], op=ALU.add), reads=["oacc", "oaccB"], writes=["oacc"])
            p.dve(lambda e, ov=ov: e.tensor_reduce(out=gst[:, 0, c_lo:18], in_=ov, axis=AX.X, op=ALU.add), reads=["oacc"], writes=["gst"])
            p.act(lambda e, ov=ov: e.activation(out=osq[:, c_lo:18, :], in_=ov, func=AF.Square), reads=["oacc"], writes=["osq"])
            p.dve(lambda e: e.tensor_reduce(out=gst[:, 1, c_lo:18], in_=osq[:, c_lo:18, :], axis=AX.X, op=ALU.add), reads=["osq"], writes=["gst"])
            p.dve(lambda e: e.tensor_scalar(out=gst[:, 0, :], in0=gst[:, 0, :], scalar1=1.0 / HD, scalar2=None, op0=ALU.mult), reads=["gst"], writes=["gst"])
            p.dve(lambda e: e.tensor_tensor(out=gst[:, 2, :], in0=gst[:, 0, :], in1=gst[:, 0, :], op=ALU.mult), reads=["gst"], writes=["gst"])
            p.dve(lambda e: e.scalar_tensor_tensor(out=gst[:, 1, :], in0=gst[:, 1, :], scalar=1.0 / HD, in1=gst[:, 2, :], op0=ALU.mult, op1=ALU.subtract), reads=["gst"], writes=["gst"])
            p.act(lambda e: e.activation(out=gst[:, 1, c_lo:18], in_=gst[:, 1, c_lo:18], func=AF.Sqrt, bias=LEPS, scale=1.0), reads=["gst"], writes=["gst"])
            p.dve(lambda e: e.reciprocal(out=gst[:, 1, c_lo:18], in_=gst[:, 1, c_lo:18]), reads=["gst"], writes=["gst"])
            p.dve(lambda e, ov=ov: e.tensor_tensor(out=ov, in0=ov, in1=gst[:, 0, c_lo:18].unsqueeze(2).to_broadcast([128, ncn, 128]), op=ALU.subtract), reads=["oacc", "gst"], writes=["oacc"])
            p.dve(lambda e, ov=ov: e.tensor_tensor(out=onb[:, c_lo:18, :], in0=ov, in1=gst[:, 1, c_lo:18].unsqueeze(2).to_broadcast([128, ncn, 128]), op=ALU.mult), reads=["oacc", "gst"], writes=["onb"])
            for c4 in range(c_lo, 18, 4):
                nn = min(4, 18 - c4)
                pt, ptk = rT.next()
                for cc in range(nn):
                    p.pe(lambda e, pt=pt, cc=cc, c4=c4: e.transpose(out=pt[:, cc, :], in_=onb[:, c4 + cc, :], identity=ident[:]), reads=["onb", "ident"], writes=[ptk])
                w = nn * 128
                p.dve(lambda e, pt=pt, w=w, nn=nn: e.tensor_scalar(out=yf[:, 0:w], in0=pt[:, 0:nn, :].rearrange("p a b -> p (a b)"), scalar1=small[:, 24 + h:25 + h], scalar2=small[:, 32 + h:33 + h], op0=ALU.mult, op1=ALU.add),
                      reads=[ptk, "small"], writes=["yf"])
                y, yk = rY.next()
                p.pool(lambda e, y=y, w=w, c4=c4: e.tensor_tensor(out=y[:, 0:w], in0=yf[:, 0:w], in1=S_[:, c4 * 128:c4 * 128 + w], op=ALU.mult), reads=["yf", Sk], writes=[yk])
                p.dma("sp", "yst%d" % yk[1], yT[2 * GW + h * 128:2 * GW + (h + 1) * 128, c4 * 128:c4 * 128 + w], y[:, 0:w], reads=[yk], writes=["yT"])
        p.barrier()
        p.release()
        if stop_after == "ret":
            break

        for (tok0, T) in groups:
            p.mark()
            yTg = p.sb([128, 32, T], BF16, "yTg")
            p.dma("sp", "ytg", yTg[:], yT[:, tok0:tok0 + T].rearrange("(k p) t -> p k t", p=128), reads=["yT"], writes=["yTg"])
            gtb = p.sb([128, D], F32, "gtb")
            wbuf = [p.sb([128, 32, 512], BF16, f"wo{i}") for i in range(2)]
            rW = Rot(wbuf, "wo")
            xs = [p.sb([128, 512], F32, f"xs{i}") for i in range(3)]
            rXs = Rot(xs, "xs")
            ts = [p.sb([128, 512], F32, f"ts{i}") for i in range(2)]
            rTs = Rot(ts, "ts")
            tiles = [tt for tt in range(T // 128) if not (last and tok0 + tt * 128 < CTX)]
            cur_row = None
            for nb in range(8):
                wb, wk = rW.next()
                p.dma("pool", "w%d" % wk[1], wb[:], w_out[l, :, nb * 512:(nb + 1) * 512].rearrange("(k p) n -> p k n", p=128), writes=[wk])
                for tt in tiles:
                    g0 = tok0 + tt * 128
                    is_ctx = g0 < CTX
                    row = 1 if is_ctx else 0
                    if row != cur_row:
                        p.dma("sp", "modld", gtb[:], mod[l, row, 2 * D:3 * D].partition_broadcast(128), reads=["mod"], writes=["gtb"])
                        cur_row = row
                    ps, pk = rA.next()
                    for k in range(32):
                        p.pe(lambda e, ps=ps, k=k, tt=tt, wb=wb: e.matmul(ps[:], lhsT=yTg[:, k, tt * 128:(tt + 1) * 128], rhs=wb[:, k, :], start=(k == 0), stop=(k == 31)),
                             reads=["yTg", wk], writes=[pk])
                    if l == 0:
                        src = ctx_in[g0:g0 + 128, nb * 512:(nb + 1) * 512] if is_ctx else x_in[g0 - CTX:g0 - CTX + 128, nb * 512:(nb + 1) * 512]
                    else:
                        src = xres[g0:g0 + 128, nb * 512:(nb + 1) * 512]
                    x_, xk = rXs.next()
                    p.dma("act", "xs%d" % xk[1], x_[:], src, reads=["xres_r"], writes=[xk])
                    t_, tk = rTs.next()
                    p.dve(lambda e, t_=t_, ps=ps, nb=nb: e.tensor_tensor(out=t_[:], in0=ps[:], in1=gtb[:, nb * 512:(nb + 1) * 512], op=ALU.mult), reads=[pk, "gtb"], writes=[tk])
                    p.dve(lambda e, t_=t_, x_=x_: e.tensor_tensor(out=x_[:], in0=x_[:], in1=t_[:], op=ALU.add), reads=[tk, xk], writes=[xk])
                    p.dma("sp", "xst%d" % xk[1], xres[g0:g0 + 128, nb * 512:(nb + 1) * 512], x_[:], reads=[xk], writes=["xres_w"])
            p.barrier()
            p.release()
        if stop_after == "l0":
            break

    if stop_after is None:
        p.mark()
        fgb = p.sb([128, D], F32, "fgb")
        p.dma("sp", "const", fgb[:], final_g.partition_broadcast(128), writes=["fgb"])
        xt = [p.sb([128, D], F32, f"fxt{i}") for i in range(3)]
        rX = Rot(xt, "fxt")
        junk = p.sb([128, D], BF16, "fjunk")
        st = p.sb([128, 8], F32, "fst")
        for tt in range(SEQ // 128):
            xb_, xk = rX.next()
            p.dma("pool", "x%d" % xk[1], xb_[:], xres[CTX + tt * 128:CTX + (tt + 1) * 128, :], reads=["xres_w"], writes=[xk])
            p.act(lambda e, xb_=xb_: e.activation(out=junk[:], in_=xb_[:], func=AF.Square, accum_out=st[:, 0:1]), reads=[xk], writes=["junk", "st0"])
            p.act(lambda e: e.activation(out=st[:, 1:2], in_=st[:, 0:1], func=AF.Sqrt, bias=NEPS, scale=1.0 / D), reads=["st0"], writes=["st1"])
            p.dve(lambda e: e.reciprocal(out=st[:, 2:3], in_=st[:, 1:2]), reads=["st1"], writes=["st2"])
            p.dve(lambda e, xb_=xb_: e.scalar_tensor_tensor(out=xb_[:], in0=xb_[:], scalar=st[:, 2:3], in1=fgb[:], op0=ALU.mult, op1=ALU.mult),
                  reads=[xk, "st2", "fgb"], writes=[xk])
            p.dma("sp", "ost%d" % xk[1], out[tt * 128:(tt + 1) * 128, :], xb_[:], reads=[xk], writes=["out"])
        p.release()
    p.barrier()
    stats = p.emit()
    nc_ctx.__exit__(None, None, None)
    return nc, cs, stats, dict(mod=mod, xres=xres, uT=uT, sgT=sgT, qT=qT, kT=kT, vS=vS, yT=yT)


def make_in_maps(inputs, cs):
    bf = ml_dtypes.bfloat16
    f = lambda a: np.ascontiguousarray(np.asarray(a, dtype=np.float32))
    shared = {
        "ada_w": f(inputs["ada_w"]), "ada_b": f(inputs["ada_b"]), "norm_g": f(inputs["norm_g"]),
        "w_in": f(inputs["w_in"]), "conv_w": f(inputs["conv_w"]), "conv_b": f(inputs["conv_b"]),
        "conv_ln_g": f(inputs["conv_ln_g"]), "conv_ln_b": f(inputs["conv_ln_b"]),
        "gqa_qn_g": f(inputs["gqa_qn_g"]), "gqa_kn_g": f(inputs["gqa_kn_g"]),
        "ret_dec": f(np.concatenate([inputs["ret_decay_fwd"], inputs["ret_decay_bwd"]], axis=1)),
        "ret_gn_g": f(inputs["ret_gn_g"]), "ret_gn_b": f(inputs["ret_gn_b"]),
        "na_bT": np.ascontiguousarray(np.transpose(np.asarray(inputs["na_bias"], np.float32)[:, :, ::-1, :], (0, 3, 1, 2))),
        "w_out": f(inputs["w_out"]), "final_g": f(inputs["final_g"]),
    }
    for k, v in cs.items():
        shared["k_" + k] = v
    maps = []
    for core in range(8):
        b = core // 2
        m = dict(shared)
        m["x"] = f(inputs["x"][b])
        m["ctx"] = f(inputs["ctx"][b])
        m["c2"] = f(np.stack([inputs["c"][b], inputs["c_ctx"]], axis=0))
        maps.append(m)
    return maps


_CACHE = {}


def kernel(**inputs):
    if "prog" not in _CACHE:
        _CACHE["prog"] = build_program()
    nc, cs, stats, _ = _CACHE["prog"]
    maps = make_in_maps(inputs, cs)
    res = run_bass_kernel_spmd(nc, maps, core_ids=list(range(8)))
    outs = [res.results[2 * b]["out"] for b in range(4)]
    return np.stack(outs, axis=0).astype(np.float32)
```

```python
import contextlib
import numpy as np
import ml_dtypes
import concourse.bass as bass
import concourse.mybir as mybir
from concourse.bass_utils import run_bass_kernel_spmd

F32 = mybir.dt.float32
BF16 = mybir.dt.bfloat16
AF = mybir.ActivationFunctionType
ALU = mybir.AluOpType
AX = mybir.AxisListType

D = 4096
SEQ = 2048
CTX = 256
TALL = SEQ + CTX
DEPTH = 2
GW = 1024
HD = 128
PT = 13824
GRID_W = 64
SCALE = HD ** -0.5
NEPS = 1e-6
LEPS = 1e-5
SAME_ENG_SYNC = True

OFF = dict(A_a=0, A_g=1024, A_gate=2048, B_q=3072, B_k=4096, B_v=4352, B_gate=4608,
           C_q=5632, C_k=6656, C_v=7680, C_gate=8704, D_q=9728, D_k=10752, D_v=11776, D_gate=12800)


class _Op:
    __slots__ = ("eng", "fn", "deps", "dsem", "cnt", "sig", "idx")


class _Rec:
    def __init__(self):
        self.call = None

    def __getattr__(self, name):
        def f(*a, **k):
            self.call = (name, a, k)
            return self
        return f


class Prog:
    def __init__(self, nc):
        self.nc = nc
        self.ops = []
        self.state = {}
        self.dcnt = {}
        self.sb_off = 20480
        self.sb_marks = []
        self.ntens = 0
        self.debug = False
        self.last_dma = {}

    def sb(self, shape, dtype, name=None):
        esz = 4 if dtype == F32 else 2
        n = 1
        for s in shape[1:]:
            n *= s
        nbytes = (n * esz + 31) // 32 * 32
        self.ntens += 1
        t = self.nc.alloc_sbuf_tensor_at(f"{name or 't'}_{self.ntens}", list(shape), dtype, offset=self.sb_off)
        self.sb_off += nbytes
        assert self.sb_off <= 229376, f"SBUF overflow {self.sb_off} at {name}"
        return t

    def mark(self):
        self.sb_marks.append(self.sb_off)

    def release(self):
        self.sb_off = self.sb_marks.pop()

    def _add(self, eng, fn, reads, writes, dsem=None):
        op = _Op()
        if fn is not None:
            rec = _Rec()
            fn(rec)
            c = rec.call
            fn = lambda e, c=c: getattr(e, c[0])(*c[1], **c[2])
        op.eng = eng; op.fn = fn; op.dsem = dsem; op.idx = len(self.ops); op.sig = None
        deps = []

        def push(lst, o):
            lst[:] = [w for w in lst if not ((w.dsem is None and o.dsem is None and w.eng == o.eng)
                                             or (w.dsem is not None and w.dsem == o.dsem))]
            lst.append(o)
        for k in reads:
            st = self.state.setdefault(k, [[], [], []])
            deps.extend(st[1])
            push(st[2], op)
        for k in writes:
            st = self.state.setdefault(k, [[], [], []])
            if st[2]:
                st[0] = st[2]; old_w = st[1]; st[1] = [op]; st[2] = []
                deps.extend(st[0]); deps.extend(old_w)
            else:
                deps.extend(st[0]); deps.extend(st[1])
                push(st[1], op)
        if dsem is not None:
            prev = self.last_dma.get(dsem)
            if prev is not None:
                deps.append(prev)
            self.last_dma[dsem] = op
        op.deps = [d for d in deps if d is not op]
        if dsem is not None:
            self.dcnt[dsem] = self.dcnt.get(dsem, 0) + 16
            op.cnt = self.dcnt[dsem]
        self.ops.append(op)
        return op

    def pe(self, fn, reads=(), writes=()):
        return self._add("pe", fn, reads, writes)

    def act(self, fn, reads=(), writes=()):
        return self._add("act", fn, reads, writes)

    def dve(self, fn, reads=(), writes=()):
        return self._add("dve", fn, reads, writes)

    def pool(self, fn, reads=(), writes=()):
        return self._add("pool", fn, reads, writes)

    def dma(self, q, dsem, out, in_, reads=(), writes=(), **kw):
        return self._add(q, lambda e: e.dma_start(out=out, in_=in_, **kw), reads, writes, dsem=dsem)

    def barrier(self):
        alld = {}
        for st in self.state.values():
            for lst in st:
                for o in lst:
                    alld[o.idx] = o
        alld = list(alld.values())
        for eng in ("pe", "act", "dve", "pool", "sp"):
            op = self._add(eng, None, (), ())
            op.deps = list(alld)
        self.state = {}

    def wait_all(self, eng, keys):
        return self._add(eng, None, list(keys), ())

    def emit(self):
        nc = self.nc
        ops = self.ops
        need = set()
        for op in ops:
            for d in op.deps:
                if d.dsem is None:
                    if d.eng == op.eng and op.dsem is None and (d.eng == "pe" or not SAME_ENG_SYNC):
                        continue
                    need.add(d.idx)
        cnt = {e: 0 for e in ("pe", "act", "dve", "pool", "sp")}
        for op in ops:
            if op.dsem is None:
                if op.fn is not None and op.idx in need:
                    cnt[op.eng] += 1
                    op.sig = True
                    op.cnt = cnt[op.eng]
                else:
                    op.sig = False
                    op.cnt = None
        dsems = sorted(self.dcnt.keys())
        import bisect
        dlist = {d: ([], []) for d in dsems}
        for op in ops:
            if op.dsem is not None:
                dlist[op.dsem][0].append(op.idx)
                dlist[op.dsem][1].append(op.cnt)

        def dma_wait_val(d, idx):
            ii, cc = dlist[d]
            j = bisect.bisect_left(ii, idx)
            return cc[j - 1] if j > 0 else 0
        with contextlib.ExitStack() as es:
            esem = {e: es.enter_context(nc.semaphore(f"s_{e}")) for e in ("pe", "act", "dve", "pool")}
            dsem = {d: es.enter_context(nc.semaphore(f"d_{d}")) for d in dsems}
            block = es.enter_context(nc.Block())

            def stream(engname):
                def body(eng):
                    waited = {}
                    for op in ops:
                        if op.eng != engname:
                            continue
                        w = {}
                        for d in op.deps:
                            if d.dsem is not None:
                                key = ("d", d.dsem); val = max(d.cnt, dma_wait_val(d.dsem, op.idx))
                            else:
                                if d.fn is None:
                                    continue
                                if d.eng == engname and op.dsem is None and (engname == "pe" or not SAME_ENG_SYNC):
                                    continue
                                key = ("e", d.eng); val = d.cnt
                            if val > w.get(key, 0):
                                w[key] = val
                        for key, val in w.items():
                            if waited.get(key, 0) >= val:
                                continue
                            waited[key] = val
                            s = dsem[key[1]] if key[0] == "d" else esem[key[1]]
                            eng.wait_ge(s, val)
                        if op.fn is None:
                            continue
                        ins = op.fn(eng)
                        if op.dsem is not None:
                            ins.then_inc(dsem[op.dsem], 16)
                        elif op.sig:
                            ins.then_inc(esem[engname], 1)
                return body

            block.tensor(stream("pe"))
            block.scalar(stream("act"))
            block.vector(stream("dve"))
            block.gpsimd(stream("pool"))
            block.sync(stream("sp"))
        return cnt, {d: self.dcnt[d] for d in dsems}


class Rot:
    def __init__(self, bufs, name):
        self.bufs = bufs
        self.name = name
        self.i = 0

    def next(self):
        b = self.bufs[self.i % len(self.bufs)]
        k = (self.name, self.i % len(self.bufs))
        self.i += 1
        return b, k


def _r_start(r, rows=32, kr=8):
    return min(max(r - kr // 2, 0), rows - kr)


NA_PAIRS = []
for _qb in range(4):
    for _j in range(16):
        lo = min(_r_start(r) for r in range(8 * _qb, 8 * _qb + 8))
        hi = max(_r_start(r) + 7 for r in range(8 * _qb, 8 * _qb + 8))
        if 2 * _j + 1 >= lo and 2 * _j <= hi:
            NA_PAIRS.append((_qb, _j))


def host_consts():
    c = {}
    bf = ml_dtypes.bfloat16
    c["ident"] = np.eye(128, dtype=np.float32).astype(bf)
    c["identf"] = np.eye(128, dtype=np.float32)
    c["onesf"] = np.ones((128, 128), np.float32)
    c["onesb"] = np.ones((128, 128), np.float32).astype(bf)
    t = np.arange(SEQ)
    row = (t // GRID_W).astype(np.float32)
    col = (t % GRID_W).astype(np.float32)
    half = HD // 2
    inv_freq = (np.float32(10000.0) ** (-np.arange(0, half, 2, dtype=np.float32) / np.float32(half))).astype(np.float32)
    ang_r = row[:, None] * inv_freq[None, :]
    ang_c = col[:, None] * inv_freq[None, :]
    ang = np.concatenate([ang_r, ang_r, ang_c, ang_c], axis=-1).astype(np.float32)
    cosT = np.ones((128, TALL), np.float32)
    sinT = np.zeros((128, TALL), np.float32)
    cosT[:, CTX:] = np.cos(ang).T
    sinT[:, CTX:] = np.sin(ang).T
    c["cosT"] = cosT
    c["sinT"] = sinT
    R = np.zeros((128, 128), np.float32)
    for i in range(32):
        R[i, 32 + i] = -1.0
        R[32 + i, i] = 1.0
        R[64 + i, 96 + i] = -1.0
        R[96 + i, 64 + i] = 1.0
    c["permT"] = R.T.copy().astype(bf)
    a = np.arange(128, dtype=np.float32)
    diff = a[None, :] - a[:, None]
    c["ret_tabs"] = np.stack([
        diff, -diff,
        (diff >= 0).astype(np.float32), (diff <= 0).astype(np.float32),
        np.broadcast_to(a[None, :] + 1.0, (128, 128)), np.broadcast_to(128.0 - a[None, :], (128, 128)),
    ], axis=1).astype(np.float32)
    c["ret_pidx"] = np.stack([127.0 - a, a, np.full(128, 128.0, np.float32)], axis=1).astype(np.float32)
    rm = np.zeros((128, len(NA_PAIRS), 512), np.float32)
    for pi, (qb, j) in enumerate(NA_PAIRS):
        for i in range(2):
            for jj in range(8):
                kr = 2 * j + i
                qr = 8 * qb + jj
                rs = _r_start(qr)
                if rs <= kr <= rs + 7:
                    rm[64 * i:64 * i + 64, pi, 64 * jj:64 * jj + 64] = 1.0
    c["na_rmask"] = rm.astype(bf)
    cv = np.zeros((64, 64), np.float32)
    for qc in range(64):
        cs = min(max(qc - 8, 0), 48)
        cv[cs:cs + 16, qc] = 1.0
    c["na_cv"] = np.concatenate([cv, cv], 0)
    z = np.zeros((31, 64, 128), np.float32)
    for qc in range(64):
        for p in range(128):
            j = (p % 64) - qc + 15
            if 0 <= j < 31:
                z[j, qc, p] = 1.0
    c["na_z"] = z.astype(bf)
    return c


CONST_SPECS = None


def build_program(stop_after=None, dbg=False):
    nc = bass.Bass("TRN2", target_bir_lowering=False)
    p = Prog(nc)

    def din(name, shape, dt=F32):
        return nc.dram_tensor(name, list(shape), dt, kind="ExternalInput").ap()

    def dscr(name, shape, dt):
        if dbg:
            return nc.dram_tensor(name, list(shape), dt, kind="ExternalOutput").ap()
        return nc.dram_tensor(name, list(shape), dt).ap()

    x_in = din("x", [SEQ, D])
    ctx_in = din("ctx", [CTX, D])
    c2_in = din("c2", [2, D])
    ada_w = din("ada_w", [DEPTH, D, 3 * D])
    ada_b = din("ada_b", [DEPTH, 3 * D])
    norm_g = din("norm_g", [DEPTH, D])
    w_in = din("w_in", [DEPTH, D, PT])
    conv_w = din("conv_w", [DEPTH, 31, GW])
    conv_b = din("conv_b", [DEPTH, GW])
    conv_ln_g = din("conv_ln_g", [DEPTH, GW])
    conv_ln_b = din("conv_ln_b", [DEPTH, GW])
    gqa_qn_g = din("gqa_qn_g", [DEPTH, HD])
    gqa_kn_g = din("gqa_kn_g", [DEPTH, HD])
    ret_dec = din("ret_dec", [DEPTH, 16])
    ret_gn_g = din("ret_gn_g", [DEPTH, GW])
    ret_gn_b = din("ret_gn_b", [DEPTH, GW])
    na_bT = din("na_bT", [DEPTH, 31, 8, 15])
    w_out = din("w_out", [DEPTH, D, D])
    final_g = din("final_g", [D])
    hc = {}
    cs = host_consts()
    for k, v in cs.items():
        hc[k] = din("k_" + k, v.shape, BF16 if v.dtype == ml_dtypes.bfloat16 else F32)
    out = nc.dram_tensor("out", [SEQ, D], F32, kind="ExternalOutput").ap()

    mod = dscr("mod", [DEPTH, 2, 3 * D], F32)
    xres = dscr("xres", [TALL, D], F32)
    uT = dscr("uT", [GW, TALL], BF16)
    sgT = dscr("sgT", [D, TALL], BF16)
    qT = {m: dscr("qT" + m, [GW, TALL], BF16) for m in "BCD"}
    kT = {"B": dscr("kTB", [256, TALL], BF16), "C": dscr("kTC", [GW, TALL], BF16), "D": dscr("kTD", [GW, TALL], BF16)}
    vS = {"B": dscr("vB", [TALL, 256], BF16), "C": dscr("vC", [TALL, GW], BF16), "D": dscr("vD", [TALL, GW], BF16)}
    yT = dscr("yT", [D, TALL], BF16)
    dbg_outs = {}

    psA = [nc.alloc_psum_tensor(f"psA{i}", [128, 512], F32) for i in range(4)]
    psB = [nc.alloc_psum_tensor(f"psB{i}", [128, 512], F32) for i in range(2)]
    psT = [nc.alloc_psum_tensor(f"psT{i}", [128, 8, 128], BF16) for i in range(2)]
    rA = Rot(psA, "psA")
    rB = Rot(psB, "psB")
    rT = Rot(psT, "psT")

    ident = p.sb([128, 128], BF16, "ident")
    identf = p.sb([128, 128], F32, "identf")
    onesf = p.sb([128, 128], F32, "onesf")
    onesb = p.sb([128, 128], BF16, "onesb")
    permT = p.sb([128, 128], BF16, "permT")
    for t, n in ((ident, "ident"), (identf, "identf"), (onesf, "onesf"), (onesb, "onesb"), (permT, "permT")):
        p.dma("sp", "const", t[:], hc[n], writes=[n])
    small = p.sb([128, 64], F32, "small")
    nc_ctx = nc.allow_non_contiguous_dma(reason="small per-channel vectors")
    nc_ctx.__enter__()

    def wload(q, dsem, dst, src, wkey):
        p.dma(q, dsem, dst, src, writes=[wkey])

    p.mark()
    cT = p.sb([128, 2, 32], F32, "cT")
    scT = p.sb([128, 2, 32], BF16, "scT")
    modsb = p.sb([2, 3 * D], F32, "modsb")
    tmp2 = p.sb([2, 3 * D], F32, "tmp2")
    wbuf = [p.sb([128, 32, 256], BF16, f"adaw{i}") for i in range(3)]
    rW = Rot(wbuf, "adaw")
    for r in range(2):
        p.dma("sp", "c2", cT[:, r, :], c2_in[r].rearrange("(k p) -> p k", p=128), writes=["cT"])
    p.act(lambda e: e.activation(out=scT[:], in_=cT[:], func=AF.Silu), reads=["cT"], writes=["scT"])
    for l in range(DEPTH):
        for nb in range(3 * D // 256):
            wb, wk = rW.next()
            p.dma("pool", "w%d" % wk[1], wb[:], ada_w[l, :, nb * 256:(nb + 1) * 256].rearrange("(k p) n -> p k n", p=128), writes=[wk])
            ps, pk = rA.next()
            for k in range(32):
                p.pe(lambda e, ps=ps, wb=wb, k=k: e.matmul(ps[0:2, 0:256], lhsT=scT[:, :, k], rhs=wb[:, k, :], start=(k == 0), stop=(k == 31)),
                     reads=["scT", wk], writes=[pk])
            p.dve(lambda e, ps=ps, nb=nb: e.tensor_copy(out=modsb[:, nb * 256:(nb + 1) * 256], in_=ps[0:2, 0:256]), reads=[pk], writes=["modsb"])
        p.dma("sp", "c2", tmp2[:], ada_b[l:l + 1, :].broadcast_to([2, 3 * D]) if False else ada_b[l].partition_broadcast(2), writes=["tmp2"])
        p.dve(lambda e: e.tensor_tensor(out=modsb[:], in0=modsb[:], in1=tmp2[:], op=ALU.add), reads=["modsb", "tmp2"], writes=["modsb"])
        p.dma("sp", "c2", tmp2[:, 0:D], norm_g[l].partition_broadcast(2), reads=[], writes=["tmp2"])
        p.dve(lambda e: e.scalar_tensor_tensor(out=modsb[:, D:2 * D], in0=modsb[:, D:2 * D], scalar=1.0, in1=tmp2[:, 0:D], op0=ALU.add, op1=ALU.mult),
              reads=["modsb", "tmp2"], writes=["modsb"])
        p.dma("sp", "modst", mod[l], modsb[:], reads=["modsb"], writes=["mod"])
    p.barrier()
    p.release()

    def load_small(l):
        def col(dst_c, vec, nb):
            p.dma("sp", "small", small[:, dst_c:dst_c + nb], vec.rearrange("(b p) -> p b", p=128), writes=["small"])
        col(0, conv_b[l], 8); col(8, conv_ln_g[l], 8); col(16, conv_ln_b[l], 8)
        col(24, ret_gn_g[l], 8); col(32, ret_gn_b[l], 8)
        col(40, gqa_qn_g[l], 1); col(41, gqa_kn_g[l], 1)

    groups = [(0, 1280), (1280, 1024)]

    for l in range(DEPTH):
        last = (l == DEPTH - 1)
        load_small(l)
        for (tok0, T) in groups:
            p.mark()
            hlT = p.sb([128, 32, T], BF16, "hlT")
            p.mark()
            xt = [p.sb([128, D], F32, f"xt{i}") for i in range(3)]
            rX = Rot(xt, "xt")
            junk = p.sb([128, D], BF16, "junk")
            hlbs = [p.sb([128, D], BF16, f"hlb{i}") for i in range(2)]
            rH = Rot(hlbs, "hlb")
            gsb = p.sb([128, D], F32, "gsb")
            shb = p.sb([128, D], F32, "shb")
            st = p.sb([128, 8], F32, "st")
            cur_row = None
            for tt in range(T // 128):
                g0 = tok0 + tt * 128
                is_ctx = g0 < CTX
                row = 1 if is_ctx else 0
                if row != cur_row:
                    p.dma("sp", "modg", gsb[:], mod[l, row, D:2 * D].partition_broadcast(128), reads=["mod"], writes=["gsb"])
                    p.dma("sp", "mods", shb[:], mod[l, row, 0:D].partition_broadcast(128), reads=["mod"], writes=["shb"])
                    cur_row = row
                if l == 0:
                    src = ctx_in[g0:g0 + 128, :] if is_ctx else x_in[g0 - CTX:g0 - CTX + 128, :]
                else:
                    src = xres[g0:g0 + 128, :]
                xb_, xk = rX.next()
                p.dma("sp", "x%d" % xk[1], xb_[:], src, reads=["xres"], writes=[xk])
                p.act(lambda e, xb_=xb_: e.activation(out=junk[:], in_=xb_[:], func=AF.Square, accum_out=st[:, 0:1]), reads=[xk], writes=["junk", "st0"])
                p.act(lambda e: e.activation(out=st[:, 1:2], in_=st[:, 0:1], func=AF.Sqrt, bias=NEPS, scale=1.0 / D), reads=["st0"], writes=["st1"])
                p.dve(lambda e: e.reciprocal(out=st[:, 2:3], in_=st[:, 1:2]), reads=["st1"], writes=["st2"])
                p.dve(lambda e, xb_=xb_: e.scalar_tensor_tensor(out=xb_[:], in0=xb_[:], scalar=st[:, 2:3], in1=gsb[:], op0=ALU.mult, op1=ALU.mult),
                      reads=[xk, "st2", "gsb"], writes=[xk])
                hlb, hlbk = rH.next()
                p.pool(lambda e, xb_=xb_, hlb=hlb: e.tensor_tensor(out=hlb[:], in0=xb_[:], in1=shb[:], op=ALU.add), reads=[xk, "shb"], writes=[hlbk])
                for k8 in range(4):
                    pt, ptk = rT.next()
                    for kk in range(8):
                        k = k8 * 8 + kk
                        p.pe(lambda e, pt=pt, kk=kk, k=k, hlb=hlb: e.transpose(out=pt[:, kk, :], in_=hlb[:, k * 128:(k + 1) * 128], identity=ident[:]),
                             reads=[hlbk, "ident"], writes=[ptk])
                    p.dve(lambda e, pt=pt, k8=k8, tt=tt: e.tensor_copy(out=hlT[:, k8 * 8:(k8 + 1) * 8, tt * 128:(tt + 1) * 128], in_=pt[:]),
                          reads=[ptk], writes=["hlT"])
            p.barrier()
            p.release()
            if stop_after == "norm":
                break
            p.mark()
            wbuf = [p.sb([128, 32, 256], BF16, f"winw{i}") for i in range(3)]
            rW = Rot(wbuf, "winw")
            cosT = p.sb([128, T], F32, "cosT")
            sinT = p.sb([128, T], F32, "sinT")
            p.dma("sp", "const", cosT[:], hc["cosT"][:, tok0:tok0 + T], writes=["cosT"])
            p.dma("sp", "const", sinT[:], hc["sinT"][:, tok0:tok0 + T], writes=["sinT"])
            ef = [p.sb([128, 512], F32, f"ef{i}") for i in range(10)]
            rE = Rot(ef, "ef")
            eb = [p.sb([128, 512], BF16, f"eb{i}") for i in range(4)]
            rEb = Rot(eb, "eb")
            ob = [p.sb([128, 512], BF16, f"ob{i}") for i in range(4)]
            rO = Rot(ob, "ob")
            chunks = [(c0, min(512, T - c0)) for c0 in range(0, T, 512)]
            if last and tok0 == 0:
                chunks_q = [(c0, min(512, T - c0)) for c0 in range(CTX, T, 512)]
            else:
                chunks_q = chunks
            pending = []

            def flush():
                cur = pending[:]
                del pending[:]
                for cb_ in cur:
                    cb_()

            def load_w(pieces):
                wb, wk = rW.next()
                o = 0
                for (c0, wd) in pieces:
                    p.dma("pool", "w%d" % wk[1], wb[:, :, o:o + wd], w_in[l, :, c0:c0 + wd].rearrange("(k p) n -> p k n", p=128), writes=[wk])
                    o += wd
                return wb, wk

            def mm_fm(wb, wk, j, c0, n):
                ps, pk = rA.next()
                for k in range(32):
                    p.pe(lambda e, ps=ps, k=k: e.matmul(ps[:, 0:n], lhsT=wb[:, k, j * 128:(j + 1) * 128], rhs=hlT[:, k, c0:c0 + n], start=(k == 0), stop=(k == 31)),
                         reads=["hlT", wk], writes=[pk])
                return ps, pk

            def store_fm(src, sk, dst, r0, c0, n):
                p.dma("sp", "st_%s%d" % sk, dst[r0:r0 + 128, tok0 + c0:tok0 + c0 + n], src[:, 0:n], reads=[sk], writes=["scr"])

            def rope_tail(qn, qnk, c0, n, dst, r0):
                qb, qbk = rEb.next()
                p.dve(lambda e: e.tensor_copy(out=qb[:, 0:n], in_=qn[:, 0:n]), reads=[qnk], writes=[qbk])
                pending.append(lambda: rope_tail2(qn, qnk, qb, qbk, c0, n, dst, r0))

            def rope_tail2(qn, qnk, qb, qbk, c0, n, dst, r0):
                ps2, p2k = rB.next()
                p.pe(lambda e: e.matmul(ps2[:, 0:n], lhsT=permT[:], rhs=qb[:, 0:n], start=True, stop=True), reads=["permT", qbk], writes=[p2k])
                t2, t2k = rE.next()
                p.dve(lambda e: e.tensor_tensor(out=t2[:, 0:n], in0=ps2[:, 0:n], in1=sinT[:, c0:c0 + n], op=ALU.mult), reads=[p2k, "sinT"], writes=[t2k])
                p.dve(lambda e: e.tensor_tensor(out=qn[:, 0:n], in0=qn[:, 0:n], in1=cosT[:, c0:c0 + n], op=ALU.mult), reads=[qnk, "cosT"], writes=[qnk])
                o, ok = rO.next()
                p.dve(lambda e: e.tensor_tensor(out=o[:, 0:n], in0=qn[:, 0:n], in1=t2[:, 0:n], op=ALU.add), reads=[qnk, t2k], writes=[ok])
                store_fm(o, ok, dst, r0, c0, n)

            for jb in range(8):
                wb, wk = load_w([(OFF["A_a"] + jb * 128, 128), (OFF["A_g"] + jb * 128, 128)])
                for (c0, n) in chunks_q:
                    pa, pak = mm_fm(wb, wk, 0, c0, n)
                    pg, pgk = mm_fm(wb, wk, 1, c0, n)
                    sg, sgk = rE.next()
                    p.act(lambda e, sg=sg, pg=pg, n=n: e.activation(out=sg[:, 0:n], in_=pg[:, 0:n], func=AF.Sigmoid), reads=[pgk], writes=[sgk])
                    o, ok = rO.next()
                    p.dve(lambda e, o=o, pa=pa, sg=sg, n=n: e.tensor_tensor(out=o[:, 0:n], in0=pa[:, 0:n], in1=sg[:, 0:n], op=ALU.mult), reads=[pak, sgk], writes=[ok])
                    store_fm(o, ok, uT, jb * 128, c0, n)
            for mi, m in enumerate("ABCD"):
                for jb in range(4):
                    wb, wk = load_w([(OFF[m + "_gate"] + jb * 256, 256)])
                    for j in range(2):
                        for (c0, n) in chunks_q:
                            ps, pk = mm_fm(wb, wk, j, c0, n)
                            o, ok = rO.next()
                            p.act(lambda e, o=o, ps=ps, n=n: e.activation(out=o[:, 0:n], in_=ps[:, 0:n], func=AF.Silu), reads=[pk], writes=[ok])
                            store_fm(o, ok, sgT, mi * GW + jb * 256 + j * 128, c0, n)
            fm_list = [("B", "q", "normrope", 1.0, 1024, 40), ("B", "k", "normrope", 1.0, 256, 41),
                       ("C", "q", "rope", 1.0, 1024, None), ("C", "k", "rope", SCALE, 1024, None),
                       ("D", "q", "plain", 1.0, 1024, None), ("D", "k", "plain", 1.0, 1024, None)]
            for (m, which, kind, scl, width, gcol) in fm_list:
                dst = qT[m] if which == "q" else kT[m]
                for jb in range(width // 256):
                    wb, wk = load_w([(OFF[m + "_" + which] + jb * 256, 256)])
                    for j in range(2):
                        r0 = jb * 256 + j * 128
                        for (c0, n) in (chunks_q if which == "q" else chunks):
                            ps, pk = mm_fm(wb, wk, j, c0, n)
                            flush()
                            if kind == "plain":
                                o, ok = rO.next()
                                p.dve(lambda e, o=o, ps=ps, n=n: e.tensor_copy(out=o[:, 0:n], in_=ps[:, 0:n]), reads=[pk], writes=[ok])
                                store_fm(o, ok, dst, r0, c0, n)
                            elif kind == "rope":
                                qn, qnk = rE.next()
                                p.act(lambda e, qn=qn, ps=ps, n=n, scl=scl: e.activation(out=qn[:, 0:n], in_=ps[:, 0:n], func=AF.Copy, scale=scl), reads=[pk], writes=[qnk])
                                rope_tail(qn, qnk, c0, n, dst, r0)
                            else:
                                sq, sqk = rE.next()
                                p.act(lambda e, sq=sq, ps=ps, n=n: e.activation(out=sq[:, 0:n], in_=ps[:, 0:n], func=AF.Square), reads=[pk], writes=[sqk])

                                def stage_b(sq=sq, sqk=sqk, ps=ps, pk=pk, c0=c0, n=n, gcol=gcol, dst=dst, r0=r0):
                                    ps3, p3k = rB.next()
                                    p.pe(lambda e: e.matmul(ps3[:, 0:n], lhsT=onesf[:], rhs=sq[:, 0:n], start=True, stop=True), reads=["onesf", sqk], writes=[p3k])
                                    rs, rsk = rE.next()
                                    p.act(lambda e: e.activation(out=rs[:, 0:n], in_=ps3[:, 0:n], func=AF.Sqrt, bias=NEPS, scale=1.0 / HD), reads=[p3k], writes=[rsk])
                                    p.dve(lambda e: e.reciprocal(out=rs[:, 0:n], in_=rs[:, 0:n]), reads=[rsk], writes=[rsk])
                                    qn, qnk = rE.next()
                                    p.dve(lambda e: e.scalar_tensor_tensor(out=qn[:, 0:n], in0=ps[:, 0:n], scalar=small[:, gcol:gcol + 1], in1=rs[:, 0:n], op0=ALU.mult, op1=ALU.mult),
                                          reads=[pk, rsk, "small"], writes=[qnk])
                                    rope_tail(qn, qnk, c0, n, dst, r0)
                                pending.append(stage_b)
            flush(); flush(); flush()
            for m, width in (("B", 256), ("C", 1024), ("D", 1024)):
                for jb in range(width // 256):
                    wb, wk = load_w([(OFF[m + "_v"] + jb * 256, 256)])
                    for tt in range(T // 128):
                        ps, pk = rA.next()
                        for k in range(32):
                            p.pe(lambda e, ps=ps, k=k, tt=tt, wb=wb: e.matmul(ps[:, 0:256], lhsT=hlT[:, k, tt * 128:(tt + 1) * 128], rhs=wb[:, k, :], start=(k == 0), stop=(k == 31)),
                                 reads=["hlT", wk], writes=[pk])
                        o, ok = rO.next()
                        p.act(lambda e, o=o, ps=ps: e.activation(out=o[:, 0:256], in_=ps[:, 0:256], func=AF.Copy), reads=[pk], writes=[ok])
                        p.dma("sp", "st_%s%d" % ok, vS[m][tok0 + tt * 128:tok0 + (tt + 1) * 128, jb * 256:(jb + 1) * 256], o[:, 0:256], reads=[ok], writes=["scr"])
            p.barrier()
            p.release()
            p.release()
        if stop_after in ("norm", "proj"):
            break

        p.mark()
        cw31 = p.sb([31, GW], F32, "cw31")
        cwT = p.sb([128, 8, 31], F32, "cwT")
        p.dma("sp", "cw", cw31[:], conv_w[l], writes=["cw31"])
        for cb in range(8):
            ps, pk = rA.next()
            p.pe(lambda e, ps=ps, cb=cb: e.transpose(out=ps[:, 0:31], in_=cw31[:, cb * 128:(cb + 1) * 128], identity=identf[0:31, 0:31]), reads=["cw31", "identf"], writes=[pk])
            p.dve(lambda e, ps=ps, cb=cb: e.tensor_copy(out=cwT[:, cb, :], in_=ps[:, 0:31]), reads=[pk], writes=["cwT"])
        dg = p.sb([128, 8 * 31, 128], BF16, "dg")
        for cb in range(8):
            p.dve(lambda e, cb=cb: e.tensor_tensor(out=dg[:, cb * 31:(cb + 1) * 31, :], in0=identf[:].unsqueeze(1).to_broadcast([128, 31, 128]),
                                                   in1=cwT[:, cb, :].unsqueeze(2).to_broadcast([128, 31, 128]), op=ALU.mult),
                  reads=["identf", "cwT"], writes=["dg"])
        ub = [p.sb([128, 512 + 30], BF16, f"ub{i}") for i in range(3)]
        rU = Rot(ub, "ub")
        vb = [p.sb([128, 512], F32, f"vb{i}") for i in range(8)]
        sqb = [p.sb([128, 512], F32, f"sqb{i}") for i in range(2)]
        rSq = Rot(sqb, "sqb")
        mean = p.sb([128, 512], F32, "mean")
        rstd = p.sb([128, 512], F32, "rstd")
        msq = p.sb([128, 512], F32, "msq")
        sgl = [p.sb([128, 512], BF16, f"sgl{i}") for i in range(2)]
        rSg = Rot(sgl, "sgl")
        tb = [p.sb([128, 512], F32, f"tb{i}") for i in range(2)]
        rTb = Rot(tb, "tb")
        yo = [p.sb([128, 512], BF16, f"yo{i}") for i in range(3)]
        rY = Rot(yo, "yo")
        segs = [(CTX, SEQ)] if last else [(0, CTX), (CTX, SEQ)]
        for (s0, slen) in segs:
            for c0 in range(0, slen, 512):
                n = min(512, slen - c0)
                ps_s, pssk = rB.next()
                ps_q, psqk = rB.next()
                for cb in range(8):
                    u, uk = rU.next()
                    lo = max(c0 - 15, 0)
                    hi = min(c0 + n + 15, slen)
                    p.pool(lambda e, u=u: e.memset(u[:], 0.0), writes=[uk])
                    p.dma("sp", "u%d" % uk[1], u[:, lo - (c0 - 15):hi - (c0 - 15)], uT[cb * 128:(cb + 1) * 128, s0 + lo:s0 + hi], reads=["scr"], writes=[uk])
                    v, vk = vb[cb], ("vb", cb)
                    pc, pck = rA.next()
                    for j in range(31):
                        p.pe(lambda e, pc=pc, u=u, cb=cb, n=n, j=j: e.matmul(pc[:, 0:n], lhsT=dg[:, cb * 31 + j, :], rhs=u[:, j:j + n], start=(j == 0), stop=(j == 30)),
                             reads=["dg", uk], writes=[pck])
                    p.dve(lambda e, v=v, pc=pc, cb=cb, n=n: e.tensor_scalar(out=v[:, 0:n], in0=pc[:, 0:n], scalar1=small[:, cb:cb + 1], scalar2=None, op0=ALU.add),
                          reads=[pck, "small"], writes=[vk])
                    sq, sqk = rSq.next()
                    p.act(lambda e, sq=sq, v=v, n=n: e.activation(out=sq[:, 0:n], in_=v[:, 0:n], func=AF.Square), reads=[vk], writes=[sqk])
                    p.pe(lambda e, v=v, cb=cb, n=n, ps_s=ps_s: e.matmul(ps_s[:, 0:n], lhsT=onesf[:], rhs=v[:, 0:n], start=(cb == 0), stop=(cb == 7)), reads=["onesf", vk], writes=[pssk])
                    p.pe(lambda e, sq=sq, cb=cb, n=n, ps_q=ps_q: e.matmul(ps_q[:, 0:n], lhsT=onesf[:], rhs=sq[:, 0:n], start=(cb == 0), stop=(cb == 7)), reads=["onesf", sqk], writes=[psqk])
                p.dve(lambda e, n=n, ps_s=ps_s: e.tensor_scalar(out=mean[:, 0:n], in0=ps_s[:, 0:n], scalar1=1.0 / GW, scalar2=None, op0=ALU.mult), reads=[pssk], writes=["mean"])
                p.dve(lambda e, n=n: e.tensor_tensor(out=msq[:, 0:n], in0=mean[:, 0:n], in1=mean[:, 0:n], op=ALU.mult), reads=["mean"], writes=["msq"])
                p.dve(lambda e, n=n, ps_q=ps_q: e.scalar_tensor_tensor(out=rstd[:, 0:n], in0=ps_q[:, 0:n], scalar=1.0 / GW, in1=msq[:, 0:n], op0=ALU.mult, op1=ALU.subtract),
                      reads=[psqk, "msq"], writes=["rstd"])
                p.act(lambda e, n=n: e.activation(out=rstd[:, 0:n], in_=rstd[:, 0:n], func=AF.Sqrt, bias=LEPS, scale=1.0), reads=["rstd"], writes=["rstd"])
                p.dve(lambda e, n=n: e.reciprocal(out=rstd[:, 0:n], in_=rstd[:, 0:n]), reads=["rstd"], writes=["rstd"])
                for cb in range(8):
                    v, vk = vb[cb], ("vb", cb)
                    sg, sgk = rSg.next()
                    p.dma("sp", "sg%d" % sgk[1], sg[:, 0:n], sgT[cb * 128:(cb + 1) * 128, s0 + c0:s0 + c0 + n], reads=["scr"], writes=[sgk])
                    p.dve(lambda e, v=v, n=n: e.tensor_tensor(out=v[:, 0:n], in0=v[:, 0:n], in1=mean[:, 0:n], op=ALU.subtract), reads=[vk, "mean"], writes=[vk])
                    p.pool(lambda e, v=v, n=n: e.tensor_tensor(out=v[:, 0:n], in0=v[:, 0:n], in1=rstd[:, 0:n], op=ALU.mult), reads=[vk, "rstd"], writes=[vk])
                    t_, tk = rTb.next()
                    p.act(lambda e, t_=t_, v=v, cb=cb, n=n: e.activation(out=t_[:, 0:n], in_=v[:, 0:n], func=AF.Silu, scale=small[:, 8 + cb:9 + cb], bias=small[:, 16 + cb:17 + cb]),
                          reads=[vk, "small"], writes=[tk])
                    y, yk = rY.next()
                    p.dve(lambda e, y=y, t_=t_, sg=sg, n=n: e.tensor_tensor(out=y[:, 0:n], in0=t_[:, 0:n], in1=sg[:, 0:n], op=ALU.mult), reads=[tk, sgk], writes=[yk])
                    p.dma("sp", "yst%d" % yk[1], yT[cb * 128:(cb + 1) * 128, s0 + c0:s0 + c0 + n], y[:, 0:n], reads=[yk], writes=["yT"])
        p.barrier()
        p.release()
        if stop_after == "conv":
            break

        def attention(m):
            p.mark()
            nkv = 2 if m == "B" else 8
            KT = [p.sb([128, TALL], BF16, f"KT{i}") for i in range(2)]
            rK = Rot(KT, "KT")
            VV = [p.sb([128, 18, 128], BF16, f"VV{i}") for i in range(2)]
            rV = Rot(VV, "VV")
            QT = [p.sb([128, TALL], BF16, f"QT{i}") for i in range(2)]
            rQ = Rot(QT, "QT")
            SG = [p.sb([128, TALL], BF16, f"SG{i}") for i in range(2)]
            rS = Rot(SG, "SG")
            pT = [p.sb([128, 512], BF16, f"pT{i}") for i in range(4)]
            rP = Rot(pT, "pT")
            pM = [p.sb([128, 512], BF16, f"pM{i}") for i in range(3)]
            rM = Rot(pM, "pM")
            rden = p.sb([128, 512], F32, "rden")
            of = p.sb([128, 512], F32, "of")
            yo = [p.sb([128, 512], BF16, f"yo{i}") for i in range(3)]
            rY = Rot(yo, "yo")
            mrow = {"B": GW, "D": 3 * GW}[m]
            if m == "D":
                rmask = p.sb([128, len(NA_PAIRS), 512], BF16, "rmask")
                p.dma("sp", "const", rmask[:], hc["na_rmask"], writes=["rmask"])
                cv = p.sb([128, 64], F32, "cv")
                p.dma("sp", "const", cv[:], hc["na_cv"], writes=["cv"])
                zd = p.sb([31, 64, 128], BF16, "zd")
                p.dma("sp", "const", zd[:], hc["na_z"], writes=["zd"])
                nb = p.sb([31, 8, 15], BF16, "nb")
                p.dma("pool", "nbld", nb[:], na_bT[l], writes=["nb"])
                G2 = [p.sb([128, 29 * 64], BF16, f"G2{i}") for i in range(2)]
                for g2 in G2:
                    p.pool(lambda e, g2=g2: e.memset(g2[:], 0.0), writes=[("G2", G2.index(g2))])
                rG = Rot(G2, "G2")
                gex = p.sb([128, 15, 64], F32, "gex")
            kvl = {}
            qsl = {}
            allheads = [(kv, h) for kv in range(nkv) for h in ([kv * 4 + g for g in range(4)] if m == "B" else [kv])]

            def load_kv(kv):
                K_, Kk = rK.next()
                p.dma("sp", "K%d" % Kk[1], K_[:], kT[m][kv * 128:(kv + 1) * 128, :], reads=["scr"], writes=[Kk])
                V_, Vk = rV.next()
                p.dma("sp", "V%d" % Vk[1], V_[:], vS[m][:, kv * 128:(kv + 1) * 128].rearrange("(t p) d -> p t d", p=128), reads=["scr"], writes=[Vk])
                kvl[kv] = (K_, Kk, V_, Vk)

            def load_qs(i):
                h = allheads[i][1]
                Q_, Qk = rQ.next()
                p.dma("sp", "Q%d" % Qk[1], Q_[:], qT[m][h * 128:(h + 1) * 128, :], reads=["scr"], writes=[Qk])
                S_, Sk = rS.next()
                p.dma("sp", "S%d" % Sk[1], S_[:], sgT[mrow + h * 128:mrow + (h + 1) * 128, :], reads=["scr"], writes=[Sk])
                qsl[i] = (Q_, Qk, S_, Sk)
            load_kv(0)
            load_qs(0)
            hidx = 0
            for kv in range(nkv):
                K_, Kk, V_, Vk = kvl[kv]
                if kv + 1 < nkv:
                    load_kv(kv + 1)
                heads = [kv * 4 + g for g in range(4)] if m == "B" else [kv]
                if m == "D":
                    h = kv
                    g2, g2k = rG.next()
                    for half in range(2):
                        e0, e1 = (0, 8) if half == 0 else (8, 15)
                        ps, pk = rB.next()
                        psv = ps[:, 0:(e1 - e0) * 64].rearrange("p (e q) -> p e q", q=64)
                        for qc in range(64):
                            p.pe(lambda e, psv=psv, qc=qc, e0=e0, e1=e1, h=h: e.matmul(psv[:, :, qc], lhsT=zd[:, qc, :], rhs=nb[:, h, e0:e1], start=True, stop=True),
                                 reads=["zd", "nb"], writes=[pk])
                        p.act(lambda e, psv=psv, e0=e0, e1=e1: e.activation(out=gex[:, e0:e1, :], in_=psv, func=AF.Exp), reads=[pk], writes=["gex"])
                    g2v = g2[:].rearrange("p (e q) -> p e q", q=64)
                    p.dve(lambda e, g2v=g2v: e.tensor_tensor(out=g2v[0:64, 7:22, :], in0=gex[0:64, :, :], in1=cv[0:64, :].unsqueeze(1).to_broadcast([64, 15, 64]), op=ALU.mult),
                          reads=["gex", "cv"], writes=[g2k])
                    p.dve(lambda e, g2v=g2v: e.tensor_tensor(out=g2v[64:128, 8:23, :], in0=gex[64:128, :, :], in1=cv[64:128, :].unsqueeze(1).to_broadcast([64, 15, 64]), op=ALU.mult),
                          reads=["gex", "cv"], writes=[g2k])
                for h in heads:
                    Q_, Qk, S_, Sk = qsl[hidx]
                    if hidx + 1 < len(allheads):
                        load_qs(hidx + 1)
                    hidx += 1
                    qchunks = []
                    if not last:
                        qchunks.append((0, 256, [(0, None), (1, None)]))
                    for qb in range(4):
                        if m == "B":
                            kts = [(t, None) for t in range(18)]
                        else:
                            kts = [(0, None), (1, None)] + [(2 + j, NA_PAIRS.index((qb, j))) for j in range(16) if (qb, j) in NA_PAIRS]
                        qchunks.append((CTX + qb * 512, 512, kts))
                    for (q0, n, kts) in qchunks:
                        ps_o, pok = rB.next()
                        ps_d, pdk = rB.next()

                        def score(i):
                            kt, _ = kts[i]
                            ps, pk = rA.next()
                            p.pe(lambda e, ps=ps, kt=kt: e.matmul(ps[:, 0:n], lhsT=K_[:, kt * 128:(kt + 1) * 128], rhs=Q_[:, q0:q0 + n], start=True, stop=True),
                                 reads=[Kk, Qk], writes=[pk])
                            return ps, pk
                        nxtq = [score(0)]
                        if len(kts) > 1:
                            nxtq.append(score(1))
                        for i, (kt, pi) in enumerate(kts):
                            ps, pk = nxtq.pop(0)
                            if i + 2 < len(kts):
                                nxtq.append(score(i + 2))
                            pt, ptk = rP.next()
                            p.act(lambda e, pt=pt, ps=ps: e.activation(out=pt[:, 0:n], in_=ps[:, 0:n], func=AF.Exp, scale=SCALE), reads=[pk], writes=[ptk])
                            if pi is not None:
                                qb_, j_ = NA_PAIRS[pi]
                                e0 = 8 * qb_ - 2 * j_ + 7
                                pm, pmk = rM.next()
                                p.dve(lambda e, pm=pm, pt=pt, e0=e0: e.tensor_tensor(out=pm[:, 0:n], in0=pt[:, 0:n], in1=g2[:, (e0 + 7) * 64:(e0 + 7) * 64 + 512], op=ALU.mult),
                                      reads=[ptk, g2k], writes=[pmk])
                                p.pool(lambda e, pm=pm, pi=pi: e.tensor_tensor(out=pm[:, 0:n], in0=pm[:, 0:n], in1=rmask[:, pi, :], op=ALU.mult), reads=[pmk, "rmask"], writes=[pmk])
                                pt, ptk = pm, pmk
                            first = (i == 0)
                            lastk = (i == len(kts) - 1)
                            p.pe(lambda e, pt=pt, kt=kt, first=first, lastk=lastk: e.matmul(ps_o[:, 0:n], lhsT=V_[:, kt, :], rhs=pt[:, 0:n], start=first, stop=lastk),
                                 reads=[Vk, ptk], writes=[pok])
                            p.pe(lambda e, pt=pt, first=first, lastk=lastk: e.matmul(ps_d[:, 0:n], lhsT=onesb[:], rhs=pt[:, 0:n], start=first, stop=lastk),
                                 reads=["onesb", ptk], writes=[pdk])
                        p.dve(lambda e: e.reciprocal(out=rden[:, 0:n], in_=ps_d[:, 0:n]), reads=[pdk], writes=["rden"])
                        p.dve(lambda e: e.tensor_tensor(out=of[:, 0:n], in0=ps_o[:, 0:n], in1=rden[:, 0:n], op=ALU.mult), reads=[pok, "rden"], writes=["of"])
                        y, yk = rY.next()
                        p.pool(lambda e, y=y: e.tensor_tensor(out=y[:, 0:n], in0=of[:, 0:n], in1=S_[:, q0:q0 + n], op=ALU.mult), reads=["of", Sk], writes=[yk])
                        p.dma("sp", "yst%d" % yk[1], yT[mrow + h * 128:mrow + (h + 1) * 128, q0:q0 + n], y[:, 0:n], reads=[yk], writes=["yT"])
            p.barrier()
            p.release()

        attention("B")
        if stop_after == "gqa":
            break
        attention("D")
        if stop_after == "na":
            break

        p.mark()
        rtab = p.sb([128, 6, 128], F32, "rtab")
        p.dma("sp", "const", rtab[:], hc["ret_tabs"], writes=["rtab"])
        pidx = p.sb([128, 3], F32, "pidx")
        p.dma("sp", "const", pidx[:], hc["ret_pidx"], writes=["pidx"])
        LG = p.sb([128, 16], F32, "LG")
        p.dma("sp", "const", LG[:], ret_dec[l].partition_broadcast(128), writes=["LG"])
        p.act(lambda e: e.activation(out=LG[:], in_=LG[:], func=AF.Exp, scale=-1.0), reads=["LG"], writes=["LG"])
        p.act(lambda e: e.activation(out=LG[:], in_=LG[:], func=AF.Ln, bias=1.0, scale=1.0), reads=["LG"], writes=["LG"])
        p.dve(lambda e: e.tensor_scalar(out=LG[:], in0=LG[:], scalar1=-1.0, scalar2=None, op0=ALU.mult), reads=["LG"], writes=["LG"])
        Dm = p.sb([128, 16, 128], F32, "Dm")
        qdec = p.sb([128, 16, 128], F32, "qdec")
        kdec = p.sb([128, 16], F32, "kdec")
        cdec = p.sb([128, 16], F32, "cdec")
        for hd in range(16):
            d = hd // 8
            p.act(lambda e, hd=hd, d=d: e.activation(out=Dm[:, hd, :], in_=rtab[:, d, :], func=AF.Exp, scale=LG[:, hd:hd + 1]), reads=["rtab", "LG"], writes=["Dm"])
            p.dve(lambda e, hd=hd, d=d: e.tensor_tensor(out=Dm[:, hd, :], in0=Dm[:, hd, :], in1=rtab[:, 2 + d, :], op=ALU.mult), reads=["Dm", "rtab"], writes=["Dm"])
            p.act(lambda e, hd=hd, d=d: e.activation(out=qdec[:, hd, :], in_=rtab[:, 4 + d, :], func=AF.Exp, scale=LG[:, hd:hd + 1]), reads=["rtab", "LG"], writes=["qdec"])
            p.act(lambda e, hd=hd, d=d: e.activation(out=kdec[:, hd:hd + 1], in_=pidx[:, d:d + 1], func=AF.Exp, scale=LG[:, hd:hd + 1]), reads=["pidx", "LG"], writes=["kdec"])
            p.act(lambda e, hd=hd: e.activation(out=cdec[:, hd:hd + 1], in_=pidx[:, 2:3], func=AF.Exp, scale=LG[:, hd:hd + 1]), reads=["pidx", "LG"], writes=["cdec"])
        KT = [p.sb([128, TALL], BF16, f"rKT{i}") for i in range(2)]
        rK = Rot(KT, "rKT")
        QT = [p.sb([128, TALL], BF16, f"rQT{i}") for i in range(2)]
        rQ = Rot(QT, "rQT")
        VV = [p.sb([128, 18, 128], BF16, f"rVV{i}") for i in range(2)]
        rV = Rot(VV, "rVV")
        SG = [p.sb([128, TALL], BF16, f"rSG{i}") for i in range(2)]
        rS = Rot(SG, "rSG")
        Kd = [p.sb([128, 18, 128], BF16, f"Kd{i}") for i in range(2)]
        oacc = p.sb([128, 18, 128], F32, "oacc")
        osq = p.sb([128, 18, 128], F32, "osq")
        onb = p.sb([128, 18, 128], BF16, "onb")
        gst = p.sb([128, 4, 18], F32, "gst")
        attm = [p.sb([128, 128], BF16, f"attm{i}") for i in range(3)]
        rAt = Rot(attm, "attm")
        qdb = [p.sb([128, 128], BF16, f"qdb{i}") for i in range(3)]
        rQd = Rot(qdb, "qdb")
        qd_all = [p.sb([128, 18, 128], BF16, f"qdall{i}") for i in range(2)]
        attm_all = [p.sb([128, 18, 128], BF16, f"attmall{i}") for i in range(2)]
        S32 = [p.sb([128, 128], F32, f"S32{i}") for i in range(2)]
        Sbf = [p.sb([128, 128], BF16, f"Sbf{i}") for i in range(2)]
        yf = p.sb([128, 512], F32, "yf")
        yo = [p.sb([128, 512], BF16, f"ryo{i}") for i in range(3)]
        rY = Rot(yo, "ryo")
        rl = {}

        def load_ret(h):
            K_, Kk = rK.next()
            p.dma("sp", "K%d" % Kk[1], K_[:], kT["C"][h * 128:(h + 1) * 128, :], reads=["scr"], writes=[Kk])
            Q_, Qk = rQ.next()
            p.dma("sp", "Q%d" % Qk[1], Q_[:], qT["C"][h * 128:(h + 1) * 128, :], reads=["scr"], writes=[Qk])
            V_, Vk = rV.next()
            p.dma("sp", "V%d" % Vk[1], V_[:], vS["C"][:, h * 128:(h + 1) * 128].rearrange("(t p) d -> p t d", p=128), reads=["scr"], writes=[Vk])
            S_, Sk = rS.next()
            p.dma("sp", "S%d" % Sk[1], S_[:], sgT[2 * GW + h * 128:2 * GW + (h + 1) * 128, :], reads=["scr"], writes=[Sk])
            rl[h] = (K_, Kk, Q_, Qk, V_, Vk, S_, Sk)
        load_ret(0)
        oaccB = p.sb([128, 18, 128], F32, "oaccB")
        for h in range(8):
            K_, Kk, Q_, Qk, V_, Vk, S_, Sk = rl[h]
            if h + 1 < 8:
                load_ret(h + 1)
            for c in range(18):
                pt, ptk = rT.next()
                p.pe(lambda e, pt=pt, c=c: e.transpose(out=pt[:, 0, :], in_=K_[:, c * 128:(c + 1) * 128], identity=ident[:]), reads=[Kk, "ident"], writes=[ptk])
                for d in range(2):
                    hd = d * 8 + h
                    p.dve(lambda e, pt=pt, c=c, d=d, hd=hd: e.tensor_scalar(out=Kd[d][:, c, :], in0=pt[:, 0, :], scalar1=kdec[:, hd:hd + 1], scalar2=None, op0=ALU.mult),
                          reads=[ptk, "kdec"], writes=[("Kd", d)])
            orders = [list(range(18)), [1, 0] + list(range(17, 1, -1))]
            for c in range(18):
                if last and c < 2:
                    continue
                for d in range(2):
                    hd = d * 8 + h
                    p.pool(lambda e, c=c, d=d, hd=hd: e.tensor_tensor(out=qd_all[d][:, c, :], in0=Q_[:, c * 128:(c + 1) * 128], in1=qdec[:, hd, :], op=ALU.mult),
                           reads=[Qk, "qdec"], writes=[("qd", d, c)])
                ps, pk = rA.next()
                p.pe(lambda e, ps=ps, c=c: e.matmul(ps[:, 0:128], lhsT=K_[:, c * 128:(c + 1) * 128], rhs=Q_[:, c * 128:(c + 1) * 128], start=True, stop=True),
                     reads=[Kk, Qk], writes=[pk])
                for d in range(2):
                    hd = d * 8 + h
                    p.dve(lambda e, ps=ps, c=c, d=d, hd=hd: e.tensor_tensor(out=attm_all[d][:, c, :], in0=ps[:, 0:128], in1=Dm[:, hd, :], op=ALU.mult),
                          reads=[pk, "Dm"], writes=[("attm", d, c)])
            for ci in range(18):
              for d in range(2):
                hd = d * 8 + h
                c = orders[d][ci]
                s32, sbf = S32[d], Sbf[d]
                s32k, sbfk = ("S32", d), ("Sbf", d)
                if True:
                    need_o = not (last and c < 2)
                    if need_o:
                        po, pok = rB.next()
                        p.pe(lambda e, po=po, c=c, ci=ci, d=d: e.matmul(po[:, 0:128], lhsT=attm_all[d][:, c, :], rhs=V_[:, c, :], start=True, stop=(ci == 0)), reads=[("attm", d, c), Vk], writes=[pok])
                        if ci > 0:
                            p.pe(lambda e, po=po, c=c, d=d, sbf=sbf: e.matmul(po[:, 0:128], lhsT=qd_all[d][:, c, :], rhs=sbf[:], start=False, stop=True), reads=[("qd", d, c), sbfk], writes=[pok])
                        if d == 0:
                            p.act(lambda e, po=po, c=c: e.activation(out=oacc[:, c, :], in_=po[:, 0:128], func=AF.Copy), reads=[pok], writes=["oacc"])
                        else:
                            p.act(lambda e, po=po, c=c: e.activation(out=oaccB[:, c, :], in_=po[:, 0:128], func=AF.Copy), reads=[pok], writes=["oaccB"])
                    if ci < 17:
                        pkv, pkvk = rA.next()
                        p.pe(lambda e, pkv=pkv, c=c, d=d: e.matmul(pkv[:, 0:128], lhsT=Kd[d][:, c, :], rhs=V_[:, c, :], start=True, stop=True), reads=[("Kd", d), Vk], writes=[pkvk])
                        if ci == 0:
                            p.dve(lambda e, pkv=pkv, s32=s32: e.tensor_copy(out=s32[:], in_=pkv[:, 0:128]), reads=[pkvk], writes=[s32k])
                        else:
                            p.dve(lambda e, pkv=pkv, s32=s32, hd=hd: e.scalar_tensor_tensor(out=s32[:], in0=s32[:], scalar=cdec[:, hd:hd + 1], in1=pkv[:, 0:128], op0=ALU.mult, op1=ALU.add),
                                  reads=[pkvk, s32k, "cdec"], writes=[s32k])
                        p.act(lambda e, s32=s32, sbf=sbf: e.activation(out=sbf[:], in_=s32[:], func=AF.Copy), reads=[s32k], writes=[sbfk])
            c_lo = 2 if last else 0
            ncn = 18 - c_lo
            ov = oacc[:, c_lo:18, :]
            p.dve(lambda e, ov=ov: e.tensor_tensor(out=ov, in0=ov, in1=oaccB[:, c_lo:18, :], op=ALU.add), reads=["oacc", "oaccB"], writes=["oacc"])
            p.dve(lambda e, ov=ov: e.tensor_reduce(out=gst[:, 0, c_lo:18], in_=ov, axis=AX.X, op=ALU.add), reads=["oacc"], writes=["gst"])
            p.act(lambda e, ov=ov: e.activation(out=osq[:, c_lo:18, :], in_=ov, func=AF.Square), reads=["oacc"], writes=["osq"])
            p.dve(lambda e: e.tensor_reduce(out=gst[:, 1, c_lo:18], in_=osq[:, c_lo:18, :], axis=AX.X, op=ALU.add), reads=["osq"], writes=["gst"])
            p.dve(lambda e: e.tensor_scalar(out=gst[:, 0, :], in0=gst[:, 0, :], scalar1=1.0 / HD, scalar2=None, op0=ALU.mult), reads=["gst"], writes=["gst"])
            p.dve(lambda e: e.tensor_tensor(out=gst[:, 2, :], in0=gst[:, 0, :], in1=gst[:, 0, :], op=ALU.mult), reads=["gst"], writes=["gst"])
            p.dve(lambda e: e.scalar_tensor_tensor(out=gst[:, 1, :], in0=gst[:, 1, :], scalar=1.0 / HD, in1=gst[:, 2, :], op0=ALU.mult, op1=ALU.subtract), reads=["gst"], writes=["gst"])
            p.act(lambda e: e.activation(out=gst[:, 1, c_lo:18], in_=gst[:, 1, c_lo:18], func=AF.Sqrt, bias=LEPS, scale=1.0), reads=["gst"], writes=["gst"])
            p.dve(lambda e: e.reciprocal(out=gst[:, 1, c_lo:18], in_=gst[:, 1, c_lo:18]), reads=["gst"], writes=["gst"])
            p.dve(lambda e, ov=ov: e.tensor_tensor(out=ov, in0=ov, in1=gst[:, 0, c_lo:18].unsqueeze(2).to_broadcast([128, ncn, 128]), op=ALU.subtract), reads=["oacc", "gst"], writes=["oacc"])
            p.dve(lambda e, ov=ov: e.tensor_tensor(out=onb[:, c_lo:18, :], in0=ov, in1=gst[:, 1, c_lo:18].unsqueeze(2).to_broadcast([128, ncn, 128]), op=ALU.mult), reads=["oacc", "gst"], writes=["onb"])
            for c4 in range(c_lo, 18, 4):
                nn = min(4, 18 - c4)
                pt, ptk = rT.next()
                for cc in range(nn):
                    p.pe(lambda e, pt=pt, cc=cc, c4=c4: e.transpose(out=pt[:, cc, :], in_=onb[:, c4 + cc, :], identity=ident[:]), reads=["onb", "ident"], writes=[ptk])
                w = nn * 128
                p.dve(lambda e, pt=pt, w=w, nn=nn: e.tensor_scalar(out=yf[:, 0:w], in0=pt[:, 0:nn, :].rearrange("p a b -> p (a b)"), scalar1=small[:, 24 + h:25 + h], scalar2=small[:, 32 + h:33 + h], op0=ALU.mult, op1=ALU.add),
                      reads=[ptk, "small"], writes=["yf"])
                y, yk = rY.next()
                p.pool(lambda e, y=y, w=w, c4=c4: e.tensor_tensor(out=y[:, 0:w], in0=yf[:, 0:w], in1=S_[:, c4 * 128:c4 * 128 + w], op=ALU.mult), reads=["yf", Sk], writes=[yk])
                p.dma("sp", "yst%d" % yk[1], yT[2 * GW + h * 128:2 * GW + (h + 1) * 128, c4 * 128:c4 * 128 + w], y[:, 0:w], reads=[yk], writes=["yT"])
        p.barrier()
        p.release()
        if stop_after == "ret":
            break

        for (tok0, T) in groups:
            p.mark()
            yTg = p.sb([128, 32, T], BF16, "yTg")
            p.dma("sp", "ytg", yTg[:], yT[:, tok0:tok0 + T].rearrange("(k p) t -> p k t", p=128), reads=["yT"], writes=["yTg"])
            gtb = p.sb([128, D], F32, "gtb")
            wbuf = [p.sb([128, 32, 512], BF16, f"wo{i}") for i in range(2)]
            rW = Rot(wbuf, "wo")
            xs = [p.sb([128, 512], F32, f"xs{i}") for i in range(3)]
            rXs = Rot(xs, "xs")
            ts = [p.sb([128, 512], F32, f"ts{i}") for i in range(2)]
            rTs = Rot(ts, "ts")
            tiles = [tt for tt in range(T // 128) if not (last and tok0 + tt * 128 < CTX)]
            cur_row = None
            for nb in range(8):
                wb, wk = rW.next()
                p.dma("pool", "w%d" % wk[1], wb[:], w_out[l, :, nb * 512:(nb + 1) * 512].rearrange("(k p) n -> p k n", p=128), writes=[wk])
                for tt in tiles:
                    g0 = tok0 + tt * 128
                    is_ctx = g0 < CTX
                    row = 1 if is_ctx else 0
                    if row != cur_row:
                        p.dma("sp", "modld", gtb[:], mod[l, row, 2 * D:3 * D].partition_broadcast(128), reads=["mod"], writes=["gtb"])
                        cur_row = row
                    ps, pk = rA.next()
                    for k in range(32):
                        p.pe(lambda e, ps=ps, k=k, tt=tt, wb=wb: e.matmul(ps[:], lhsT=yTg[:, k, tt * 128:(tt + 1) * 128], rhs=wb[:, k, :], start=(k == 0), stop=(k == 31)),
                             reads=["yTg", wk], writes=[pk])
                    if l == 0:
                        src = ctx_in[g0:g0 + 128, nb * 512:(nb + 1) * 512] if is_ctx else x_in[g0 - CTX:g0 - CTX + 128, nb * 512:(nb + 1) * 512]
                    else:
                        src = xres[g0:g0 + 128, nb * 512:(nb + 1) * 512]
                    x_, xk = rXs.next()
                    p.dma("act", "xs%d" % xk[1], x_[:], src, reads=["xres_r"], writes=[xk])
                    t_, tk = rTs.next()
                    p.dve(lambda e, t_=t_, ps=ps, nb=nb: e.tensor_tensor(out=t_[:], in0=ps[:], in1=gtb[:, nb * 512:(nb + 1) * 512], op=ALU.mult), reads=[pk, "gtb"], writes=[tk])
                    p.dve(lambda e, t_=t_, x_=x_: e.tensor_tensor(out=x_[:], in0=x_[:], in1=t_[:], op=ALU.add), reads=[tk, xk], writes=[xk])
                    p.dma("sp", "xst%d" % xk[1], xres[g0:g0 + 128, nb * 512:(nb + 1) * 512], x_[:], reads=[xk], writes=["xres_w"])
            p.barrier()
            p.release()
        if stop_after == "l0":
            break

    if stop_after is None:
        p.mark()
        fgb = p.sb([128, D], F32, "fgb")
        p.dma("sp", "const", fgb[:], final_g.partition_broadcast(128), writes=["fgb"])
        xt = [p.sb([128, D], F32, f"fxt{i}") for i in range(3)]
        rX = Rot(xt, "fxt")
        junk = p.sb([128, D], BF16, "fjunk")
        st = p.sb([128, 8], F32, "fst")
        for tt in range(SEQ // 128):
            xb_, xk = rX.next()
            p.dma("pool", "x%d" % xk[1], xb_[:], xres[CTX + tt * 128:CTX + (tt + 1) * 128, :], reads=["xres_w"], writes=[xk])
            p.act(lambda e, xb_=xb_: e.activation(out=junk[:], in_=xb_[:], func=AF.Square, accum_out=st[:, 0:1]), reads=[xk], writes=["junk", "st0"])
            p.act(lambda e: e.activation(out=st[:, 1:2], in_=st[:, 0:1], func=AF.Sqrt, bias=NEPS, scale=1.0 / D), reads=["st0"], writes=["st1"])
            p.dve(lambda e: e.reciprocal(out=st[:, 2:3], in_=st[:, 1:2]), reads=["st1"], writes=["st2"])
            p.dve(lambda e, xb_=xb_: e.scalar_tensor_tensor(out=xb_[:], in0=xb_[:], scalar=st[:, 2:3], in1=fgb[:], op0=ALU.mult, op1=ALU.mult),
                  reads=[xk, "st2", "fgb"], writes=[xk])
            p.dma("sp", "ost%d" % xk[1], out[tt * 128:(tt + 1) * 128, :], xb_[:], reads=[xk], writes=["out"])
        p.release()
    p.barrier()
    stats = p.emit()
    nc_ctx.__exit__(None, None, None)
    return nc, cs, stats, dict(mod=mod, xres=xres, uT=uT, sgT=sgT, qT=qT, kT=kT, vS=vS, yT=yT)


def make_in_maps(inputs, cs):
    bf = ml_dtypes.bfloat16
    f = lambda a: np.ascontiguousarray(np.asarray(a, dtype=np.float32))
    shared = {
        "ada_w": f(inputs["ada_w"]), "ada_b": f(inputs["ada_b"]), "norm_g": f(inputs["norm_g"]),
        "w_in": f(inputs["w_in"]), "conv_w": f(inputs["conv_w"]), "conv_b": f(inputs["conv_b"]),
        "conv_ln_g": f(inputs["conv_ln_g"]), "conv_ln_b": f(inputs["conv_ln_b"]),
        "gqa_qn_g": f(inputs["gqa_qn_g"]), "gqa_kn_g": f(inputs["gqa_kn_g"]),
        "ret_dec": f(np.concatenate([inputs["ret_decay_fwd"], inputs["ret_decay_bwd"]], axis=1)),
        "ret_gn_g": f(inputs["ret_gn_g"]), "ret_gn_b": f(inputs["ret_gn_b"]),
        "na_bT": np.ascontiguousarray(np.transpose(np.asarray(inputs["na_bias"], np.float32)[:, :, ::-1, :], (0, 3, 1, 2))),
        "w_out": f(inputs["w_out"]), "final_g": f(inputs["final_g"]),
    }
    for k, v in cs.items():
        shared["k_" + k] = v
    maps = []
    for core in range(8):
        b = core // 2
        m = dict(shared)
        m["x"] = f(inputs["x"][b])
        m["ctx"] = f(inputs["ctx"][b])
        m["c2"] = f(np.stack([inputs["c"][b], inputs["c_ctx"]], axis=0))
        maps.append(m)
    return maps


_CACHE = {}


def kernel(**inputs):
    if "prog" not in _CACHE:
        _CACHE["prog"] = build_program()
    nc, cs, stats, _ = _CACHE["prog"]
    maps = make_in_maps(inputs, cs)
    res = run_bass_kernel_spmd(nc, maps, core_ids=list(range(8)))
    outs = [res.results[2 * b]["out"] for b in range(4)]
    return np.stack(outs, axis=0).astype(np.float32)
```

```python
import contextlib
import numpy as np
import ml_dtypes
import concourse.bass as bass
import concourse.mybir as mybir
from concourse.bass_utils import run_bass_kernel_spmd

F32 = mybir.dt.float32
BF16 = mybir.dt.bfloat16
AF = mybir.ActivationFunctionType
ALU = mybir.AluOpType
AX = mybir.AxisListType

D = 4096
SEQ = 2048
CTX = 256
TALL = SEQ + CTX
DEPTH = 2
GW = 1024
HD = 128
PT = 13824
GRID_W = 64
SCALE = HD ** -0.5
NEPS = 1e-6
LEPS = 1e-5
SAME_ENG_SYNC = True

OFF = dict(A_a=0, A_g=1024, A_gate=2048, B_q=3072, B_k=4096, B_v=4352, B_gate=4608,
           C_q=5632, C_k=6656, C_v=7680, C_gate=8704, D_q=9728, D_k=10752, D_v=11776, D_gate=12800)


class _Op:
    __slots__ = ("eng", "fn", "deps", "dsem", "cnt", "sig", "idx")


class _Rec:
    def __init__(self):
        self.call = None

    def __getattr__(self, name):
        def f(*a, **k):
            self.call = (name, a, k)
            return self
        return f


class Prog:
    def __init__(self, nc):
        self.nc = nc
        self.ops = []
        self.state = {}
        self.dcnt = {}
        self.sb_off = 20480
        self.sb_marks = []
        self.ntens = 0
        self.debug = False
        self.last_dma = {}

    def sb(self, shape, dtype, name=None):
        esz = 4 if dtype == F32 else 2
        n = 1
        for s in shape[1:]:
            n *= s
        nbytes = (n * esz + 31) // 32 * 32
        self.ntens += 1
        t = self.nc.alloc_sbuf_tensor_at(f"{name or 't'}_{self.ntens}", list(shape), dtype, offset=self.sb_off)
        self.sb_off += nbytes
        assert self.sb_off <= 229376, f"SBUF overflow {self.sb_off} at {name}"
        return t

    def mark(self):
        self.sb_marks.append(self.sb_off)

    def release(self):
        self.sb_off = self.sb_marks.pop()

    def _add(self, eng, fn, reads, writes, dsem=None):
        op = _Op()
        if fn is not None:
            rec = _Rec()
            fn(rec)
            c = rec.call
            fn = lambda e, c=c: getattr(e, c[0])(*c[1], **c[2])
        op.eng = eng; op.fn = fn; op.dsem = dsem; op.idx = len(self.ops); op.sig = None
        deps = []

        def push(lst, o):
            lst[:] = [w for w in lst if not ((w.dsem is None and o.dsem is None and w.eng == o.eng)
                                             or (w.dsem is not None and w.dsem == o.dsem))]
            lst.append(o)
        for k in reads:
            st = self.state.setdefault(k, [[], [], []])
            deps.extend(st[1])
            push(st[2], op)
        for k in writes:
            st = self.state.setdefault(k, [[], [], []])
            if st[2]:
                st[0] = st[2]; old_w = st[1]; st[1] = [op]; st[2] = []
                deps.extend(st[0]); deps.extend(old_w)
            else:
                deps.extend(st[0]); deps.extend(st[1])
                push(st[1], op)
        if dsem is not None:
            prev = self.last_dma.get(dsem)
            if prev is not None:
                deps.append(prev)
            self.last_dma[dsem] = op
        op.deps = [d for d in deps if d is not op]
        if dsem is not None:
            self.dcnt[dsem] = self.dcnt.get(dsem, 0) + 16
            op.cnt = self.dcnt[dsem]
        self.ops.append(op)
        return op

    def pe(self, fn, reads=(), writes=()):
        return self._add("pe", fn, reads, writes)

    def act(self, fn, reads=(), writes=()):
        return self._add("act", fn, reads, writes)

    def dve(self, fn, reads=(), writes=()):
        return self._add("dve", fn, reads, writes)

    def pool(self, fn, reads=(), writes=()):
        return self._add("pool", fn, reads, writes)

    def dma(self, q, dsem, out, in_, reads=(), writes=(), **kw):
        return self._add(q, lambda e: e.dma_start(out=out, in_=in_, **kw), reads, writes, dsem=dsem)

    def barrier(self):
        alld = {}
        for st in self.state.values():
            for lst in st:
                for o in lst:
                    alld[o.idx] = o
        alld = list(alld.values())
        for eng in ("pe", "act", "dve", "pool", "sp"):
            op = self._add(eng, None, (), ())
            op.deps = list(alld)
        self.state = {}

    def wait_all(self, eng, keys):
        return self._add(eng, None, list(keys), ())

    def emit(self):
        nc = self.nc
        ops = self.ops
        need = set()
        for op in ops:
            for d in op.deps:
                if d.dsem is None:
                    if d.eng == op.eng and op.dsem is None and (d.eng == "pe" or not SAME_ENG_SYNC):
                        continue
                    need.add(d.idx)
        cnt = {e: 0 for e in ("pe", "act", "dve", "pool", "sp")}
        for op in ops:
            if op.dsem is None:
                if op.fn is not None and op.idx in need:
                    cnt[op.eng] += 1
                    op.sig = True
                    op.cnt = cnt[op.eng]
                else:
                    op.sig = False
                    op.cnt = None
        dsems = sorted(self.dcnt.keys())
        import bisect
        dlist = {d: ([], []) for d in dsems}
        for op in ops:
            if op.dsem is not None:
                dlist[op.dsem][0].append(op.idx)
                dlist[op.dsem][1].append(op.cnt)

        def dma_wait_val(d, idx):
            ii, cc = dlist[d]
            j = bisect.bisect_left(ii, idx)
            return cc[j - 1] if j > 0 else 0
        with contextlib.ExitStack() as es:
            esem = {e: es.enter_context(nc.semaphore(f"s_{e}")) for e in ("pe", "act", "dve", "pool")}
            dsem = {d: es.enter_context(nc.semaphore(f"d_{d}")) for d in dsems}
            block = es.enter_context(nc.Block())

            def stream(engname):
                def body(eng):
                    waited = {}
                    for op in ops:
                        if op.eng != engname:
                            continue
                        w = {}
                        for d in op.deps:
                            if d.dsem is not None:
                                key = ("d", d.dsem); val = max(d.cnt, dma_wait_val(d.dsem, op.idx))
                            else:
                                if d.fn is None:
                                    continue
                                if d.eng == engname and op.dsem is None and (engname == "pe" or not SAME_ENG_SYNC):
                                    continue
                                key = ("e", d.eng); val = d.cnt
                            if val > w.get(key, 0):
                                w[key] = val
                        for key, val in w.items():
                            if waited.get(key, 0) >= val:
                                continue
                            waited[key] = val
                            s = dsem[key[1]] if key[0] == "d" else esem[key[1]]
                            eng.wait_ge(s, val)
                        if op.fn is None:
                            continue
                        ins = op.fn(eng)
                        if op.dsem is not None:
                            ins.then_inc(dsem[op.dsem], 16)
                        elif op.sig:
                            ins.then_inc(esem[engname], 1)
                return body

            block.tensor(stream("pe"))
            block.scalar(stream("act"))
            block.vector(stream("dve"))
            block.gpsimd(stream("pool"))
            block.sync(stream("sp"))
        return cnt, {d: self.dcnt[d] for d in dsems}


class Rot:
    def __init__(self, bufs, name):
        self.bufs = bufs
        self.name = name
        self.i = 0

    def next(self):
        b = self.bufs[self.i % len(self.bufs)]
        k = (self.name, self.i % len(self.bufs))
        self.i += 1
        return b, k


def _r_start(r, rows=32, kr=8):
    return min(max(r - kr // 2, 0), rows - kr)


NA_PAIRS = []
for _qb in range(4):
    for _j in range(16):
        lo = min(_r_start(r) for r in range(8 * _qb, 8 * _qb + 8))
        hi = max(_r_start(r) + 7 for r in range(8 * _qb, 8 * _qb + 8))
        if 2 * _j + 1 >= lo and 2 * _j <= hi:
            NA_PAIRS.append((_qb, _j))


def host_consts():
    c = {}
    bf = ml_dtypes.bfloat16
    c["ident"] = np.eye(128, dtype=np.float32).astype(bf)
    c["identf"] = np.eye(128, dtype=np.float32)
    c["onesf"] = np.ones((128, 128), np.float32)
    c["onesb"] = np.ones((128, 128), np.float32).astype(bf)
    t = np.arange(SEQ)
    row = (t // GRID_W).astype(np.float32)
    col = (t % GRID_W).astype(np.float32)
    half = HD // 2
    inv_freq = (np.float32(10000.0) ** (-np.arange(0, half, 2, dtype=np.float32) / np.float32(half))).astype(np.float32)
    ang_r = row[:, None] * inv_freq[None, :]
    ang_c = col[:, None] * inv_freq[None, :]
    ang = np.concatenate([ang_r, ang_r, ang_c, ang_c], axis=-1).astype(np.float32)
    cosT = np.ones((128, TALL), np.float32)
    sinT = np.zeros((128, TALL), np.float32)
    cosT[:, CTX:] = np.cos(ang).T
    sinT[:, CTX:] = np.sin(ang).T
    c["cosT"] = cosT
    c["sinT"] = sinT
    R = np.zeros((128, 128), np.float32)
    for i in range(32):
        R[i, 32 + i] = -1.0
        R[32 + i, i] = 1.0
        R[64 + i, 96 + i] = -1.0
        R[96 + i, 64 + i] = 1.0
    c["permT"] = R.T.copy().astype(bf)
    a = np.arange(128, dtype=np.float32)
    diff = a[None, :] - a[:, None]
    c["ret_tabs"] = np.stack([
        diff, -diff,
        (diff >= 0).astype(np.float32), (diff <= 0).astype(np.float32),
        np.broadcast_to(a[None, :] + 1.0, (128, 128)), np.broadcast_to(128.0 - a[None, :], (128, 128)),
    ], axis=1).astype(np.float32)
    c["ret_pidx"] = np.stack([127.0 - a, a, np.full(128, 128.0, np.float32)], axis=1).astype(np.float32)
    rm = np.zeros((128, len(NA_PAIRS), 512), np.float32)
    for pi, (qb, j) in enumerate(NA_PAIRS):
        for i in range(2):
            for jj in range(8):
                kr = 2 * j + i
                qr = 8 * qb + jj
                rs = _r_start(qr)
                if rs <= kr <= rs + 7:
                    rm[64 * i:64 * i + 64, pi, 64 * jj:64 * jj + 64] = 1.0
    c["na_rmask"] = rm.astype(bf)
    cv = np.zeros((64, 64), np.float32)
    for qc in range(64):
        cs = min(max(qc - 8, 0), 48)
        cv[cs:cs + 16, qc] = 1.0
    c["na_cv"] = np.concatenate([cv, cv], 0)
    z = np.zeros((31, 64, 128), np.float32)
    for qc in range(64):
        for p in range(128):
            j = (p % 64) - qc + 15
            if 0 <= j < 31:
                z[j, qc, p] = 1.0
    c["na_z"] = z.astype(bf)
    return c


CONST_SPECS = None


def build_program(stop_after=None, dbg=False):
    nc = bass.Bass("TRN2", target_bir_lowering=False)
    p = Prog(nc)

    def din(name, shape, dt=F32):
        return nc.dram_tensor(name, list(shape), dt, kind="ExternalInput").ap()

    def dscr(name, shape, dt):
        if dbg:
            return nc.dram_tensor(name, list(shape), dt, kind="ExternalOutput").ap()
        return nc.dram_tensor(name, list(shape), dt).ap()

    x_in = din("x", [SEQ, D])
    ctx_in = din("ctx", [CTX, D])
    c2_in = din("c2", [2, D])
    ada_w = din("ada_w", [DEPTH, D, 3 * D])
    ada_b = din("ada_b", [DEPTH, 3 * D])
    norm_g = din("norm_g", [DEPTH, D])
    w_in = din("w_in", [DEPTH, D, PT])
    conv_w = din("conv_w", [DEPTH, 31, GW])
    conv_b = din("conv_b", [DEPTH, GW])
    conv_ln_g = din("conv_ln_g", [DEPTH, GW])
    conv_ln_b = din("conv_ln_b", [DEPTH, GW])
    gqa_qn_g = din("gqa_qn_g", [DEPTH, HD])
    gqa_kn_g = din("gqa_kn_g", [DEPTH, HD])
    ret_dec = din("ret_dec", [DEPTH, 16])
    ret_gn_g = din("ret_gn_g", [DEPTH, GW])
    ret_gn_b = din("ret_gn_b", [DEPTH, GW])
    na_bT = din("na_bT", [DEPTH, 31, 8, 15])
    w_out = din("w_out", [DEPTH, D, D])
    final_g = din("final_g", [D])
    hc = {}
    cs = host_consts()
    for k, v in cs.items():
        hc[k] = din("k_" + k, v.shape, BF16 if v.dtype == ml_dtypes.bfloat16 else F32)
    out = nc.dram_tensor("out", [SEQ, D], F32, kind="ExternalOutput").ap()

    mod = dscr("mod", [DEPTH, 2, 3 * D], F32)
    xres = dscr("xres", [TALL, D], F32)
    uT = dscr("uT", [GW, TALL], BF16)
    sgT = dscr("sgT", [D, TALL], BF16)
    qT = {m: dscr("qT" + m, [GW, TALL], BF16) for m in "BCD"}
    kT = {"B": dscr("kTB", [256, TALL], BF16), "C": dscr("kTC", [GW, TALL], BF16), "D": dscr("kTD", [GW, TALL], BF16)}
    vS = {"B": dscr("vB", [TALL, 256], BF16), "C": dscr("vC", [TALL, GW], BF16), "D": dscr("vD", [TALL, GW], BF16)}
    yT = dscr("yT", [D, TALL], BF16)
    dbg_outs = {}

    psA = [nc.alloc_psum_tensor(f"psA{i}", [128, 512], F32) for i in range(4)]
    psB = [nc.alloc_psum_tensor(f"psB{i}", [128, 512], F32) for i in range(2)]
    psT = [nc.alloc_psum_tensor(f"psT{i}", [128, 8, 128], BF16) for i in range(2)]
    rA = Rot(psA, "psA")
    rB = Rot(psB, "psB")
    rT = Rot(psT, "psT")

    ident = p.sb([128, 128], BF16, "ident")
    identf = p.sb([128, 128], F32, "identf")
    onesf = p.sb([128, 128], F32, "onesf")
    onesb = p.sb([128, 128], BF16, "onesb")
    permT = p.sb([128, 128], BF16, "permT")
    for t, n in ((ident, "ident"), (identf, "identf"), (onesf, "onesf"), (onesb, "onesb"), (permT, "permT")):
        p.dma("sp", "const", t[:], hc[n], writes=[n])
    small = p.sb([128, 64], F32, "small")
    nc_ctx = nc.allow_non_contiguous_dma(reason="small per-channel vectors")
    nc_ctx.__enter__()

    def wload(q, dsem, dst, src, wkey):
        p.dma(q, dsem, dst, src, writes=[wkey])

    p.mark()
    cT = p.sb([128, 2, 32], F32, "cT")
    scT = p.sb([128, 2, 32], BF16, "scT")
    modsb = p.sb([2, 3 * D], F32, "modsb")
    tmp2 = p.sb([2, 3 * D], F32, "tmp2")
    wbuf = [p.sb([128, 32, 256], BF16, f"adaw{i}") for i in range(3)]
    rW = Rot(wbuf, "adaw")
    for r in range(2):
        p.dma("sp", "c2", cT[:, r, :], c2_in[r].rearrange("(k p) -> p k", p=128), writes=["cT"])
    p.act(lambda e: e.activation(out=scT[:], in_=cT[:], func=AF.Silu), reads=["cT"], writes=["scT"])
    for l in range(DEPTH):
        for nb in range(3 * D // 256):
            wb, wk = rW.next()
            p.dma("pool", "w%d" % wk[1], wb[:], ada_w[l, :, nb * 256:(nb + 1) * 256].rearrange("(k p) n -> p k n", p=128), writes=[wk])
            ps, pk = rA.next()
            for k in range(32):
                p.pe(lambda e, ps=ps, wb=wb, k=k: e.matmul(ps[0:2, 0:256], lhsT=scT[:, :, k], rhs=wb[:, k, :], start=(k == 0), stop=(k == 31)),
                     reads=["scT", wk], writes=[pk])
            p.dve(lambda e, ps=ps, nb=nb: e.tensor_copy(out=modsb[:, nb * 256:(nb + 1) * 256], in_=ps[0:2, 0:256]), reads=[pk], writes=["modsb"])
        p.dma("sp", "c2", tmp2[:], ada_b[l:l + 1, :].broadcast_to([2, 3 * D]) if False else ada_b[l].partition_broadcast(2), writes=["tmp2"])
        p.dve(lambda e: e.tensor_tensor(out=modsb[:], in0=modsb[:], in1=tmp2[:], op=ALU.add), reads=["modsb", "tmp2"], writes=["modsb"])
        p.dma("sp", "c2", tmp2[:, 0:D], norm_g[l].partition_broadcast(2), reads=[], writes=["tmp2"])
        p.dve(lambda e: e.scalar_tensor_tensor(out=modsb[:, D:2 * D], in0=modsb[:, D:2 * D], scalar=1.0, in1=tmp2[:, 0:D], op0=ALU.add, op1=ALU.mult),
              reads=["modsb", "tmp2"], writes=["modsb"])
        p.dma("sp", "modst", mod[l], modsb[:], reads=["modsb"], writes=["mod"])
    p.barrier()
    p.release()

    def load_small(l):
        def col(dst_c, vec, nb):
            p.dma("sp", "small", small[:, dst_c:dst_c + nb], vec.rearrange("(b p) -> p b", p=128), writes=["small"])
        col(0, conv_b[l], 8); col(8, conv_ln_g[l], 8); col(16, conv_ln_b[l], 8)
        col(24, ret_gn_g[l], 8); col(32, ret_gn_b[l], 8)
        col(40, gqa_qn_g[l], 1); col(41, gqa_kn_g[l], 1)

    groups = [(0, 1280), (1280, 1024)]

    for l in range(DEPTH):
        last = (l == DEPTH - 1)
        load_small(l)
        for (tok0, T) in groups:
            p.mark()
            hlT = p.sb([128, 32, T], BF16, "hlT")
            p.mark()
            xt = [p.sb([128, D], F32, f"xt{i}") for i in range(3)]
            rX = Rot(xt, "xt")
            junk = p.sb([128, D], BF16, "junk")
            hlbs = [p.sb([128, D], BF16, f"hlb{i}") for i in range(2)]
            rH = Rot(hlbs, "hlb")
            gsb = p.sb([128, D], F32, "gsb")
            shb = p.sb([128, D], F32, "shb")
            st = p.sb([128, 8], F32, "st")
            cur_row = [None]

            def norm_a(tt):
                g0 = tok0 + tt * 128
                is_ctx = g0 < CTX
                row = 1 if is_ctx else 0
                if row != cur_row[0]:
                    p.dma("sp", "modg", gsb[:], mod[l, row, D:2 * D].partition_broadcast(128), reads=["mod"], writes=["gsb"])
                    p.dma("sp", "mods", shb[:], mod[l, row, 0:D].partition_broadcast(128), reads=["mod"], writes=["shb"])
                    cur_row[0] = row
                if l == 0:
                    src = ctx_in[g0:g0 + 128, :] if is_ctx else x_in[g0 - CTX:g0 - CTX + 128, :]
                else:
                    src = xres[g0:g0 + 128, :]
                xb_, xk = rX.next()
                p.dma("sp", "x%d" % xk[1], xb_[:], src, reads=["xres"], writes=[xk])
                p.act(lambda e: e.activation(out=junk[:], in_=xb_[:], func=AF.Square, accum_out=st[:, 0:1]), reads=[xk], writes=["junk", "st0"])
                p.act(lambda e: e.activation(out=st[:, 1:2], in_=st[:, 0:1], func=AF.Sqrt, bias=NEPS, scale=1.0 / D), reads=["st0"], writes=["st1"])
                p.dve(lambda e: e.reciprocal(out=st[:, 2:3], in_=st[:, 1:2]), reads=["st1"], writes=["st2"])
                p.dve(lambda e: e.scalar_tensor_tensor(out=xb_[:], in0=xb_[:], scalar=st[:, 2:3], in1=gsb[:], op0=ALU.mult, op1=ALU.mult),
                      reads=[xk, "st2", "gsb"], writes=[xk])
                hlb, hlbk = rH.next()
                p.pool(lambda e: e.tensor_tensor(out=hlb[:], in0=xb_[:], in1=shb[:], op=ALU.add), reads=[xk, "shb"], writes=[hlbk])
                return hlb, hlbk

            def norm_b(tt, hlb, hlbk):
                for k8 in range(4):
                    pt, ptk = rT.next()
                    for kk in range(8):
                        k = k8 * 8 + kk
                        p.pe(lambda e, pt=pt, kk=kk, k=k: e.transpose(out=pt[:, kk, :], in_=hlb[:, k * 128:(k + 1) * 128], identity=ident[:]),
                             reads=[hlbk, "ident"], writes=[ptk])
                    p.dve(lambda e, pt=pt, k8=k8: e.tensor_copy(out=hlT[:, k8 * 8:(k8 + 1) * 8, tt * 128:(tt + 1) * 128], in_=pt[:]),
                          reads=[ptk], writes=["hlT"])
            ntile = T // 128
            pend = norm_a(0)
            for tt in range(ntile):
                nxt_ = norm_a(tt + 1) if tt + 1 < ntile else None
                norm_b(tt, *pend)
                pend = nxt_
            p.barrier()
            p.release()
            if stop_after == "norm":
                break
            p.mark()
            wbuf = [p.sb([128, 32, 256], BF16, f"winw{i}") for i in range(3)]
            rW = Rot(wbuf, "winw")
            cosT = p.sb([128, T], F32, "cosT")
            sinT = p.sb([128, T], F32, "sinT")
            p.dma("sp", "const", cosT[:], hc["cosT"][:, tok0:tok0 + T], writes=["cosT"])
            p.dma("sp", "const", sinT[:], hc["sinT"][:, tok0:tok0 + T], writes=["sinT"])
            ef = [p.sb([128, 512], F32, f"ef{i}") for i in range(10)]
            rE = Rot(ef, "ef")
            eb = [p.sb([128, 512], BF16, f"eb{i}") for i in range(4)]
            rEb = Rot(eb, "eb")
            ob = [p.sb([128, 512], BF16, f"ob{i}") for i in range(4)]
            rO = Rot(ob, "ob")
            chunks = [(c0, min(512, T - c0)) for c0 in range(0, T, 512)]
            if last and tok0 == 0:
                chunks_q = [(c0, min(512, T - c0)) for c0 in range(CTX, T, 512)]
            else:
                chunks_q = chunks
            pending = []

            def flush():
                cur = pending[:]
                del pending[:]
                for cb_ in cur:
                    cb_()

            def load_w(pieces):
                wb, wk = rW.next()
                o = 0
                for (c0, wd) in pieces:
                    p.dma("pool", "w%d" % wk[1], wb[:, :, o:o + wd], w_in[l, :, c0:c0 + wd].rearrange("(k p) n -> p k n", p=128), writes=[wk])
                    o += wd
                return wb, wk

            def mm_fm(wb, wk, j, c0, n):
                ps, pk = rA.next()
                for k in range(32):
                    p.pe(lambda e, ps=ps, k=k: e.matmul(ps[:, 0:n], lhsT=wb[:, k, j * 128:(j + 1) * 128], rhs=hlT[:, k, c0:c0 + n], start=(k == 0), stop=(k == 31)),
                         reads=["hlT", wk], writes=[pk])
                return ps, pk

            def store_fm(src, sk, dst, r0, c0, n):
                p.dma("sp", "st_%s%d" % sk, dst[r0:r0 + 128, tok0 + c0:tok0 + c0 + n], src[:, 0:n], reads=[sk], writes=["scr"])

            def rope_tail(qn, qnk, c0, n, dst, r0):
                qb, qbk = rEb.next()
                p.dve(lambda e: e.tensor_copy(out=qb[:, 0:n], in_=qn[:, 0:n]), reads=[qnk], writes=[qbk])
                pending.append(lambda: rope_tail2(qn, qnk, qb, qbk, c0, n, dst, r0))

            def rope_tail2(qn, qnk, qb, qbk, c0, n, dst, r0):
                ps2, p2k = rB.next()
                p.pe(lambda e: e.matmul(ps2[:, 0:n], lhsT=permT[:], rhs=qb[:, 0:n], start=True, stop=True), reads=["permT", qbk], writes=[p2k])
                t2, t2k = rE.next()
                p.dve(lambda e: e.tensor_tensor(out=t2[:, 0:n], in0=ps2[:, 0:n], in1=sinT[:, c0:c0 + n], op=ALU.mult), reads=[p2k, "sinT"], writes=[t2k])
                p.dve(lambda e: e.tensor_tensor(out=qn[:, 0:n], in0=qn[:, 0:n], in1=cosT[:, c0:c0 + n], op=ALU.mult), reads=[qnk, "cosT"], writes=[qnk])
                o, ok = rO.next()
                p.dve(lambda e: e.tensor_tensor(out=o[:, 0:n], in0=qn[:, 0:n], in1=t2[:, 0:n], op=ALU.add), reads=[qnk, t2k], writes=[ok])
                store_fm(o, ok, dst, r0, c0, n)

            for jb in range(8):
                wb, wk = load_w([(OFF["A_a"] + jb * 128, 128), (OFF["A_g"] + jb * 128, 128)])
                for (c0, n) in chunks_q:
                    pa, pak = mm_fm(wb, wk, 0, c0, n)
                    pg, pgk = mm_fm(wb, wk, 1, c0, n)
                    sg, sgk = rE.next()
                    p.act(lambda e, sg=sg, pg=pg, n=n: e.activation(out=sg[:, 0:n], in_=pg[:, 0:n], func=AF.Sigmoid), reads=[pgk], writes=[sgk])
                    o, ok = rO.next()
                    p.dve(lambda e, o=o, pa=pa, sg=sg, n=n: e.tensor_tensor(out=o[:, 0:n], in0=pa[:, 0:n], in1=sg[:, 0:n], op=ALU.mult), reads=[pak, sgk], writes=[ok])
                    store_fm(o, ok, uT, jb * 128, c0, n)
            for mi, m in enumerate("ABCD"):
                for jb in range(4):
                    wb, wk = load_w([(OFF[m + "_gate"] + jb * 256, 256)])
                    for j in range(2):
                        for (c0, n) in chunks_q:
                            ps, pk = mm_fm(wb, wk, j, c0, n)
                            o, ok = rO.next()
                            p.act(lambda e, o=o, ps=ps, n=n: e.activation(out=o[:, 0:n], in_=ps[:, 0:n], func=AF.Silu), reads=[pk], writes=[ok])
                            store_fm(o, ok, sgT, mi * GW + jb * 256 + j * 128, c0, n)
            fm_list = [("B", "q", "normrope", 1.0, 1024, 40), ("B", "k", "normrope", 1.0, 256, 41),
                       ("C", "q", "rope", 1.0, 1024, None), ("C", "k", "rope", SCALE, 1024, None),
                       ("D", "q", "plain", 1.0, 1024, None), ("D", "k", "plain", 1.0, 1024, None)]
            for (m, which, kind, scl, width, gcol) in fm_list:
                dst = qT[m] if which == "q" else kT[m]
                for jb in range(width // 256):
                    wb, wk = load_w([(OFF[m + "_" + which] + jb * 256, 256)])
                    for j in range(2):
                        r0 = jb * 256 + j * 128
                        for (c0, n) in (chunks_q if which == "q" else chunks):
                            ps, pk = mm_fm(wb, wk, j, c0, n)
                            flush()
                            if kind == "plain":
                                o, ok = rO.next()
                                p.dve(lambda e, o=o, ps=ps, n=n: e.tensor_copy(out=o[:, 0:n], in_=ps[:, 0:n]), reads=[pk], writes=[ok])
                                store_fm(o, ok, dst, r0, c0, n)
                            elif kind == "rope":
                                qn, qnk = rE.next()
                                p.act(lambda e, qn=qn, ps=ps, n=n, scl=scl: e.activation(out=qn[:, 0:n], in_=ps[:, 0:n], func=AF.Copy, scale=scl), reads=[pk], writes=[qnk])
                                rope_tail(qn, qnk, c0, n, dst, r0)
                            else:
                                sq, sqk = rE.next()
                                p.act(lambda e, sq=sq, ps=ps, n=n: e.activation(out=sq[:, 0:n], in_=ps[:, 0:n], func=AF.Square), reads=[pk], writes=[sqk])

                                def stage_b(sq=sq, sqk=sqk, ps=ps, pk=pk, c0=c0, n=n, gcol=gcol, dst=dst, r0=r0):
                                    ps3, p3k = rB.next()
                                    p.pe(lambda e: e.matmul(ps3[:, 0:n], lhsT=onesf[:], rhs=sq[:, 0:n], start=True, stop=True), reads=["onesf", sqk], writes=[p3k])
                                    rs, rsk = rE.next()
                                    p.act(lambda e: e.activation(out=rs[:, 0:n], in_=ps3[:, 0:n], func=AF.Sqrt, bias=NEPS, scale=1.0 / HD), reads=[p3k], writes=[rsk])
                                    p.dve(lambda e: e.reciprocal(out=rs[:, 0:n], in_=rs[:, 0:n]), reads=[rsk], writes=[rsk])
                                    qn, qnk = rE.next()
                                    p.dve(lambda e: e.scalar_tensor_tensor(out=qn[:, 0:n], in0=ps[:, 0:n], scalar=small[:, gcol:gcol + 1], in1=rs[:, 0:n], op0=ALU.mult, op1=ALU.mult),
                                          reads=[pk, rsk, "small"], writes=[qnk])
                                    rope_tail(qn, qnk, c0, n, dst, r0)
                                pending.append(stage_b)
            flush(); flush(); flush()
            for m, width in (("B", 256), ("C", 1024), ("D", 1024)):
                for jb in range(width // 256):
                    wb, wk = load_w([(OFF[m + "_v"] + jb * 256, 256)])
                    for tt in range(T // 128):
                        ps, pk = rA.next()
                        for k in range(32):
                            p.pe(lambda e, ps=ps, k=k, tt=tt, wb=wb: e.matmul(ps[:, 0:256], lhsT=hlT[:, k, tt * 128:(tt + 1) * 128], rhs=wb[:, k, :], start=(k == 0), stop=(k == 31)),
                                 reads=["hlT", wk], writes=[pk])
                        o, ok = rO.next()
                        p.act(lambda e, o=o, ps=ps: e.activation(out=o[:, 0:256], in_=ps[:, 0:256], func=AF.Copy), reads=[pk], writes=[ok])
                        p.dma("sp", "st_%s%d" % ok, vS[m][tok0 + tt * 128:tok0 + (tt + 1) * 128, jb * 256:(jb + 1) * 256], o[:, 0:256], reads=[ok], writes=["scr"])
            p.barrier()
            p.release()
            p.release()
        if stop_after in ("norm", "proj"):
            break

        p.mark()
        cw31 = p.sb([31, GW], F32, "cw31")
        cwT = p.sb([128, 8, 31], F32, "cwT")
        p.dma("sp", "cw", cw31[:], conv_w[l], writes=["cw31"])
        for cb in range(8):
            ps, pk = rA.next()
            p.pe(lambda e, ps=ps, cb=cb: e.transpose(out=ps[:, 0:31], in_=cw31[:, cb * 128:(cb + 1) * 128], identity=identf[0:31, 0:31]), reads=["cw31", "identf"], writes=[pk])
            p.dve(lambda e, ps=ps, cb=cb: e.tensor_copy(out=cwT[:, cb, :], in_=ps[:, 0:31]), reads=[pk], writes=["cwT"])
        dg = p.sb([128, 8 * 31, 128], BF16, "dg")
        for cb in range(8):
            p.dve(lambda e, cb=cb: e.tensor_tensor(out=dg[:, cb * 31:(cb + 1) * 31, :], in0=identf[:].unsqueeze(1).to_broadcast([128, 31, 128]),
                                                   in1=cwT[:, cb, :].unsqueeze(2).to_broadcast([128, 31, 128]), op=ALU.mult),
                  reads=["identf", "cwT"], writes=["dg"])
        ub = [p.sb([128, 512 + 30], BF16, f"ub{i}") for i in range(3)]
        rU = Rot(ub, "ub")
        vb = [p.sb([128, 512], F32, f"vb{i}") for i in range(8)]
        sqb = [p.sb([128, 512], F32, f"sqb{i}") for i in range(2)]
        rSq = Rot(sqb, "sqb")
        mean = p.sb([128, 512], F32, "mean")
        rstd = p.sb([128, 512], F32, "rstd")
        msq = p.sb([128, 512], F32, "msq")
        sgl = [p.sb([128, 512], BF16, f"sgl{i}") for i in range(2)]
        rSg = Rot(sgl, "sgl")
        tb = [p.sb([128, 512], F32, f"tb{i}") for i in range(2)]
        rTb = Rot(tb, "tb")
        yo = [p.sb([128, 512], BF16, f"yo{i}") for i in range(3)]
        rY = Rot(yo, "yo")
        segs = [(CTX, SEQ)] if last else [(0, CTX), (CTX, SEQ)]
        for (s0, slen) in segs:
            for c0 in range(0, slen, 512):
                n = min(512, slen - c0)
                ps_s, pssk = rB.next()
                ps_q, psqk = rB.next()
                for cb in range(8):
                    u, uk = rU.next()
                    lo = max(c0 - 15, 0)
                    hi = min(c0 + n + 15, slen)
                    p.pool(lambda e, u=u: e.memset(u[:], 0.0), writes=[uk])
                    p.dma("sp", "u%d" % uk[1], u[:, lo - (c0 - 15):hi - (c0 - 15)], uT[cb * 128:(cb + 1) * 128, s0 + lo:s0 + hi], reads=["scr"], writes=[uk])
                    v, vk = vb[cb], ("vb", cb)
                    pc, pck = rA.next()
                    for j in range(31):
                        p.pe(lambda e, pc=pc, u=u, cb=cb, n=n, j=j: e.matmul(pc[:, 0:n], lhsT=dg[:, cb * 31 + j, :], rhs=u[:, j:j + n], start=(j == 0), stop=(j == 30)),
                             reads=["dg", uk], writes=[pck])
                    p.dve(lambda e, v=v, pc=pc, cb=cb, n=n: e.tensor_scalar(out=v[:, 0:n], in0=pc[:, 0:n], scalar1=small[:, cb:cb + 1], scalar2=None, op0=ALU.add),
                          reads=[pck, "small"], writes=[vk])
                    sq, sqk = rSq.next()
                    p.act(lambda e, sq=sq, v=v, n=n: e.activation(out=sq[:, 0:n], in_=v[:, 0:n], func=AF.Square), reads=[vk], writes=[sqk])
                    p.pe(lambda e, v=v, cb=cb, n=n, ps_s=ps_s: e.matmul(ps_s[:, 0:n], lhsT=onesf[:], rhs=v[:, 0:n], start=(cb == 0), stop=(cb == 7)), reads=["onesf", vk], writes=[pssk])
                    p.pe(lambda e, sq=sq, cb=cb, n=n, ps_q=ps_q: e.matmul(ps_q[:, 0:n], lhsT=onesf[:], rhs=sq[:, 0:n], start=(cb == 0), stop=(cb == 7)), reads=["onesf", sqk], writes=[psqk])
                p.dve(lambda e, n=n, ps_s=ps_s: e.tensor_scalar(out=mean[:, 0:n], in0=ps_s[:, 0:n], scalar1=1.0 / GW, scalar2=None, op0=ALU.mult), reads=[pssk], writes=["mean"])
                p.dve(lambda e, n=n: e.tensor_tensor(out=msq[:, 0:n], in0=mean[:, 0:n], in1=mean[:, 0:n], op=ALU.mult), reads=["mean"], writes=["msq"])
                p.dve(lambda e, n=n, ps_q=ps_q: e.scalar_tensor_tensor(out=rstd[:, 0:n], in0=ps_q[:, 0:n], scalar=1.0 / GW, in1=msq[:, 0:n], op0=ALU.mult, op1=ALU.subtract),
                      reads=[psqk, "msq"], writes=["rstd"])
                p.act(lambda e, n=n: e.activation(out=rstd[:, 0:n], in_=rstd[:, 0:n], func=AF.Sqrt, bias=LEPS, scale=1.0), reads=["rstd"], writes=["rstd"])
                p.dve(lambda e, n=n: e.reciprocal(out=rstd[:, 0:n], in_=rstd[:, 0:n]), reads=["rstd"], writes=["rstd"])
                for cb in range(8):
                    v, vk = vb[cb], ("vb", cb)
                    sg, sgk = rSg.next()
                    p.dma("sp", "sg%d" % sgk[1], sg[:, 0:n], sgT[cb * 128:(cb + 1) * 128, s0 + c0:s0 + c0 + n], reads=["scr"], writes=[sgk])
                    p.dve(lambda e, v=v, n=n: e.tensor_tensor(out=v[:, 0:n], in0=v[:, 0:n], in1=mean[:, 0:n], op=ALU.subtract), reads=[vk, "mean"], writes=[vk])
                    p.pool(lambda e, v=v, n=n: e.tensor_tensor(out=v[:, 0:n], in0=v[:, 0:n], in1=rstd[:, 0:n], op=ALU.mult), reads=[vk, "rstd"], writes=[vk])
                    t_, tk = rTb.next()
                    p.act(lambda e, t_=t_, v=v, cb=cb, n=n: e.activation(out=t_[:, 0:n], in_=v[:, 0:n], func=AF.Silu, scale=small[:, 8 + cb:9 + cb], bias=small[:, 16 + cb:17 + cb]),
                          reads=[vk, "small"], writes=[tk])
                    y, yk = rY.next()
                    p.dve(lambda e, y=y, t_=t_, sg=sg, n=n: e.tensor_tensor(out=y[:, 0:n], in0=t_[:, 0:n], in1=sg[:, 0:n], op=ALU.mult), reads=[tk, sgk], writes=[yk])
                    p.dma("sp", "yst%d" % yk[1], yT[cb * 128:(cb + 1) * 128, s0 + c0:s0 + c0 + n], y[:, 0:n], reads=[yk], writes=["yT"])
        p.barrier()
        p.release()
        if stop_after == "conv":
            break

        def attention(m):
            p.mark()
            nkv = 2 if m == "B" else 8
            KT = [p.sb([128, TALL], BF16, f"KT{i}") for i in range(2)]
            rK = Rot(KT, "KT")
            VV = [p.sb([128, 18, 128], BF16, f"VV{i}") for i in range(2)]
            rV = Rot(VV, "VV")
            QT = [p.sb([128, TALL], BF16, f"QT{i}") for i in range(2)]
            rQ = Rot(QT, "QT")
            SG = [p.sb([128, TALL], BF16, f"SG{i}") for i in range(2)]
            rS = Rot(SG, "SG")
            pT = [p.sb([128, 512], BF16, f"pT{i}") for i in range(4)]
            rP = Rot(pT, "pT")
            pM = [p.sb([128, 512], BF16, f"pM{i}") for i in range(3)]
            rM = Rot(pM, "pM")
            rden = p.sb([128, 512], F32, "rden")
            of = p.sb([128, 512], F32, "of")
            yo = [p.sb([128, 512], BF16, f"yo{i}") for i in range(3)]
            rY = Rot(yo, "yo")
            mrow = {"B": GW, "D": 3 * GW}[m]
            if m == "D":
                rmask = p.sb([128, len(NA_PAIRS), 512], BF16, "rmask")
                p.dma("sp", "const", rmask[:], hc["na_rmask"], writes=["rmask"])
                cv = p.sb([128, 64], F32, "cv")
                p.dma("sp", "const", cv[:], hc["na_cv"], writes=["cv"])
                zd = p.sb([31, 64, 128], BF16, "zd")
                p.dma("sp", "const", zd[:], hc["na_z"], writes=["zd"])
                nb = p.sb([31, 8, 15], BF16, "nb")
                p.dma("pool", "nbld", nb[:], na_bT[l], writes=["nb"])
                G2 = [p.sb([128, 29 * 64], BF16, f"G2{i}") for i in range(2)]
                for g2 in G2:
                    p.pool(lambda e, g2=g2: e.memset(g2[:], 0.0), writes=[("G2", G2.index(g2))])
                rG = Rot(G2, "G2")
                gex = p.sb([128, 15, 64], F32, "gex")
            kvl = {}
            qsl = {}
            allheads = [(kv, h) for kv in range(nkv) for h in ([kv * 4 + g for g in range(4)] if m == "B" else [kv])]

            def load_kv(kv):
                K_, Kk = rK.next()
                p.dma("sp", "K%d" % Kk[1], K_[:], kT[m][kv * 128:(kv + 1) * 128, :], reads=["scr"], writes=[Kk])
                V_, Vk = rV.next()
                p.dma("sp", "V%d" % Vk[1], V_[:], vS[m][:, kv * 128:(kv + 1) * 128].rearrange("(t p) d -> p t d", p=128), reads=["scr"], writes=[Vk])
                kvl[kv] = (K_, Kk, V_, Vk)

            def load_qs(i):
                h = allheads[i][1]
                Q_, Qk = rQ.next()
                p.dma("sp", "Q%d" % Qk[1], Q_[:], qT[m][h * 128:(h + 1) * 128, :], reads=["scr"], writes=[Qk])
                S_, Sk = rS.next()
                p.dma("sp", "S%d" % Sk[1], S_[:], sgT[mrow + h * 128:mrow + (h + 1) * 128, :], reads=["scr"], writes=[Sk])
                qsl[i] = (Q_, Qk, S_, Sk)
            load_kv(0)
            load_qs(0)
            hidx = 0
            for kv in range(nkv):
                K_, Kk, V_, Vk = kvl[kv]
                if kv + 1 < nkv:
                    load_kv(kv + 1)
                heads = [kv * 4 + g for g in range(4)] if m == "B" else [kv]
                if m == "D":
                    h = kv
                    g2, g2k = rG.next()
                    for half in range(2):
                        e0, e1 = (0, 8) if half == 0 else (8, 15)
                        ps, pk = rB.next()
                        psv = ps[:, 0:(e1 - e0) * 64].rearrange("p (e q) -> p e q", q=64)
                        for qc in range(64):
                            p.pe(lambda e, psv=psv, qc=qc, e0=e0, e1=e1, h=h: e.matmul(psv[:, :, qc], lhsT=zd[:, qc, :], rhs=nb[:, h, e0:e1], start=True, stop=True),
                                 reads=["zd", "nb"], writes=[pk])
                        p.act(lambda e, psv=psv, e0=e0, e1=e1: e.activation(out=gex[:, e0:e1, :], in_=psv, func=AF.Exp), reads=[pk], writes=["gex"])
                    g2v = g2[:].rearrange("p (e q) -> p e q", q=64)
                    p.dve(lambda e, g2v=g2v: e.tensor_tensor(out=g2v[0:64, 7:22, :], in0=gex[0:64, :, :], in1=cv[0:64, :].unsqueeze(1).to_broadcast([64, 15, 64]), op=ALU.mult),
                          reads=["gex", "cv"], writes=[g2k])
                    p.dve(lambda e, g2v=g2v: e.tensor_tensor(out=g2v[64:128, 8:23, :], in0=gex[64:128, :, :], in1=cv[64:128, :].unsqueeze(1).to_broadcast([64, 15, 64]), op=ALU.mult),
                          reads=["gex", "cv"], writes=[g2k])
                for h in heads:
                    Q_, Qk, S_, Sk = qsl[hidx]
                    if hidx + 1 < len(allheads):
                        load_qs(hidx + 1)
                    hidx += 1
                    qchunks = []
                    if not last:
                        qchunks.append((0, 256, [(0, None), (1, None)]))
                    for qb in range(4):
                        if m == "B":
                            kts = [(t, None) for t in range(18)]
                        else:
                            kts = [(0, None), (1, None)] + [(2 + j, NA_PAIRS.index((qb, j))) for j in range(16) if (qb, j) in NA_PAIRS]
                        qchunks.append((CTX + qb * 512, 512, kts))
                    for (q0, n, kts) in qchunks:
                        ps_o, pok = rB.next()
                        ps_d, pdk = rB.next()

                        def score(i):
                            kt, _ = kts[i]
                            ps, pk = rA.next()
                            p.pe(lambda e, ps=ps, kt=kt: e.matmul(ps[:, 0:n], lhsT=K_[:, kt * 128:(kt + 1) * 128], rhs=Q_[:, q0:q0 + n], start=True, stop=True),
                                 reads=[Kk, Qk], writes=[pk])
                            return ps, pk
                        nxtq = [score(0)]
                        if len(kts) > 1:
                            nxtq.append(score(1))
                        for i, (kt, pi) in enumerate(kts):
                            ps, pk = nxtq.pop(0)
                            if i + 2 < len(kts):
                                nxtq.append(score(i + 2))
                            pt, ptk = rP.next()
                            p.act(lambda e, pt=pt, ps=ps: e.activation(out=pt[:, 0:n], in_=ps[:, 0:n], func=AF.Exp, scale=SCALE), reads=[pk], writes=[ptk])
                            if pi is not None:
                                qb_, j_ = NA_PAIRS[pi]
                                e0 = 8 * qb_ - 2 * j_ + 7
                                pm, pmk = rM.next()
                                p.dve(lambda e, pm=pm, pt=pt, e0=e0: e.tensor_tensor(out=pm[:, 0:n], in0=pt[:, 0:n], in1=g2[:, (e0 + 7) * 64:(e0 + 7) * 64 + 512], op=ALU.mult),
                                      reads=[ptk, g2k], writes=[pmk])
                                p.pool(lambda e, pm=pm, pi=pi: e.tensor_tensor(out=pm[:, 0:n], in0=pm[:, 0:n], in1=rmask[:, pi, :], op=ALU.mult), reads=[pmk, "rmask"], writes=[pmk])
                                pt, ptk = pm, pmk
                            first = (i == 0)
                            lastk = (i == len(kts) - 1)
                            p.pe(lambda e, pt=pt, kt=kt, first=first, lastk=lastk: e.matmul(ps_o[:, 0:n], lhsT=V_[:, kt, :], rhs=pt[:, 0:n], start=first, stop=lastk),
                                 reads=[Vk, ptk], writes=[pok])
                            p.pe(lambda e, pt=pt, first=first, lastk=lastk: e.matmul(ps_d[:, 0:n], lhsT=onesb[:], rhs=pt[:, 0:n], start=first, stop=lastk),
                                 reads=["onesb", ptk], writes=[pdk])
                        p.dve(lambda e: e.reciprocal(out=rden[:, 0:n], in_=ps_d[:, 0:n]), reads=[pdk], writes=["rden"])
                        p.dve(lambda e: e.tensor_tensor(out=of[:, 0:n], in0=ps_o[:, 0:n], in1=rden[:, 0:n], op=ALU.mult), reads=[pok, "rden"], writes=["of"])
                        y, yk = rY.next()
                        p.pool(lambda e, y=y: e.tensor_tensor(out=y[:, 0:n], in0=of[:, 0:n], in1=S_[:, q0:q0 + n], op=ALU.mult), reads=["of", Sk], writes=[yk])
                        p.dma("sp", "yst%d" % yk[1], yT[mrow + h * 128:mrow + (h + 1) * 128, q0:q0 + n], y[:, 0:n], reads=[yk], writes=["yT"])
            p.barrier()
            p.release()

        attention("B")
        if stop_after == "gqa":
            break
        attention("D")
        if stop_after == "na":
            break

        p.mark()
        rtab = p.sb([128, 6, 128], F32, "rtab")
        p.dma("sp", "const", rtab[:], hc["ret_tabs"], writes=["rtab"])
        pidx = p.sb([128, 3], F32, "pidx")
        p.dma("sp", "const", pidx[:], hc["ret_pidx"], writes=["pidx"])
        LG = p.sb([128, 16], F32, "LG")
        p.dma("sp", "const", LG[:], ret_dec[l].partition_broadcast(128), writes=["LG"])
        p.act(lambda e: e.activation(out=LG[:], in_=LG[:], func=AF.Exp, scale=-1.0), reads=["LG"], writes=["LG"])
        p.act(lambda e: e.activation(out=LG[:], in_=LG[:], func=AF.Ln, bias=1.0, scale=1.0), reads=["LG"], writes=["LG"])
        p.dve(lambda e: e.tensor_scalar(out=LG[:], in0=LG[:], scalar1=-1.0, scalar2=None, op0=ALU.mult), reads=["LG"], writes=["LG"])
        Dm = p.sb([128, 16, 128], F32, "Dm")
        qdec = p.sb([128, 16, 128], F32, "qdec")
        kdec = p.sb([128, 16], F32, "kdec")
        cdec = p.sb([128, 16], F32, "cdec")
        for hd in range(16):
            d = hd // 8
            p.act(lambda e, hd=hd, d=d: e.activation(out=Dm[:, hd, :], in_=rtab[:, d, :], func=AF.Exp, scale=LG[:, hd:hd + 1]), reads=["rtab", "LG"], writes=["Dm"])
            p.dve(lambda e, hd=hd, d=d: e.tensor_tensor(out=Dm[:, hd, :], in0=Dm[:, hd, :], in1=rtab[:, 2 + d, :], op=ALU.mult), reads=["Dm", "rtab"], writes=["Dm"])
            p.act(lambda e, hd=hd, d=d: e.activation(out=qdec[:, hd, :], in_=rtab[:, 4 + d, :], func=AF.Exp, scale=LG[:, hd:hd + 1]), reads=["rtab", "LG"], writes=["qdec"])
            p.act(lambda e, hd=hd, d=d: e.activation(out=kdec[:, hd:hd + 1], in_=pidx[:, d:d + 1], func=AF.Exp, scale=LG[:, hd:hd + 1]), reads=["pidx", "LG"], writes=["kdec"])
            p.act(lambda e, hd=hd: e.activation(out=cdec[:, hd:hd + 1], in_=pidx[:, 2:3], func=AF.Exp, scale=LG[:, hd:hd + 1]), reads=["pidx", "LG"], writes=["cdec"])
        KT = [p.sb([128, TALL], BF16, f"rKT{i}") for i in range(2)]
        rK = Rot(KT, "rKT")
        QT = [p.sb([128, TALL], BF16, f"rQT{i}") for i in range(2)]
        rQ = Rot(QT, "rQT")
        VV = [p.sb([128, 18, 128], BF16, f"rVV{i}") for i in range(2)]
        rV = Rot(VV, "rVV")
        SG = [p.sb([128, TALL], BF16, f"rSG{i}") for i in range(2)]
        rS = Rot(SG, "rSG")
        Kd = [p.sb([128, 18, 128], BF16, f"Kd{i}") for i in range(2)]
        oacc = p.sb([128, 18, 128], F32, "oacc")
        osq = p.sb([128, 18, 128], F32, "osq")
        onb = p.sb([128, 18, 128], BF16, "onb")
        gst = p.sb([128, 4, 18], F32, "gst")
        attm = [p.sb([128, 128], BF16, f"attm{i}") for i in range(3)]
        rAt = Rot(attm, "attm")
        qdb = [p.sb([128, 128], BF16, f"qdb{i}") for i in range(3)]
        rQd = Rot(qdb, "qdb")
        qd_all = [p.sb([128, 18, 128], BF16, f"qdall{i}") for i in range(2)]
        attm_all = [p.sb([128, 18, 128], BF16, f"attmall{i}") for i in range(2)]
        S32 = [p.sb([128, 128], F32, f"S32{i}") for i in range(2)]
        Sbf = [p.sb([128, 128], BF16, f"Sbf{i}") for i in range(2)]
        yf = p.sb([128, 512], F32, "yf")
        yo = [p.sb([128, 512], BF16, f"ryo{i}") for i in range(3)]
        rY = Rot(yo, "ryo")
        rl = {}

        def load_ret(h):
            K_, Kk = rK.next()
            p.dma("sp", "K%d" % Kk[1], K_[:], kT["C"][h * 128:(h + 1) * 128, :], reads=["scr"], writes=[Kk])
            Q_, Qk = rQ.next()
            p.dma("sp", "Q%d" % Qk[1], Q_[:], qT["C"][h * 128:(h + 1) * 128, :], reads=["scr"], writes=[Qk])
            V_, Vk = rV.next()
            p.dma("sp", "V%d" % Vk[1], V_[:], vS["C"][:, h * 128:(h + 1) * 128].rearrange("(t p) d -> p t d", p=128), reads=["scr"], writes=[Vk])
            S_, Sk = rS.next()
            p.dma("sp", "S%d" % Sk[1], S_[:], sgT[2 * GW + h * 128:2 * GW + (h + 1) * 128, :], reads=["scr"], writes=[Sk])
            rl[h] = (K_, Kk, Q_, Qk, V_, Vk, S_, Sk)
        load_ret(0)
        oaccB = p.sb([128, 18, 128], F32, "oaccB")
        for h in range(8):
            K_, Kk, Q_, Qk, V_, Vk, S_, Sk = rl[h]
            if h + 1 < 8:
                load_ret(h + 1)
            for c in range(18):
                pt, ptk = rT.next()
                p.pe(lambda e, pt=pt, c=c: e.transpose(out=pt[:, 0, :], in_=K_[:, c * 128:(c + 1) * 128], identity=ident[:]), reads=[Kk, "ident"], writes=[ptk])
                for d in range(2):
                    hd = d * 8 + h
                    p.dve(lambda e, pt=pt, c=c, d=d, hd=hd: e.tensor_scalar(out=Kd[d][:, c, :], in0=pt[:, 0, :], scalar1=kdec[:, hd:hd + 1], scalar2=None, op0=ALU.mult),
                          reads=[ptk, "kdec"], writes=[("Kd", d)])
            orders = [list(range(18)), [1, 0] + list(range(17, 1, -1))]
            for c in range(18):
                if last and c < 2:
                    continue
                for d in range(2):
                    hd = d * 8 + h
                    p.pool(lambda e, c=c, d=d, hd=hd: e.tensor_tensor(out=qd_all[d][:, c, :], in0=Q_[:, c * 128:(c + 1) * 128], in1=qdec[:, hd, :], op=ALU.mult),
                           reads=[Qk, "qdec"], writes=[("qd", d, c)])
                ps, pk = rA.next()
                p.pe(lambda e, ps=ps, c=c: e.matmul(ps[:, 0:128], lhsT=K_[:, c * 128:(c + 1) * 128], rhs=Q_[:, c * 128:(c + 1) * 128], start=True, stop=True),
                     reads=[Kk, Qk], writes=[pk])
                for d in range(2):
                    hd = d * 8 + h
                    p.dve(lambda e, ps=ps, c=c, d=d, hd=hd: e.tensor_tensor(out=attm_all[d][:, c, :], in0=ps[:, 0:128], in1=Dm[:, hd, :], op=ALU.mult),
                          reads=[pk, "Dm"], writes=[("attm", d, c)])
            for ci in range(18):
              for d in range(2):
                hd = d * 8 + h
                c = orders[d][ci]
                s32, sbf = S32[d], Sbf[d]
                s32k, sbfk = ("S32", d), ("Sbf", d)
                if True:
                    need_o = not (last and c < 2)
                    if need_o:
                        po, pok = rB.next()
                        p.pe(lambda e, po=po, c=c, ci=ci, d=d: e.matmul(po[:, 0:128], lhsT=attm_all[d][:, c, :], rhs=V_[:, c, :], start=True, stop=(ci == 0)), reads=[("attm", d, c), Vk], writes=[pok])
                        if ci > 0:
                            p.pe(lambda e, po=po, c=c, d=d, sbf=sbf: e.matmul(po[:, 0:128], lhsT=qd_all[d][:, c, :], rhs=sbf[:], start=False, stop=True), reads=[("qd", d, c), sbfk], writes=[pok])
                        if d == 0:
                            p.act(lambda e, po=po, c=c: e.activation(out=oacc[:, c, :], in_=po[:, 0:128], func=AF.Copy), reads=[pok], writes=["oacc"])
                        else:
                            p.act(lambda e, po=po, c=c: e.activation(out=oaccB[:, c, :], in_=po[:, 0:128], func=AF.Copy), reads=[pok], writes=["oaccB"])
                    if ci < 17:
                        pkv, pkvk = rA.next()
                        p.pe(lambda e, pkv=pkv, c=c, d=d: e.matmul(pkv[:, 0:128], lhsT=Kd[d][:, c, :], rhs=V_[:, c, :], start=True, stop=True), reads=[("Kd", d), Vk], writes=[pkvk])
                        if ci == 0:
                            p.dve(lambda e, pkv=pkv, s32=s32: e.tensor_copy(out=s32[:], in_=pkv[:, 0:128]), reads=[pkvk], writes=[s32k])
                        else:
                            p.dve(lambda e, pkv=pkv, s32=s32, hd=hd: e.scalar_tensor_tensor(out=s32[:], in0=s32[:], scalar=cdec[:, hd:hd + 1], in1=pkv[:, 0:128], op0=ALU.mult, op1=ALU.add),
                                  reads=[pkvk, s32k, "cdec"], writes=[s32k])
                        p.act(lambda e, s32=s32, sbf=sbf: e.activation(out=sbf[:], in_=s32[:], func=AF.Copy), reads=[s32k], writes=[sbfk])
            c_lo = 2 if last else 0
            ncn = 18 - c_lo
            ov = oacc[:, c_lo:18, :]
            p.dve(lambda e, ov=ov: e.tensor_tensor(out=ov, in0=ov, in1=oaccB[:, c_lo:18, :], op=ALU.add), reads=["oacc", "oaccB"], writes=["oacc"])
            p.dve(lambda e, ov=ov: e.tensor_reduce(out=gst[:, 0, c_lo:18], in_=ov, axis=AX.X, op=ALU.add), reads=["oacc"], writes=["gst"])
            p.act(lambda e, ov=ov: e.activation(out=osq[:, c_lo:18, :], in_=ov, func=AF.Square), reads=["oacc"], writes=["osq"])
            p.dve(lambda e: e.tensor_reduce(out=gst[:, 1, c_lo:18], in_=osq[:, c_lo:18, :], axis=AX.X, op=ALU.add), reads=["osq"], writes=["gst"])
            p.dve(lambda e: e.tensor_scalar(out=gst[:, 0, :], in0=gst[:, 0, :], scalar1=1.0 / HD, scalar2=None, op0=ALU.mult), reads=["gst"], writes=["gst"])
            p.dve(lambda e: e.tensor_tensor(out=gst[:, 2, :], in0=gst[:, 0, :], in1=gst[:, 0, :], op=ALU.mult), reads=["gst"], writes=["gst"])
            p.dve(lambda e: e.scalar_tensor_tensor(out=gst[:, 1, :], in0=gst[:, 1, :], scalar=1.0 / HD, in1=gst[:, 2, :], op0=ALU.mult, op1=ALU.subtract), reads=["gst"], writes=["gst"])
            p.act(lambda e: e.activation(out=gst[:, 1, c_lo:18], in_=gst[:, 1, c_lo:18], func=AF.Sqrt, bias=LEPS, scale=1.0), reads=["gst"], writes=["gst"])
            p.dve(lambda e: e.reciprocal(out=gst[:, 1, c_lo:18], in_=gst[:, 1, c_lo:18]), reads=["gst"], writes=["gst"])
            p.dve(lambda e, ov=ov: e.tensor_tensor(out=ov, in0=ov, in1=gst[:, 0, c_lo:18].unsqueeze(2).to_broadcast([128, ncn, 128]), op=ALU.subtract), reads=["oacc", "gst"], writes=["oacc"])
            p.dve(lambda e, ov=ov: e.tensor_tensor(out=onb[:, c_lo:18, :], in0=ov, in1=gst[:, 1, c_lo:18].unsqueeze(2).to_broadcast([128, ncn, 128]), op=ALU.mult), reads=["oacc", "gst"], writes=["onb"])
            for c4 in range(c_lo, 18, 4):
                nn = min(4, 18 - c4)
                pt, ptk = rT.next()
                for cc in range(nn):
                    p.pe(lambda e, pt=pt, cc=cc, c4=c4: e.transpose(out=pt[:, cc, :], in_=onb[:, c4 + cc, :], identity=ident[:]), reads=["onb", "ident"], writes=[ptk])
                w = nn * 128
                p.dve(lambda e, pt=pt, w=w, nn=nn: e.tensor_scalar(out=yf[:, 0:w], in0=pt[:, 0:nn, :].rearrange("p a b -> p (a b)"), scalar1=small[:, 24 + h:25 + h], scalar2=small[:, 32 + h:33 + h], op0=ALU.mult, op1=ALU.add),
                      reads=[ptk, "small"], writes=["yf"])
                y, yk = rY.next()
                p.pool(lambda e, y=y, w=w, c4=c4: e.tensor_tensor(out=y[:, 0:w], in0=yf[:, 0:w], in1=S_[:, c4 * 128:c4 * 128 + w], op=ALU.mult), reads=["yf", Sk], writes=[yk])
                p.dma("sp", "yst%d" % yk[1], yT[2 * GW + h * 128:2 * GW + (h + 1) * 128, c4 * 128:c4 * 128 + w], y[:, 0:w], reads=[yk], writes=["yT"])
        p.barrier()
        p.release()
        if stop_after == "ret":
            break

        for (tok0, T) in groups:
            p.mark()
            yTg = p.sb([128, 32, T], BF16, "yTg")
            p.dma("sp", "ytg", yTg[:], yT[:, tok0:tok0 + T].rearrange("(k p) t -> p k t", p=128), reads=["yT"], writes=["yTg"])
            gtb = p.sb([128, D], F32, "gtb")
            wbuf = [p.sb([128, 32, 512], BF16, f"wo{i}") for i in range(2)]
            rW = Rot(wbuf, "wo")
            xs = [p.sb([128, 512], F32, f"xs{i}") for i in range(3)]
            rXs = Rot(xs, "xs")
            ts = [p.sb([128, 512], F32, f"ts{i}") for i in range(2)]
            rTs = Rot(ts, "ts")
            tiles = [tt for tt in range(T // 128) if not (last and tok0 + tt * 128 < CTX)]
            cur_row = None
            for nb in range(8):
                wb, wk = rW.next()
                p.dma("pool", "w%d" % wk[1], wb[:], w_out[l, :, nb * 512:(nb + 1) * 512].rearrange("(k p) n -> p k n", p=128), writes=[wk])
                for tt in tiles:
                    g0 = tok0 + tt * 128
                    is_ctx = g0 < CTX
                    row = 1 if is_ctx else 0
                    if row != cur_row:
                        p.dma("sp", "modld", gtb[:], mod[l, row, 2 * D:3 * D].partition_broadcast(128), reads=["mod"], writes=["gtb"])
                        cur_row = row
                    ps, pk = rA.next()
                    for k in range(32):
                        p.pe(lambda e, ps=ps, k=k, tt=tt, wb=wb: e.matmul(ps[:], lhsT=yTg[:, k, tt * 128:(tt + 1) * 128], rhs=wb[:, k, :], start=(k == 0), stop=(k == 31)),
                             reads=["yTg", wk], writes=[pk])
                    if l == 0:
                        src = ctx_in[g0:g0 + 128, nb * 512:(nb + 1) * 512] if is_ctx else x_in[g0 - CTX:g0 - CTX + 128, nb * 512:(nb + 1) * 512]
                    else:
                        src = xres[g0:g0 + 128, nb * 512:(nb + 1) * 512]
                    x_, xk = rXs.next()
                    p.dma("act", "xs%d" % xk[1], x_[:], src, reads=["xres_r"], writes=[xk])
                    t_, tk = rTs.next()
                    p.dve(lambda e, t_=t_, ps=ps, nb=nb: e.tensor_tensor(out=t_[:], in0=ps[:], in1=gtb[:, nb * 512:(nb + 1) * 512], op=ALU.mult), reads=[pk, "gtb"], writes=[tk])
                    p.dve(lambda e, t_=t_, x_=x_: e.tensor_tensor(out=x_[:], in0=x_[:], in1=t_[:], op=ALU.add), reads=[tk, xk], writes=[xk])
                    p.dma("sp", "xst%d" % xk[1], xres[g0:g0 + 128, nb * 512:(nb + 1) * 512], x_[:], reads=[xk], writes=["xres_w"])
            p.barrier()
            p.release()
        if stop_after == "l0":
            break

    if stop_after is None:
        p.mark()
        fgb = p.sb([128, D], F32, "fgb")
        p.dma("sp", "const", fgb[:], final_g.partition_broadcast(128), writes=["fgb"])
        xt = [p.sb([128, D], F32, f"fxt{i}") for i in range(3)]
        rX = Rot(xt, "fxt")
        junk = p.sb([128, D], BF16, "fjunk")
        st = p.sb([128, 8], F32, "fst")
        for tt in range(SEQ // 128):
            xb_, xk = rX.next()
            p.dma("pool", "x%d" % xk[1], xb_[:], xres[CTX + tt * 128:CTX + (tt + 1) * 128, :], reads=["xres_w"], writes=[xk])
            p.act(lambda e, xb_=xb_: e.activation(out=junk[:], in_=xb_[:], func=AF.Square, accum_out=st[:, 0:1]), reads=[xk], writes=["junk", "st0"])
            p.act(lambda e: e.activation(out=st[:, 1:2], in_=st[:, 0:1], func=AF.Sqrt, bias=NEPS, scale=1.0 / D), reads=["st0"], writes=["st1"])
            p.dve(lambda e: e.reciprocal(out=st[:, 2:3], in_=st[:, 1:2]), reads=["st1"], writes=["st2"])
            p.dve(lambda e, xb_=xb_: e.scalar_tensor_tensor(out=xb_[:], in0=xb_[:], scalar=st[:, 2:3], in1=fgb[:], op0=ALU.mult, op1=ALU.mult),
                  reads=[xk, "st2", "fgb"], writes=[xk])
            p.dma("sp", "ost%d" % xk[1], out[tt * 128:(tt + 1) * 128, :], xb_[:], reads=[xk], writes=["out"])
        p.release()
    p.barrier()
    stats = p.emit()
    nc_ctx.__exit__(None, None, None)
    return nc, cs, stats, dict(mod=mod, xres=xres, uT=uT, sgT=sgT, qT=qT, kT=kT, vS=vS, yT=yT)


def make_in_maps(inputs, cs):
    bf = ml_dtypes.bfloat16
    f = lambda a: np.ascontiguousarray(np.asarray(a, dtype=np.float32))
    shared = {
        "ada_w": f(inputs["ada_w"]), "ada_b": f(inputs["ada_b"]), "norm_g": f(inputs["norm_g"]),
        "w_in": f(inputs["w_in"]), "conv_w": f(inputs["conv_w"]), "conv_b": f(inputs["conv_b"]),
        "conv_ln_g": f(inputs["conv_ln_g"]), "conv_ln_b": f(inputs["conv_ln_b"]),
        "gqa_qn_g": f(inputs["gqa_qn_g"]), "gqa_kn_g": f(inputs["gqa_kn_g"]),
        "ret_dec": f(np.concatenate([inputs["ret_decay_fwd"], inputs["ret_decay_bwd"]], axis=1)),
        "ret_gn_g": f(inputs["ret_gn_g"]), "ret_gn_b": f(inputs["ret_gn_b"]),
        "na_bT": np.ascontiguousarray(np.transpose(np.asarray(inputs["na_bias"], np.float32)[:, :, ::-1, :], (0, 3, 1, 2))),
        "w_out": f(inputs["w_out"]), "final_g": f(inputs["final_g"]),
    }
    for k, v in cs.items():
        shared["k_" + k] = v
    maps = []
    for core in range(8):
        b = core // 2
        m = dict(shared)
        m["x"] = f(inputs["x"][b])
        m["ctx"] = f(inputs["ctx"][b])
        m["c2"] = f(np.stack([inputs["c"][b], inputs["c_ctx"]], axis=0))
        maps.append(m)
    return maps


_CACHE = {}


def kernel(**inputs):
    if "prog" not in _CACHE:
        _CACHE["prog"] = build_program()
    nc, cs, stats, _ = _CACHE["prog"]
    maps = make_in_maps(inputs, cs)
    res = run_bass_kernel_spmd(nc, maps, core_ids=list(range(8)))
    outs = [res.results[2 * b]["out"] for b in range(4)]
    return np.stack(outs, axis=0).astype(np.float32)
```

```python
import contextlib
import numpy as np
import ml_dtypes
import concourse.bass as bass
import concourse.mybir as mybir
from concourse.bass_utils import run_bass_kernel_spmd

F32 = mybir.dt.float32
BF16 = mybir.dt.bfloat16
AF = mybir.ActivationFunctionType
ALU = mybir.AluOpType
AX = mybir.AxisListType

D = 4096
SEQ = 2048
CTX = 256
TALL = SEQ + CTX
DEPTH = 2
GW = 1024
HD = 128
PT = 13824
GRID_W = 64
SCALE = HD ** -0.5
NEPS = 1e-6
LEPS = 1e-5
SAME_ENG_SYNC = True

OFF = dict(A_a=0, A_g=1024, A_gate=2048, B_q=3072, B_k=4096, B_v=4352, B_gate=4608,
           C_q=5632, C_k=6656, C_v=7680, C_gate=8704, D_q=9728, D_k=10752, D_v=11776, D_gate=12800)


class _Op:
    __slots__ = ("eng", "fn", "deps", "dsem", "cnt", "sig", "idx")


class _Rec:
    def __init__(self):
        self.call = None

    def __getattr__(self, name):
        def f(*a, **k):
            self.call = (name, a, k)
            return self
        return f


class Prog:
    def __init__(self, nc):
        self.nc = nc
        self.ops = []
        self.state = {}
        self.dcnt = {}
        self.sb_off = 20480
        self.sb_marks = []
        self.ntens = 0
        self.debug = False
        self.last_dma = {}

    def sb(self, shape, dtype, name=None):
        esz = 4 if dtype == F32 else 2
        n = 1
        for s in shape[1:]:
            n *= s
        nbytes = (n * esz + 31) // 32 * 32
        self.ntens += 1
        t = self.nc.alloc_sbuf_tensor_at(f"{name or 't'}_{self.ntens}", list(shape), dtype, offset=self.sb_off)
        self.sb_off += nbytes
        assert self.sb_off <= 229376, f"SBUF overflow {self.sb_off} at {name}"
        return t

    def mark(self):
        self.sb_marks.append(self.sb_off)

    def release(self):
        self.sb_off = self.sb_marks.pop()

    def _add(self, eng, fn, reads, writes, dsem=None):
        op = _Op()
        if fn is not None:
            rec = _Rec()
            fn(rec)
            c = rec.call
            fn = lambda e, c=c: getattr(e, c[0])(*c[1], **c[2])
        op.eng = eng; op.fn = fn; op.dsem = dsem; op.idx = len(self.ops); op.sig = None
        deps = []

        def push(lst, o):
            lst[:] = [w for w in lst if not ((w.dsem is None and o.dsem is None and w.eng == o.eng)
                                             or (w.dsem is not None and w.dsem == o.dsem))]
            lst.append(o)
        for k in reads:
            st = self.state.setdefault(k, [[], [], []])
            deps.extend(st[1])
            push(st[2], op)
        for k in writes:
            st = self.state.setdefault(k, [[], [], []])
            if st[2]:
                st[0] = st[2]; old_w = st[1]; st[1] = [op]; st[2] = []
                deps.extend(st[0]); deps.extend(old_w)
            else:
                deps.extend(st[0]); deps.extend(st[1])
                push(st[1], op)
        if dsem is not None:
            prev = self.last_dma.get(dsem)
            if prev is not None:
                deps.append(prev)
            self.last_dma[dsem] = op
        op.deps = [d for d in deps if d is not op]
        if dsem is not None:
            self.dcnt[dsem] = self.dcnt.get(dsem, 0) + 16
            op.cnt = self.dcnt[dsem]
        self.ops.append(op)
        return op

    def pe(self, fn, reads=(), writes=()):
        return self._add("pe", fn, reads, writes)

    def act(self, fn, reads=(), writes=()):
        return self._add("act", fn, reads, writes)

    def dve(self, fn, reads=(), writes=()):
        return self._add("dve", fn, reads, writes)

    def pool(self, fn, reads=(), writes=()):
        return self._add("pool", fn, reads, writes)

    def dma(self, q, dsem, out, in_, reads=(), writes=(), **kw):
        return self._add(q, lambda e: e.dma_start(out=out, in_=in_, **kw), reads, writes, dsem=dsem)

    def barrier(self):
        alld = {}
        for st in self.state.values():
            for lst in st:
                for o in lst:
                    alld[o.idx] = o
        alld = list(alld.values())
        for eng in ("pe", "act", "dve", "pool", "sp"):
            op = self._add(eng, None, (), ())
            op.deps = list(alld)
        self.state = {}

    def wait_all(self, eng, keys):
        return self._add(eng, None, list(keys), ())

    def emit(self):
        nc = self.nc
        ops = self.ops
        need = set()
        for op in ops:
            for d in op.deps:
                if d.dsem is None:
                    if d.eng == op.eng and op.dsem is None and (d.eng == "pe" or not SAME_ENG_SYNC):
                        continue
                    need.add(d.idx)
        cnt = {e: 0 for e in ("pe", "act", "dve", "pool", "sp")}
        for op in ops:
            if op.dsem is None:
                if op.fn is not None and op.idx in need:
                    cnt[op.eng] += 1
                    op.sig = True
                    op.cnt = cnt[op.eng]
                else:
                    op.sig = False
                    op.cnt = None
        dsems = sorted(self.dcnt.keys())
        import bisect
        dlist = {d: ([], []) for d in dsems}
        for op in ops:
            if op.dsem is not None:
                dlist[op.dsem][0].append(op.idx)
                dlist[op.dsem][1].append(op.cnt)

        def dma_wait_val(d, idx):
            ii, cc = dlist[d]
            j = bisect.bisect_left(ii, idx)
            return cc[j - 1] if j > 0 else 0
        with contextlib.ExitStack() as es:
            esem = {e: es.enter_context(nc.semaphore(f"s_{e}")) for e in ("pe", "act", "dve", "pool")}
            dsem = {d: es.enter_context(nc.semaphore(f"d_{d}")) for d in dsems}
            block = es.enter_context(nc.Block())

            def stream(engname):
                def body(eng):
                    waited = {}
                    for op in ops:
                        if op.eng != engname:
                            continue
                        w = {}
                        for d in op.deps:
                            if d.dsem is not None:
                                key = ("d", d.dsem); val = max(d.cnt, dma_wait_val(d.dsem, op.idx))
                            else:
                                if d.fn is None:
                                    continue
                                if d.eng == engname and op.dsem is None and (engname == "pe" or not SAME_ENG_SYNC):
                                    continue
                                key = ("e", d.eng); val = d.cnt
                            if val > w.get(key, 0):
                                w[key] = val
                        for key, val in w.items():
                            if waited.get(key, 0) >= val:
                                continue
                            waited[key] = val
                            s = dsem[key[1]] if key[0] == "d" else esem[key[1]]
                            eng.wait_ge(s, val)
                        if op.fn is None:
                            continue
                        ins = op.fn(eng)
                        if op.dsem is not None:
                            ins.then_inc(dsem[op.dsem], 16)
                        elif op.sig:
                            ins.then_inc(esem[engname], 1)
                return body

            block.tensor(stream("pe"))
            block.scalar(stream("act"))
            block.vector(stream("dve"))
            block.gpsimd(stream("pool"))
            block.sync(stream("sp"))
        return cnt, {d: self.dcnt[d] for d in dsems}


class Rot:
    def __init__(self, bufs, name):
        self.bufs = bufs
        self.name = name
        self.i = 0

    def next(self):
        b = self.bufs[self.i % len(self.bufs)]
        k = (self.name, self.i % len(self.bufs))
        self.i += 1
        return b, k


def _r_start(r, rows=32, kr=8):
    return min(max(r - kr // 2, 0), rows - kr)


NA_PAIRS = []
for _qb in range(4):
    for _j in range(16):
        lo = min(_r_start(r) for r in range(8 * _qb, 8 * _qb + 8))
        hi = max(_r_start(r) + 7 for r in range(8 * _qb, 8 * _qb + 8))
        if 2 * _j + 1 >= lo and 2 * _j <= hi:
            NA_PAIRS.append((_qb, _j))


def host_consts():
    c = {}
    bf = ml_dtypes.bfloat16
    c["ident"] = np.eye(128, dtype=np.float32).astype(bf)
    c["identf"] = np.eye(128, dtype=np.float32)
    c["onesf"] = np.ones((128, 128), np.float32)
    c["onesb"] = np.ones((128, 128), np.float32).astype(bf)
    t = np.arange(SEQ)
    row = (t // GRID_W).astype(np.float32)
    col = (t % GRID_W).astype(np.float32)
    half = HD // 2
    inv_freq = (np.float32(10000.0) ** (-np.arange(0, half, 2, dtype=np.float32) / np.float32(half))).astype(np.float32)
    ang_r = row[:, None] * inv_freq[None, :]
    ang_c = col[:, None] * inv_freq[None, :]
    ang = np.concatenate([ang_r, ang_r, ang_c, ang_c], axis=-1).astype(np.float32)
    cosT = np.ones((128, TALL), np.float32)
    sinT = np.zeros((128, TALL), np.float32)
    cosT[:, CTX:] = np.cos(ang).T
    sinT[:, CTX:] = np.sin(ang).T
    c["cosT"] = cosT
    c["sinT"] = sinT
    R = np.zeros((128, 128), np.float32)
    for i in range(32):
        R[i, 32 + i] = -1.0
        R[32 + i, i] = 1.0
        R[64 + i, 96 + i] = -1.0
        R[96 + i, 64 + i] = 1.0
    c["permT"] = R.T.copy().astype(bf)
    a = np.arange(128, dtype=np.float32)
    diff = a[None, :] - a[:, None]
    c["ret_tabs"] = np.stack([
        diff, -diff,
        (diff >= 0).astype(np.float32), (diff <= 0).astype(np.float32),
        np.broadcast_to(a[None, :] + 1.0, (128, 128)), np.broadcast_to(128.0 - a[None, :], (128, 128)),
    ], axis=1).astype(np.float32)
    c["ret_pidx"] = np.stack([127.0 - a, a, np.full(128, 128.0, np.float32)], axis=1).astype(np.float32)
    rm = np.zeros((128, len(NA_PAIRS), 512), np.float32)
    for pi, (qb, j) in enumerate(NA_PAIRS):
        for i in range(2):
            for jj in range(8):
                kr = 2 * j + i
                qr = 8 * qb + jj
                rs = _r_start(qr)
                if rs <= kr <= rs + 7:
                    rm[64 * i:64 * i + 64, pi, 64 * jj:64 * jj + 64] = 1.0
    c["na_rmask"] = rm.astype(bf)
    cv = np.zeros((64, 64), np.float32)
    for qc in range(64):
        cs = min(max(qc - 8, 0), 48)
        cv[cs:cs + 16, qc] = 1.0
    c["na_cv"] = np.concatenate([cv, cv], 0)
    z = np.zeros((31, 64, 128), np.float32)
    for qc in range(64):
        for p in range(128):
            j = (p % 64) - qc + 15
            if 0 <= j < 31:
                z[j, qc, p] = 1.0
    c["na_z"] = z.astype(bf)
    return c


CONST_SPECS = None


def build_program(stop_after=None, dbg=False):
    nc = bass.Bass("TRN2", target_bir_lowering=False)
    p = Prog(nc)

    def din(name, shape, dt=F32):
        return nc.dram_tensor(name, list(shape), dt, kind="ExternalInput").ap()

    def dscr(name, shape, dt):
        if dbg:
            return nc.dram_tensor(name, list(shape), dt, kind="ExternalOutput").ap()
        return nc.dram_tensor(name, list(shape), dt).ap()

    x_in = din("x", [SEQ, D])
    ctx_in = din("ctx", [CTX, D])
    c2_in = din("c2", [2, D])
    ada_w = din("ada_w", [DEPTH, D, 3 * D])
    ada_b = din("ada_b", [DEPTH, 3 * D])
    norm_g = din("norm_g", [DEPTH, D])
    w_in = din("w_in", [DEPTH, D, PT])
    conv_w = din("conv_w", [DEPTH, 31, GW])
    conv_b = din("conv_b", [DEPTH, GW])
    conv_ln_g = din("conv_ln_g", [DEPTH, GW])
    conv_ln_b = din("conv_ln_b", [DEPTH, GW])
    gqa_qn_g = din("gqa_qn_g", [DEPTH, HD])
    gqa_kn_g = din("gqa_kn_g", [DEPTH, HD])
    ret_dec = din("ret_dec", [DEPTH, 16])
    ret_gn_g = din("ret_gn_g", [DEPTH, GW])
    ret_gn_b = din("ret_gn_b", [DEPTH, GW])
    na_bT = din("na_bT", [DEPTH, 31, 8, 15])
    w_out = din("w_out", [DEPTH, D, D])
    final_g = din("final_g", [D])
    hc = {}
    cs = host_consts()
    for k, v in cs.items():
        hc[k] = din("k_" + k, v.shape, BF16 if v.dtype == ml_dtypes.bfloat16 else F32)
    out = nc.dram_tensor("out", [SEQ, D], F32, kind="ExternalOutput").ap()

    mod = dscr("mod", [DEPTH, 2, 3 * D], F32)
    xres = dscr("xres", [TALL, D], F32)
    uT = dscr("uT", [GW, TALL], BF16)
    sgT = dscr("sgT", [D, TALL], BF16)
    qT = {m: dscr("qT" + m, [GW, TALL], BF16) for m in "BCD"}
    kT = {"B": dscr("kTB", [256, TALL], BF16), "C": dscr("kTC", [GW, TALL], BF16), "D": dscr("kTD", [GW, TALL], BF16)}
    vS = {"B": dscr("vB", [TALL, 256], BF16), "C": dscr("vC", [TALL, GW], BF16), "D": dscr("vD", [TALL, GW], BF16)}
    yT = dscr("yT", [D, TALL], BF16)
    dbg_outs = {}

    psA = [nc.alloc_psum_tensor(f"psA{i}", [128, 512], F32) for i in range(4)]
    psB = [nc.alloc_psum_tensor(f"psB{i}", [128, 512], F32) for i in range(2)]
    psT = [nc.alloc_psum_tensor(f"psT{i}", [128, 8, 128], BF16) for i in range(2)]
    rA = Rot(psA, "psA")
    rB = Rot(psB, "psB")
    rT = Rot(psT, "psT")

    ident = p.sb([128, 128], BF16, "ident")
    identf = p.sb([128, 128], F32, "identf")
    onesf = p.sb([128, 128], F32, "onesf")
    onesb = p.sb([128, 128], BF16, "onesb")
    permT = p.sb([128, 128], BF16, "permT")
    for t, n in ((ident, "ident"), (identf, "identf"), (onesf, "onesf"), (onesb, "onesb"), (permT, "permT")):
        p.dma("sp", "const", t[:], hc[n], writes=[n])
    small = p.sb([128, 64], F32, "small")
    nc_ctx = nc.allow_non_contiguous_dma(reason="small per-channel vectors")
    nc_ctx.__enter__()

    def wload(q, dsem, dst, src, wkey):
        p.dma(q, dsem, dst, src, writes=[wkey])

    p.mark()
    cT = p.sb([128, 2, 32], F32, "cT")
    scT = p.sb([128, 2, 32], BF16, "scT")
    modsb = p.sb([2, 3 * D], F32, "modsb")
    tmp2 = p.sb([2, 3 * D], F32, "tmp2")
    wbuf = [p.sb([128, 32, 256], BF16, f"adaw{i}") for i in range(3)]
    rW = Rot(wbuf, "adaw")
    for r in range(2):
        p.dma("sp", "c2", cT[:, r, :], c2_in[r].rearrange("(k p) -> p k", p=128), writes=["cT"])
    p.act(lambda e: e.activation(out=scT[:], in_=cT[:], func=AF.Silu), reads=["cT"], writes=["scT"])
    for l in range(DEPTH):
        for nb in range(3 * D // 256):
            wb, wk = rW.next()
            p.dma("pool", "w%d" % wk[1], wb[:], ada_w[l, :, nb * 256:(nb + 1) * 256].rearrange("(k p) n -> p k n", p=128), writes=[wk])
            ps, pk = rA.next()
            for k in range(32):
                p.pe(lambda e, ps=ps, wb=wb, k=k: e.matmul(ps[0:2, 0:256], lhsT=scT[:, :, k], rhs=wb[:, k, :], start=(k == 0), stop=(k == 31)),
                     reads=["scT", wk], writes=[pk])
            p.dve(lambda e, ps=ps, nb=nb: e.tensor_copy(out=modsb[:, nb * 256:(nb + 1) * 256], in_=ps[0:2, 0:256]), reads=[pk], writes=["modsb"])
        p.dma("sp", "c2", tmp2[:], ada_b[l:l + 1, :].broadcast_to([2, 3 * D]) if False else ada_b[l].partition_broadcast(2), writes=["tmp2"])
        p.dve(lambda e: e.tensor_tensor(out=modsb[:], in0=modsb[:], in1=tmp2[:], op=ALU.add), reads=["modsb", "tmp2"], writes=["modsb"])
        p.dma("sp", "c2", tmp2[:, 0:D], norm_g[l].partition_broadcast(2), reads=[], writes=["tmp2"])
        p.dve(lambda e: e.scalar_tensor_tensor(out=modsb[:, D:2 * D], in0=modsb[:, D:2 * D], scalar=1.0, in1=tmp2[:, 0:D], op0=ALU.add, op1=ALU.mult),
              reads=["modsb", "tmp2"], writes=["modsb"])
        p.dma("sp", "modst", mod[l], modsb[:], reads=["modsb"], writes=["mod"])
    p.barrier()
    p.release()

    def load_small(l):
        def col(dst_c, vec, nb):
            p.dma("sp", "small", small[:, dst_c:dst_c + nb], vec.rearrange("(b p) -> p b", p=128), writes=["small"])
        col(0, conv_b[l], 8); col(8, conv_ln_g[l], 8); col(16, conv_ln_b[l], 8)
        col(24, ret_gn_g[l], 8); col(32, ret_gn_b[l], 8)
        col(40, gqa_qn_g[l], 1); col(41, gqa_kn_g[l], 1)

    groups = [(0, 1280), (1280, 1024)]

    for l in range(DEPTH):
        last = (l == DEPTH - 1)
        load_small(l)
        for (tok0, T) in groups:
            p.mark()
            hlT = p.sb([128, 32, T], BF16, "hlT")
            p.mark()
            xt = [p.sb([128, D], F32, f"xt{i}") for i in range(2 if tok0 == 0 else 3)]
            rX = Rot(xt, "xt")
            junk = p.sb([128, D], BF16, "junk")
            hlbs = [p.sb([128, D], BF16, f"hlb{i}") for i in range(2)]
            rH = Rot(hlbs, "hlb")
            modrows = {}
            for row in ((1, 0) if tok0 == 0 else (0,)):
                g_ = p.sb([128, D], F32, f"gsb{row}")
                s_ = p.sb([128, D], F32, f"shb{row}")
                p.dma("sp", "modg%d" % row, g_[:], mod[l, row, D:2 * D].partition_broadcast(128), reads=["mod"], writes=[("gsb", row)])
                p.dma("sp", "mods%d" % row, s_[:], mod[l, row, 0:D].partition_broadcast(128), reads=["mod"], writes=[("shb", row)])
                modrows[row] = (g_, s_)
            st = p.sb([128, 8], F32, "st")
            cur_row = [None]

            def norm_a(tt):
                g0 = tok0 + tt * 128
                is_ctx = g0 < CTX
                row = 1 if is_ctx else 0
                gsb, shb = modrows[row]
                if l == 0:
                    src = ctx_in[g0:g0 + 128, :] if is_ctx else x_in[g0 - CTX:g0 - CTX + 128, :]
                else:
                    src = xres[g0:g0 + 128, :]
                xb_, xk = rX.next()
                p.dma("sp", "x%d" % xk[1], xb_[:], src, reads=["xres"], writes=[xk])
                p.act(lambda e: e.activation(out=junk[:], in_=xb_[:], func=AF.Square, accum_out=st[:, 0:1]), reads=[xk], writes=["junk", "st0"])
                p.act(lambda e: e.activation(out=st[:, 1:2], in_=st[:, 0:1], func=AF.Sqrt, bias=NEPS, scale=1.0 / D), reads=["st0"], writes=["st1"])
                p.dve(lambda e: e.reciprocal(out=st[:, 2:3], in_=st[:, 1:2]), reads=["st1"], writes=["st2"])
                p.dve(lambda e: e.scalar_tensor_tensor(out=xb_[:], in0=xb_[:], scalar=st[:, 2:3], in1=gsb[:], op0=ALU.mult, op1=ALU.mult),
                      reads=[xk, "st2", ("gsb", row)], writes=[xk])
                hlb, hlbk = rH.next()
                p.pool(lambda e: e.tensor_tensor(out=hlb[:], in0=xb_[:], in1=shb[:], op=ALU.add), reads=[xk, ("shb", row)], writes=[hlbk])
                return hlb, hlbk

            def norm_b(tt, hlb, hlbk):
                for k8 in range(4):
                    pt, ptk = rT.next()
                    for kk in range(8):
                        k = k8 * 8 + kk
                        p.pe(lambda e, pt=pt, kk=kk, k=k: e.transpose(out=pt[:, kk, :], in_=hlb[:, k * 128:(k + 1) * 128], identity=ident[:]),
                             reads=[hlbk, "ident"], writes=[ptk])
                    p.dve(lambda e, pt=pt, k8=k8: e.tensor_copy(out=hlT[:, k8 * 8:(k8 + 1) * 8, tt * 128:(tt + 1) * 128], in_=pt[:]),
                          reads=[ptk], writes=["hlT"])
            ntile = T // 128
            pend = norm_a(0)
            for tt in range(ntile):
                nxt_ = norm_a(tt + 1) if tt + 1 < ntile else None
                norm_b(tt, *pend)
                pend = nxt_
            p.barrier()
            p.release()
            if stop_after == "norm":
                break
            p.mark()
            wbuf = [p.sb([128, 32, 256], BF16, f"winw{i}") for i in range(3)]
            rW = Rot(wbuf, "winw")
            cosT = p.sb([128, T], F32, "cosT")
            sinT = p.sb([128, T], F32, "sinT")
            p.dma("sp", "const", cosT[:], hc["cosT"][:, tok0:tok0 + T], writes=["cosT"])
            p.dma("sp", "const", sinT[:], hc["sinT"][:, tok0:tok0 + T], writes=["sinT"])
            ef = [p.sb([128, 512], F32, f"ef{i}") for i in range(10)]
            rE = Rot(ef, "ef")
            eb = [p.sb([128, 512], BF16, f"eb{i}") for i in range(4)]
            rEb = Rot(eb, "eb")
            ob = [p.sb([128, 512], BF16, f"ob{i}") for i in range(4)]
            rO = Rot(ob, "ob")
            chunks = [(c0, min(512, T - c0)) for c0 in range(0, T, 512)]
            if last and tok0 == 0:
                chunks_q = [(c0, min(512, T - c0)) for c0 in range(CTX, T, 512)]
            else:
                chunks_q = chunks
            pending = []

            def flush():
                cur = pending[:]
                del pending[:]
                for cb_ in cur:
                    cb_()

            def load_w(pieces):
                wb, wk = rW.next()
                o = 0
                for (c0, wd) in pieces:
                    p.dma("pool", "w%d" % wk[1], wb[:, :, o:o + wd], w_in[l, :, c0:c0 + wd].rearrange("(k p) n -> p k n", p=128), writes=[wk])
                    o += wd
                return wb, wk

            def mm_fm(wb, wk, j, c0, n):
                ps, pk = rA.next()
                for k in range(32):
                    p.pe(lambda e, ps=ps, k=k: e.matmul(ps[:, 0:n], lhsT=wb[:, k, j * 128:(j + 1) * 128], rhs=hlT[:, k, c0:c0 + n], start=(k == 0), stop=(k == 31)),
                         reads=["hlT", wk], writes=[pk])
                return ps, pk

            def store_fm(src, sk, dst, r0, c0, n):
                p.dma("sp", "st_%s%d" % sk, dst[r0:r0 + 128, tok0 + c0:tok0 + c0 + n], src[:, 0:n], reads=[sk], writes=["scr"])

            def rope_tail(qn, qnk, c0, n, dst, r0):
                qb, qbk = rEb.next()
                p.dve(lambda e: e.tensor_copy(out=qb[:, 0:n], in_=qn[:, 0:n]), reads=[qnk], writes=[qbk])
                pending.append(lambda: rope_tail2(qn, qnk, qb, qbk, c0, n, dst, r0))

            def rope_tail2(qn, qnk, qb, qbk, c0, n, dst, r0):
                ps2, p2k = rB.next()
                p.pe(lambda e: e.matmul(ps2[:, 0:n], lhsT=permT[:], rhs=qb[:, 0:n], start=True, stop=True), reads=["permT", qbk], writes=[p2k])
                t2, t2k = rE.next()
                p.dve(lambda e: e.tensor_tensor(out=t2[:, 0:n], in0=ps2[:, 0:n], in1=sinT[:, c0:c0 + n], op=ALU.mult), reads=[p2k, "sinT"], writes=[t2k])
                p.dve(lambda e: e.tensor_tensor(out=qn[:, 0:n], in0=qn[:, 0:n], in1=cosT[:, c0:c0 + n], op=ALU.mult), reads=[qnk, "cosT"], writes=[qnk])
                o, ok = rO.next()
                p.dve(lambda e: e.tensor_tensor(out=o[:, 0:n], in0=qn[:, 0:n], in1=t2[:, 0:n], op=ALU.add), reads=[qnk, t2k], writes=[ok])
                store_fm(o, ok, dst, r0, c0, n)

            for jb in range(8):
                wb, wk = load_w([(OFF["A_a"] + jb * 128, 128), (OFF["A_g"] + jb * 128, 128)])
                for (c0, n) in chunks_q:
                    pa, pak = mm_fm(wb, wk, 0, c0, n)
                    pg, pgk = mm_fm(wb, wk, 1, c0, n)
                    sg, sgk = rE.next()
                    p.act(lambda e, sg=sg, pg=pg, n=n: e.activation(out=sg[:, 0:n], in_=pg[:, 0:n], func=AF.Sigmoid), reads=[pgk], writes=[sgk])
                    o, ok = rO.next()
                    p.dve(lambda e, o=o, pa=pa, sg=sg, n=n: e.tensor_tensor(out=o[:, 0:n], in0=pa[:, 0:n], in1=sg[:, 0:n], op=ALU.mult), reads=[pak, sgk], writes=[ok])
                    store_fm(o, ok, uT, jb * 128, c0, n)
            for mi, m in enumerate("ABCD"):
                for jb in range(4):
                    wb, wk = load_w([(OFF[m + "_gate"] + jb * 256, 256)])
                    for j in range(2):
                        for (c0, n) in chunks_q:
                            ps, pk = mm_fm(wb, wk, j, c0, n)
                            o, ok = rO.next()
                            p.act(lambda e, o=o, ps=ps, n=n: e.activation(out=o[:, 0:n], in_=ps[:, 0:n], func=AF.Silu), reads=[pk], writes=[ok])
                            store_fm(o, ok, sgT, mi * GW + jb * 256 + j * 128, c0, n)
            fm_list = [("B", "q", "normrope", 1.0, 1024, 40), ("B", "k", "normrope", 1.0, 256, 41),
                       ("C", "q", "rope", 1.0, 1024, None), ("C", "k", "rope", SCALE, 1024, None),
                       ("D", "q", "plain", 1.0, 1024, None), ("D", "k", "plain", 1.0, 1024, None)]
            for (m, which, kind, scl, width, gcol) in fm_list:
                dst = qT[m] if which == "q" else kT[m]
                for jb in range(width // 256):
                    wb, wk = load_w([(OFF[m + "_" + which] + jb * 256, 256)])
                    for j in range(2):
                        r0 = jb * 256 + j * 128
                        for (c0, n) in (chunks_q if which == "q" else chunks):
                            ps, pk = mm_fm(wb, wk, j, c0, n)
                            flush()
                            if kind == "plain":
                                o, ok = rO.next()
                                p.dve(lambda e, o=o, ps=ps, n=n: e.tensor_copy(out=o[:, 0:n], in_=ps[:, 0:n]), reads=[pk], writes=[ok])
                                store_fm(o, ok, dst, r0, c0, n)
                            elif kind == "rope":
                                qn, qnk = rE.next()
                                p.act(lambda e, qn=qn, ps=ps, n=n, scl=scl: e.activation(out=qn[:, 0:n], in_=ps[:, 0:n], func=AF.Copy, scale=scl), reads=[pk], writes=[qnk])
                                rope_tail(qn, qnk, c0, n, dst, r0)
                            else:
                                sq, sqk = rE.next()
                                p.act(lambda e, sq=sq, ps=ps, n=n: e.activation(out=sq[:, 0:n], in_=ps[:, 0:n], func=AF.Square), reads=[pk], writes=[sqk])

                                def stage_b(sq=sq, sqk=sqk, ps=ps, pk=pk, c0=c0, n=n, gcol=gcol, dst=dst, r0=r0):
                                    ps3, p3k = rB.next()
                                    p.pe(lambda e: e.matmul(ps3[:, 0:n], lhsT=onesf[:], rhs=sq[:, 0:n], start=True, stop=True), reads=["onesf", sqk], writes=[p3k])
                                    rs, rsk = rE.next()
                                    p.act(lambda e: e.activation(out=rs[:, 0:n], in_=ps3[:, 0:n], func=AF.Sqrt, bias=NEPS, scale=1.0 / HD), reads=[p3k], writes=[rsk])
                                    p.dve(lambda e: e.reciprocal(out=rs[:, 0:n], in_=rs[:, 0:n]), reads=[rsk], writes=[rsk])
                                    qn, qnk = rE.next()
                                    p.dve(lambda e: e.scalar_tensor_tensor(out=qn[:, 0:n], in0=ps[:, 0:n], scalar=small[:, gcol:gcol + 1], in1=rs[:, 0:n], op0=ALU.mult, op1=ALU.mult),
                                          reads=[pk, rsk, "small"], writes=[qnk])
                                    rope_tail(qn, qnk, c0, n, dst, r0)
                                pending.append(stage_b)
            flush(); flush(); flush()
            for m, width in (("B", 256), ("C", 1024), ("D", 1024)):
                for jb in range(width // 256):
                    wb, wk = load_w([(OFF[m + "_v"] + jb * 256, 256)])
                    for tt in range(T // 128):
                        ps, pk = rA.next()
                        for k in range(32):
                            p.pe(lambda e, ps=ps, k=k, tt=tt, wb=wb: e.matmul(ps[:, 0:256], lhsT=hlT[:, k, tt * 128:(tt + 1) * 128], rhs=wb[:, k, :], start=(k == 0), stop=(k == 31)),
                                 reads=["hlT", wk], writes=[pk])
                        o, ok = rO.next()
                        p.act(lambda e, o=o, ps=ps: e.activation(out=o[:, 0:256], in_=ps[:, 0:256], func=AF.Copy), reads=[pk], writes=[ok])
                        p.dma("sp", "st_%s%d" % ok, vS[m][tok0 + tt * 128:tok0 + (tt + 1) * 128, jb * 256:(jb + 1) * 256], o[:, 0:256], reads=[ok], writes=["scr"])
            p.barrier()
            p.release()
            p.release()
        if stop_after in ("norm", "proj"):
            break

        p.mark()
        cw31 = p.sb([31, GW], F32, "cw31")
        cwT = p.sb([128, 8, 31], F32, "cwT")
        p.dma("sp", "cw", cw31[:], conv_w[l], writes=["cw31"])
        for cb in range(8):
            ps, pk = rA.next()
            p.pe(lambda e, ps=ps, cb=cb: e.transpose(out=ps[:, 0:31], in_=cw31[:, cb * 128:(cb + 1) * 128], identity=identf[0:31, 0:31]), reads=["cw31", "identf"], writes=[pk])
            p.dve(lambda e, ps=ps, cb=cb: e.tensor_copy(out=cwT[:, cb, :], in_=ps[:, 0:31]), reads=[pk], writes=["cwT"])
        dg = p.sb([128, 8 * 31, 128], BF16, "dg")
        for cb in range(8):
            p.dve(lambda e, cb=cb: e.tensor_tensor(out=dg[:, cb * 31:(cb + 1) * 31, :], in0=identf[:].unsqueeze(1).to_broadcast([128, 31, 128]),
                                                   in1=cwT[:, cb, :].unsqueeze(2).to_broadcast([128, 31, 128]), op=ALU.mult),
                  reads=["identf", "cwT"], writes=["dg"])
        ub = [p.sb([128, 512 + 30], BF16, f"ub{i}") for i in range(3)]
        rU = Rot(ub, "ub")
        vb = [p.sb([128, 512], F32, f"vb{i}") for i in range(8)]
        sqb = [p.sb([128, 512], F32, f"sqb{i}") for i in range(2)]
        rSq = Rot(sqb, "sqb")
        mean = p.sb([128, 512], F32, "mean")
        rstd = p.sb([128, 512], F32, "rstd")
        msq = p.sb([128, 512], F32, "msq")
        sgl = [p.sb([128, 512], BF16, f"sgl{i}") for i in range(2)]
        rSg = Rot(sgl, "sgl")
        tb = [p.sb([128, 512], F32, f"tb{i}") for i in range(2)]
        rTb = Rot(tb, "tb")
        yo = [p.sb([128, 512], BF16, f"yo{i}") for i in range(3)]
        rY = Rot(yo, "yo")
        segs = [(CTX, SEQ)] if last else [(0, CTX), (CTX, SEQ)]
        for (s0, slen) in segs:
            for c0 in range(0, slen, 512):
                n = min(512, slen - c0)
                ps_s, pssk = rB.next()
                ps_q, psqk = rB.next()
                for cb in range(8):
                    u, uk = rU.next()
                    lo = max(c0 - 15, 0)
                    hi = min(c0 + n + 15, slen)
                    p.pool(lambda e, u=u: e.memset(u[:], 0.0), writes=[uk])
                    p.dma("sp", "u%d" % uk[1], u[:, lo - (c0 - 15):hi - (c0 - 15)], uT[cb * 128:(cb + 1) * 128, s0 + lo:s0 + hi], reads=["scr"], writes=[uk])
                    v, vk = vb[cb], ("vb", cb)
                    pc, pck = rA.next()
                    for j in range(31):
                        p.pe(lambda e, pc=pc, u=u, cb=cb, n=n, j=j: e.matmul(pc[:, 0:n], lhsT=dg[:, cb * 31 + j, :], rhs=u[:, j:j + n], start=(j == 0), stop=(j == 30)),
                             reads=["dg", uk], writes=[pck])
                    p.dve(lambda e, v=v, pc=pc, cb=cb, n=n: e.tensor_scalar(out=v[:, 0:n], in0=pc[:, 0:n], scalar1=small[:, cb:cb + 1], scalar2=None, op0=ALU.add),
                          reads=[pck, "small"], writes=[vk])
                    sq, sqk = rSq.next()
                    p.act(lambda e, sq=sq, v=v, n=n: e.activation(out=sq[:, 0:n], in_=v[:, 0:n], func=AF.Square), reads=[vk], writes=[sqk])
                    p.pe(lambda e, v=v, cb=cb, n=n, ps_s=ps_s: e.matmul(ps_s[:, 0:n], lhsT=onesf[:], rhs=v[:, 0:n], start=(cb == 0), stop=(cb == 7)), reads=["onesf", vk], writes=[pssk])
                    p.pe(lambda e, sq=sq, cb=cb, n=n, ps_q=ps_q: e.matmul(ps_q[:, 0:n], lhsT=onesf[:], rhs=sq[:, 0:n], start=(cb == 0), stop=(cb == 7)), reads=["onesf", sqk], writes=[psqk])
                p.dve(lambda e, n=n, ps_s=ps_s: e.tensor_scalar(out=mean[:, 0:n], in0=ps_s[:, 0:n], scalar1=1.0 / GW, scalar2=None, op0=ALU.mult), reads=[pssk], writes=["mean"])
                p.dve(lambda e, n=n: e.tensor_tensor(out=msq[:, 0:n], in0=mean[:, 0:n], in1=mean[:, 0:n], op=ALU.mult), reads=["mean"], writes=["msq"])
                p.dve(lambda e, n=n, ps_q=ps_q: e.scalar_tensor_tensor(out=rstd[:, 0:n], in0=ps_q[:, 0:n], scalar=1.0 / GW, in1=msq[:, 0:n], op0=ALU.mult, op1=ALU.subtract),
                      reads=[psqk, "msq"], writes=["rstd"])
                p.act(lambda e, n=n: e.activation(out=rstd[:, 0:n], in_=rstd[:, 0:n], func=AF.Sqrt, bias=LEPS, scale=1.0), reads=["rstd"], writes=["rstd"])
                p.dve(lambda e, n=n: e.reciprocal(out=rstd[:, 0:n], in_=rstd[:, 0:n]), reads=["rstd"], writes=["rstd"])
                for cb in range(8):
                    v, vk = vb[cb], ("vb", cb)
                    sg, sgk = rSg.next()
                    p.dma("sp", "sg%d" % sgk[1], sg[:, 0:n], sgT[cb * 128:(cb + 1) * 128, s0 + c0:s0 + c0 + n], reads=["scr"], writes=[sgk])
                    p.dve(lambda e, v=v, n=n: e.tensor_tensor(out=v[:, 0:n], in0=v[:, 0:n], in1=mean[:, 0:n], op=ALU.subtract), reads=[vk, "mean"], writes=[vk])
                    p.pool(lambda e, v=v, n=n: e.tensor_tensor(out=v[:, 0:n], in0=v[:, 0:n], in1=rstd[:, 0:n], op=ALU.mult), reads=[vk, "rstd"], writes=[vk])
                    t_, tk = rTb.next()
                    p.act(lambda e, t_=t_, v=v, cb=cb, n=n: e.activation(out=t_[:, 0:n], in_=v[:, 0:n], func=AF.Silu, scale=small[:, 8 + cb:9 + cb], bias=small[:, 16 + cb:17 + cb]),
                          reads=[vk, "small"], writes=[tk])
                    y, yk = rY.next()
                    p.dve(lambda e, y=y, t_=t_, sg=sg, n=n: e.tensor_tensor(out=y[:, 0:n], in0=t_[:, 0:n], in1=sg[:, 0:n], op=ALU.mult), reads=[tk, sgk], writes=[yk])
                    p.dma("sp", "yst%d" % yk[1], yT[cb * 128:(cb + 1) * 128, s0 + c0:s0 + c0 + n], y[:, 0:n], reads=[yk], writes=["yT"])
        p.barrier()
        p.release()
        if stop_after == "conv":
            break

        def attention(m):
            p.mark()
            nkv = 2 if m == "B" else 8
            KT = [p.sb([128, TALL], BF16, f"KT{i}") for i in range(2)]
            rK = Rot(KT, "KT")
            VV = [p.sb([128, 18, 128], BF16, f"VV{i}") for i in range(2)]
            rV = Rot(VV, "VV")
            QT = [p.sb([128, TALL], BF16, f"QT{i}") for i in range(2)]
            rQ = Rot(QT, "QT")
            SG = [p.sb([128, TALL], BF16, f"SG{i}") for i in range(2)]
            rS = Rot(SG, "SG")
            pT = [p.sb([128, 512], BF16, f"pT{i}") for i in range(4)]
            rP = Rot(pT, "pT")
            pM = [p.sb([128, 512], BF16, f"pM{i}") for i in range(3)]
            rM = Rot(pM, "pM")
            rden = p.sb([128, 512], F32, "rden")
            of = p.sb([128, 512], F32, "of")
            yo = [p.sb([128, 512], BF16, f"yo{i}") for i in range(3)]
            rY = Rot(yo, "yo")
            mrow = {"B": GW, "D": 3 * GW}[m]
            if m == "D":
                rmask = p.sb([128, len(NA_PAIRS), 512], BF16, "rmask")
                p.dma("sp", "const", rmask[:], hc["na_rmask"], writes=["rmask"])
                cv = p.sb([128, 64], F32, "cv")
                p.dma("sp", "const", cv[:], hc["na_cv"], writes=["cv"])
                zd = p.sb([31, 64, 128], BF16, "zd")
                p.dma("sp", "const", zd[:], hc["na_z"], writes=["zd"])
                nb = p.sb([31, 8, 15], BF16, "nb")
                p.dma("pool", "nbld", nb[:], na_bT[l], writes=["nb"])
                G2 = [p.sb([128, 29 * 64], BF16, f"G2{i}") for i in range(2)]
                for g2 in G2:
                    p.pool(lambda e, g2=g2: e.memset(g2[:], 0.0), writes=[("G2", G2.index(g2))])
                rG = Rot(G2, "G2")
                gex = p.sb([128, 15, 64], F32, "gex")
            kvl = {}
            qsl = {}
            allheads = [(kv, h) for kv in range(nkv) for h in ([kv * 4 + g for g in range(4)] if m == "B" else [kv])]

            def load_kv(kv):
                K_, Kk = rK.next()
                p.dma("sp", "K%d" % Kk[1], K_[:], kT[m][kv * 128:(kv + 1) * 128, :], reads=["scr"], writes=[Kk])
                V_, Vk = rV.next()
                p.dma("sp", "V%d" % Vk[1], V_[:], vS[m][:, kv * 128:(kv + 1) * 128].rearrange("(t p) d -> p t d", p=128), reads=["scr"], writes=[Vk])
                kvl[kv] = (K_, Kk, V_, Vk)

            def load_qs(i):
                h = allheads[i][1]
                Q_, Qk = rQ.next()
                p.dma("sp", "Q%d" % Qk[1], Q_[:], qT[m][h * 128:(h + 1) * 128, :], reads=["scr"], writes=[Qk])
                S_, Sk = rS.next()
                p.dma("sp", "S%d" % Sk[1], S_[:], sgT[mrow + h * 128:mrow + (h + 1) * 128, :], reads=["scr"], writes=[Sk])
                qsl[i] = (Q_, Qk, S_, Sk)
            load_kv(0)
            load_qs(0)
            hidx = 0
            for kv in range(nkv):
                K_, Kk, V_, Vk = kvl[kv]
                if kv + 1 < nkv:
                    load_kv(kv + 1)
                heads = [kv * 4 + g for g in range(4)] if m == "B" else [kv]
                if m == "D":
                    h = kv
                    g2, g2k = rG.next()
                    for half in range(2):
                        e0, e1 = (0, 8) if half == 0 else (8, 15)
                        ps, pk = rB.next()
                        psv = ps[:, 0:(e1 - e0) * 64].rearrange("p (e q) -> p e q", q=64)
                        for qc in range(64):
                            p.pe(lambda e, psv=psv, qc=qc, e0=e0, e1=e1, h=h: e.matmul(psv[:, :, qc], lhsT=zd[:, qc, :], rhs=nb[:, h, e0:e1], start=True, stop=True),
                                 reads=["zd", "nb"], writes=[pk])
                        p.act(lambda e, psv=psv, e0=e0, e1=e1: e.activation(out=gex[:, e0:e1, :], in_=psv, func=AF.Exp), reads=[pk], writes=["gex"])
                    g2v = g2[:].rearrange("p (e q) -> p e q", q=64)
                    p.dve(lambda e, g2v=g2v: e.tensor_tensor(out=g2v[0:64, 7:22, :], in0=gex[0:64, :, :], in1=cv[0:64, :].unsqueeze(1).to_broadcast([64, 15, 64]), op=ALU.mult),
                          reads=["gex", "cv"], writes=[g2k])
                    p.dve(lambda e, g2v=g2v: e.tensor_tensor(out=g2v[64:128, 8:23, :], in0=gex[64:128, :, :], in1=cv[64:128, :].unsqueeze(1).to_broadcast([64, 15, 64]), op=ALU.mult),
                          reads=["gex", "cv"], writes=[g2k])
                for h in heads:
                    Q_, Qk, S_, Sk = qsl[hidx]
                    if hidx + 1 < len(allheads):
                        load_qs(hidx + 1)
                    hidx += 1
                    qchunks = []
                    if not last:
                        qchunks.append((0, 256, [(0, None), (1, None)]))
                    for qb in range(4):
                        if m == "B":
                            kts = [(t, None) for t in range(18)]
                        else:
                            kts = [(0, None), (1, None)] + [(2 + j, NA_PAIRS.index((qb, j))) for j in range(16) if (qb, j) in NA_PAIRS]
                        qchunks.append((CTX + qb * 512, 512, kts))
                    for (q0, n, kts) in qchunks:
                        ps_o, pok = rB.next()
                        ps_d, pdk = rB.next()

                        def score(i):
                            kt, _ = kts[i]
                            ps, pk = rA.next()
                            p.pe(lambda e, ps=ps, kt=kt: e.matmul(ps[:, 0:n], lhsT=K_[:, kt * 128:(kt + 1) * 128], rhs=Q_[:, q0:q0 + n], start=True, stop=True),
                                 reads=[Kk, Qk], writes=[pk])
                            return ps, pk
                        nxtq = [score(0)]
                        if len(kts) > 1:
                            nxtq.append(score(1))
                        for i, (kt, pi) in enumerate(kts):
                            ps, pk = nxtq.pop(0)
                            if i + 2 < len(kts):
                                nxtq.append(score(i + 2))
                            pt, ptk = rP.next()
                            p.act(lambda e, pt=pt, ps=ps: e.activation(out=pt[:, 0:n], in_=ps[:, 0:n], func=AF.Exp, scale=SCALE), reads=[pk], writes=[ptk])
                            if pi is not None:
                                qb_, j_ = NA_PAIRS[pi]
                                e0 = 8 * qb_ - 2 * j_ + 7
                                pm, pmk = rM.next()
                                p.dve(lambda e, pm=pm, pt=pt, e0=e0: e.tensor_tensor(out=pm[:, 0:n], in0=pt[:, 0:n], in1=g2[:, (e0 + 7) * 64:(e0 + 7) * 64 + 512], op=ALU.mult),
                                      reads=[ptk, g2k], writes=[pmk])
                                p.pool(lambda e, pm=pm, pi=pi: e.tensor_tensor(out=pm[:, 0:n], in0=pm[:, 0:n], in1=rmask[:, pi, :], op=ALU.mult), reads=[pmk, "rmask"], writes=[pmk])
                                pt, ptk = pm, pmk
                            first = (i == 0)
                            lastk = (i == len(kts) - 1)
                            p.pe(lambda e, pt=pt, kt=kt, first=first, lastk=lastk: e.matmul(ps_o[:, 0:n], lhsT=V_[:, kt, :], rhs=pt[:, 0:n], start=first, stop=lastk),
                                 reads=[Vk, ptk], writes=[pok])
                            p.pe(lambda e, pt=pt, first=first, lastk=lastk: e.matmul(ps_d[:, 0:n], lhsT=onesb[:], rhs=pt[:, 0:n], start=first, stop=lastk),
                                 reads=["onesb", ptk], writes=[pdk])
                        p.dve(lambda e: e.reciprocal(out=rden[:, 0:n], in_=ps_d[:, 0:n]), reads=[pdk], writes=["rden"])
                        p.dve(lambda e: e.tensor_tensor(out=of[:, 0:n], in0=ps_o[:, 0:n], in1=rden[:, 0:n], op=ALU.mult), reads=[pok, "rden"], writes=["of"])
                        y, yk = rY.next()
                        p.pool(lambda e, y=y: e.tensor_tensor(out=y[:, 0:n], in0=of[:, 0:n], in1=S_[:, q0:q0 + n], op=ALU.mult), reads=["of", Sk], writes=[yk])
                        p.dma("sp", "yst%d" % yk[1], yT[mrow + h * 128:mrow + (h + 1) * 128, q0:q0 + n], y[:, 0:n], reads=[yk], writes=["yT"])
            p.barrier()
            p.release()

        attention("B")
        if stop_after == "gqa":
            break
        attention("D")
        if stop_after == "na":
            break

        p.mark()
        rtab = p.sb([128, 6, 128], F32, "rtab")
        p.dma("sp", "const", rtab[:], hc["ret_tabs"], writes=["rtab"])
        pidx = p.sb([128, 3], F32, "pidx")
        p.dma("sp", "const", pidx[:], hc["ret_pidx"], writes=["pidx"])
        LG = p.sb([128, 16], F32, "LG")
        p.dma("sp", "const", LG[:], ret_dec[l].partition_broadcast(128), writes=["LG"])
        p.act(lambda e: e.activation(out=LG[:], in_=LG[:], func=AF.Exp, scale=-1.0), reads=["LG"], writes=["LG"])
        p.act(lambda e: e.activation(out=LG[:], in_=LG[:], func=AF.Ln, bias=1.0, scale=1.0), reads=["LG"], writes=["LG"])
        p.dve(lambda e: e.tensor_scalar(out=LG[:], in0=LG[:], scalar1=-1.0, scalar2=None, op0=ALU.mult), reads=["LG"], writes=["LG"])
        Dm = p.sb([128, 16, 128], F32, "Dm")
        qdec = p.sb([128, 16, 128], F32, "qdec")
        kdec = p.sb([128, 16], F32, "kdec")
        cdec = p.sb([128, 16], F32, "cdec")
        for hd in range(16):
            d = hd // 8
            p.act(lambda e, hd=hd, d=d: e.activation(out=Dm[:, hd, :], in_=rtab[:, d, :], func=AF.Exp, scale=LG[:, hd:hd + 1]), reads=["rtab", "LG"], writes=["Dm"])
            p.dve(lambda e, hd=hd, d=d: e.tensor_tensor(out=Dm[:, hd, :], in0=Dm[:, hd, :], in1=rtab[:, 2 + d, :], op=ALU.mult), reads=["Dm", "rtab"], writes=["Dm"])
            p.act(lambda e, hd=hd, d=d: e.activation(out=qdec[:, hd, :], in_=rtab[:, 4 + d, :], func=AF.Exp, scale=LG[:, hd:hd + 1]), reads=["rtab", "LG"], writes=["qdec"])
            p.act(lambda e, hd=hd, d=d: e.activation(out=kdec[:, hd:hd + 1], in_=pidx[:, d:d + 1], func=AF.Exp, scale=LG[:, hd:hd + 1]), reads=["pidx", "LG"], writes=["kdec"])
            p.act(lambda e, hd=hd: e.activation(out=cdec[:, hd:hd + 1], in_=pidx[:, 2:3], func=AF.Exp, scale=LG[:, hd:hd + 1]), reads=["pidx", "LG"], writes=["cdec"])
        KT = [p.sb([128, TALL], BF16, f"rKT{i}") for i in range(2)]
        rK = Rot(KT, "rKT")
        QT = [p.sb([128, TALL], BF16, f"rQT{i}") for i in range(2)]
        rQ = Rot(QT, "rQT")
        VV = [p.sb([128, 18, 128], BF16, f"rVV{i}") for i in range(2)]
        rV = Rot(VV, "rVV")
        SG = [p.sb([128, TALL], BF16, f"rSG{i}") for i in range(2)]
        rS = Rot(SG, "rSG")
        Kd = [p.sb([128, 18, 128], BF16, f"Kd{i}") for i in range(2)]
        oacc = p.sb([128, 18, 128], F32, "oacc")
        osq = p.sb([128, 18, 128], F32, "osq")
        onb = p.sb([128, 18, 128], BF16, "onb")
        gst = p.sb([128, 4, 18], F32, "gst")
        attm = [p.sb([128, 128], BF16, f"attm{i}") for i in range(3)]
        rAt = Rot(attm, "attm")
        qdb = [p.sb([128, 128], BF16, f"qdb{i}") for i in range(3)]
        rQd = Rot(qdb, "qdb")
        qd_all = [p.sb([128, 18, 128], BF16, f"qdall{i}") for i in range(2)]
        attm_all = [p.sb([128, 18, 128], BF16, f"attmall{i}") for i in range(2)]
        S32 = [p.sb([128, 128], F32, f"S32{i}") for i in range(2)]
        Sbf = [p.sb([128, 128], BF16, f"Sbf{i}") for i in range(2)]
        yf = p.sb([128, 512], F32, "yf")
        yo = [p.sb([128, 512], BF16, f"ryo{i}") for i in range(3)]
        rY = Rot(yo, "ryo")
        rl = {}

        def load_ret(h):
            K_, Kk = rK.next()
            p.dma("sp", "K%d" % Kk[1], K_[:], kT["C"][h * 128:(h + 1) * 128, :], reads=["scr"], writes=[Kk])
            Q_, Qk = rQ.next()
            p.dma("sp", "Q%d" % Qk[1], Q_[:], qT["C"][h * 128:(h + 1) * 128, :], reads=["scr"], writes=[Qk])
            V_, Vk = rV.next()
            p.dma("sp", "V%d" % Vk[1], V_[:], vS["C"][:, h * 128:(h + 1) * 128].rearrange("(t p) d -> p t d", p=128), reads=["scr"], writes=[Vk])
            S_, Sk = rS.next()
            p.dma("sp", "S%d" % Sk[1], S_[:], sgT[2 * GW + h * 128:2 * GW + (h + 1) * 128, :], reads=["scr"], writes=[Sk])
            rl[h] = (K_, Kk, Q_, Qk, V_, Vk, S_, Sk)
        load_ret(0)
        oaccB = p.sb([128, 18, 128], F32, "oaccB")
        for h in range(8):
            K_, Kk, Q_, Qk, V_, Vk, S_, Sk = rl[h]
            if h + 1 < 8:
                load_ret(h + 1)
            for c in range(18):
                pt, ptk = rT.next()
                p.pe(lambda e, pt=pt, c=c: e.transpose(out=pt[:, 0, :], in_=K_[:, c * 128:(c + 1) * 128], identity=ident[:]), reads=[Kk, "ident"], writes=[ptk])
                for d in range(2):
                    hd = d * 8 + h
                    p.dve(lambda e, pt=pt, c=c, d=d, hd=hd: e.tensor_scalar(out=Kd[d][:, c, :], in0=pt[:, 0, :], scalar1=kdec[:, hd:hd + 1], scalar2=None, op0=ALU.mult),
                          reads=[ptk, "kdec"], writes=[("Kd", d)])
            orders = [list(range(18)), [1, 0] + list(range(17, 1, -1))]
            for c in range(18):
                if last and c < 2:
                    continue
                for d in range(2):
                    hd = d * 8 + h
                    p.pool(lambda e, c=c, d=d, hd=hd: e.tensor_tensor(out=qd_all[d][:, c, :], in0=Q_[:, c * 128:(c + 1) * 128], in1=qdec[:, hd, :], op=ALU.mult),
                           reads=[Qk, "qdec"], writes=[("qd", d, c)])
                ps, pk = rA.next()
                p.pe(lambda e, ps=ps, c=c: e.matmul(ps[:, 0:128], lhsT=K_[:, c * 128:(c + 1) * 128], rhs=Q_[:, c * 128:(c + 1) * 128], start=True, stop=True),
                     reads=[Kk, Qk], writes=[pk])
                for d in range(2):
                    hd = d * 8 + h
                    p.dve(lambda e, ps=ps, c=c, d=d, hd=hd: e.tensor_tensor(out=attm_all[d][:, c, :], in0=ps[:, 0:128], in1=Dm[:, hd, :], op=ALU.mult),
                          reads=[pk, "Dm"], writes=[("attm", d, c)])
            for ci in range(18):
              for d in range(2):
                hd = d * 8 + h
                c = orders[d][ci]
                s32, sbf = S32[d], Sbf[d]
                s32k, sbfk = ("S32", d), ("Sbf", d)
                if True:
                    need_o = not (last and c < 2)
                    if need_o:
                        po, pok = rB.next()
                        p.pe(lambda e, po=po, c=c, ci=ci, d=d: e.matmul(po[:, 0:128], lhsT=attm_all[d][:, c, :], rhs=V_[:, c, :], start=True, stop=(ci == 0)), reads=[("attm", d, c), Vk], writes=[pok])
                        if ci > 0:
                            p.pe(lambda e, po=po, c=c, d=d, sbf=sbf: e.matmul(po[:, 0:128], lhsT=qd_all[d][:, c, :], rhs=sbf[:], start=False, stop=True), reads=[("qd", d, c), sbfk], writes=[pok])
                        if d == 0:
                            p.act(lambda e, po=po, c=c: e.activation(out=oacc[:, c, :], in_=po[:, 0:128], func=AF.Copy), reads=[pok], writes=["oacc"])
                        else:
                            p.act(lambda e, po=po, c=c: e.activation(out=oaccB[:, c, :], in_=po[:, 0:128], func=AF.Copy), reads=[pok], writes=["oaccB"])
                    if ci < 17:
                        pkv, pkvk = rA.next()
                        p.pe(lambda e, pkv=pkv, c=c, d=d: e.matmul(pkv[:, 0:128], lhsT=Kd[d][:, c, :], rhs=V_[:, c, :], start=True, stop=True), reads=[("Kd", d), Vk], writes=[pkvk])
                        if ci == 0:
                            p.dve(lambda e, pkv=pkv, s32=s32: e.tensor_copy(out=s32[:], in_=pkv[:, 0:128]), reads=[pkvk], writes=[s32k])
                        else:
                            p.dve(lambda e, pkv=pkv, s32=s32, hd=hd: e.scalar_tensor_tensor(out=s32[:], in0=s32[:], scalar=cdec[:, hd:hd + 1], in1=pkv[:, 0:128], op0=ALU.mult, op1=ALU.add),
                                  reads=[pkvk, s32k, "cdec"], writes=[s32k])
                        p.act(lambda e, s32=s32, sbf=sbf: e.activation(out=sbf[:], in_=s32[:], func=AF.Copy), reads=[s32k], writes=[sbfk])
            c_lo = 2 if last else 0
            ncn = 18 - c_lo
            ov = oacc[:, c_lo:18, :]
            p.dve(lambda e, ov=ov: e.tensor_tensor(out=ov, in0=ov, in1=oaccB[:, c_lo:18, :], op=ALU.add), reads=["oacc", "oaccB"], writes=["oacc"])
            p.dve(lambda e, ov=ov: e.tensor_reduce(out=gst[:, 0, c_lo:18], in_=ov, axis=AX.X, op=ALU.add), reads=["oacc"], writes=["gst"])
            p.act(lambda e, ov=ov: e.activation(out=osq[:, c_lo:18, :], in_=ov, func=AF.Square), reads=["oacc"], writes=["osq"])
            p.dve(lambda e: e.tensor_reduce(out=gst[:, 1, c_lo:18], in_=osq[:, c_lo:18, :], axis=AX.X, op=ALU.add), reads=["osq"], writes=["gst"])
            p.dve(lambda e: e.tensor_scalar(out=gst[:, 0, :], in0=gst[:, 0, :], scalar1=1.0 / HD, scalar2=None, op0=ALU.mult), reads=["gst"], writes=["gst"])
            p.dve(lambda e: e.tensor_tensor(out=gst[:, 2, :], in0=gst[:, 0, :], in1=gst[:, 0, :], op=ALU.mult), reads=["gst"], writes=["gst"])
            p.dve(lambda e: e.scalar_tensor_tensor(out=gst[:, 1, :], in0=gst[:, 1, :], scalar=1.0 / HD, in1=gst[:, 2, :], op0=ALU.mult, op1=ALU.subtract), reads=["gst"], writes=["gst"])
            p.act(lambda e: e.activation(out=gst[:, 1, c_lo:18], in_=gst[:, 1, c_lo:18], func=AF.Sqrt, bias=LEPS, scale=1.0), reads=["gst"], writes=["gst"])
            p.dve(lambda e: e.reciprocal(out=gst[:, 1, c_lo:18], in_=gst[:, 1, c_lo:18]), reads=["gst"], writes=["gst"])
            p.dve(lambda e, ov=ov: e.tensor_tensor(out=ov, in0=ov, in1=gst[:, 0, c_lo:18].unsqueeze(2).to_broadcast([128, ncn, 128]), op=ALU.subtract), reads=["oacc", "gst"], writes=["oacc"])
            p.dve(lambda e, ov=ov: e.tensor_tensor(out=onb[:, c_lo:18, :], in0=ov, in1=gst[:, 1, c_lo:18].unsqueeze(2).to_broadcast([128, ncn, 128]), op=ALU.mult), reads=["oacc", "gst"], writes=["onb"])
            for c4 in range(c_lo, 18, 4):
                nn = min(4, 18 - c4)
                pt, ptk = rT.next()
                for cc in range(nn):
                    p.pe(lambda e, pt=pt, cc=cc, c4=c4: e.transpose(out=pt[:, cc, :], in_=onb[:, c4 + cc, :], identity=ident[:]), reads=["onb", "ident"], writes=[ptk])
                w = nn * 128
                p.dve(lambda e, pt=pt, w=w, nn=nn: e.tensor_scalar(out=yf[:, 0:w], in0=pt[:, 0:nn, :].rearrange("p a b -> p (a b)"), scalar1=small[:, 24 + h:25 + h], scalar2=small[:, 32 + h:33 + h], op0=ALU.mult, op1=ALU.add),
                      reads=[ptk, "small"], writes=["yf"])
                y, yk = rY.next()
                p.pool(lambda e, y=y, w=w, c4=c4: e.tensor_tensor(out=y[:, 0:w], in0=yf[:, 0:w], in1=S_[:, c4 * 128:c4 * 128 + w], op=ALU.mult), reads=["yf", Sk], writes=[yk])
                p.dma("sp", "yst%d" % yk[1], yT[2 * GW + h * 128:2 * GW + (h + 1) * 128, c4 * 128:c4 * 128 + w], y[:, 0:w], reads=[yk], writes=["yT"])
        p.barrier()
        p.release()
        if stop_after == "ret":
            break

        for (tok0, T) in groups:
            p.mark()
            yTg = p.sb([128, 32, T], BF16, "yTg")
            p.dma("sp", "ytg", yTg[:], yT[:, tok0:tok0 + T].rearrange("(k p) t -> p k t", p=128), reads=["yT"], writes=["yTg"])
            gtbs = {}
            for row in ((1, 0) if (tok0 == 0 and not last) else (0,)):
                g_ = p.sb([128, D], F32, f"gtb{row}")
                p.dma("sp", "modld%d" % row, g_[:], mod[l, row, 2 * D:3 * D].partition_broadcast(128), reads=["mod"], writes=[("gtb", row)])
                gtbs[row] = g_
            wbuf = [p.sb([128, 32, 512], BF16, f"wo{i}") for i in range(2)]
            rW = Rot(wbuf, "wo")
            xs = [p.sb([128, 512], F32, f"xs{i}") for i in range(3)]
            rXs = Rot(xs, "xs")
            ts = [p.sb([128, 512], F32, f"ts{i}") for i in range(2)]
            rTs = Rot(ts, "ts")
            tiles = [tt for tt in range(T // 128) if not (last and tok0 + tt * 128 < CTX)]
            cur_row = None
            for nb in range(8):
                wb, wk = rW.next()
                p.dma("pool", "w%d" % wk[1], wb[:], w_out[l, :, nb * 512:(nb + 1) * 512].rearrange("(k p) n -> p k n", p=128), writes=[wk])
                for tt in tiles:
                    g0 = tok0 + tt * 128
                    is_ctx = g0 < CTX
                    row = 1 if is_ctx else 0
                    gtb = gtbs[row]
                    ps, pk = rA.next()
                    for k in range(32):
                        p.pe(lambda e, ps=ps, k=k, tt=tt, wb=wb: e.matmul(ps[:], lhsT=yTg[:, k, tt * 128:(tt + 1) * 128], rhs=wb[:, k, :], start=(k == 0), stop=(k == 31)),
                             reads=["yTg", wk], writes=[pk])
                    if l == 0:
                        src = ctx_in[g0:g0 + 128, nb * 512:(nb + 1) * 512] if is_ctx else x_in[g0 - CTX:g0 - CTX + 128, nb * 512:(nb + 1) * 512]
                    else:
                        src = xres[g0:g0 + 128, nb * 512:(nb + 1) * 512]
                    x_, xk = rXs.next()
                    p.dma("act", "xs%d" % xk[1], x_[:], src, reads=["xres_r"], writes=[xk])
                    t_, tk = rTs.next()
                    p.dve(lambda e, t_=t_, ps=ps, nb=nb: e.tensor_tensor(out=t_[:], in0=ps[:], in1=gtb[:, nb * 512:(nb + 1) * 512], op=ALU.mult), reads=[pk, ("gtb", row)], writes=[tk])
                    p.dve(lambda e, t_=t_, x_=x_: e.tensor_tensor(out=x_[:], in0=x_[:], in1=t_[:], op=ALU.add), reads=[tk, xk], writes=[xk])
                    p.dma("sp", "xst%d" % xk[1], xres[g0:g0 + 128, nb * 512:(nb + 1) * 512], x_[:], reads=[xk], writes=["xres_w"])
            p.barrier()
            p.release()
        if stop_after == "l0":
            break

    if stop_after is None:
        p.mark()
        fgb = p.sb([128, D], F32, "fgb")
        p.dma("sp", "const", fgb[:], final_g.partition_broadcast(128), writes=["fgb"])
        xt = [p.sb([128, D], F32, f"fxt{i}") for i in range(3)]
        rX = Rot(xt, "fxt")
        junk = p.sb([128, D], BF16, "fjunk")
        st = p.sb([128, 8], F32, "fst")
        for tt in range(SEQ // 128):
            xb_, xk = rX.next()
            p.dma("pool", "x%d" % xk[1], xb_[:], xres[CTX + tt * 128:CTX + (tt + 1) * 128, :], reads=["xres_w"], writes=[xk])
            p.act(lambda e, xb_=xb_: e.activation(out=junk[:], in_=xb_[:], func=AF.Square, accum_out=st[:, 0:1]), reads=[xk], writes=["junk", "st0"])
            p.act(lambda e: e.activation(out=st[:, 1:2], in_=st[:, 0:1], func=AF.Sqrt, bias=NEPS, scale=1.0 / D), reads=["st0"], writes=["st1"])
            p.dve(lambda e: e.reciprocal(out=st[:, 2:3], in_=st[:, 1:2]), reads=["st1"], writes=["st2"])
            p.dve(lambda e, xb_=xb_: e.scalar_tensor_tensor(out=xb_[:], in0=xb_[:], scalar=st[:, 2:3], in1=fgb[:], op0=ALU.mult, op1=ALU.mult),
                  reads=[xk, "st2", "fgb"], writes=[xk])
            p.dma("sp", "ost%d" % xk[1], out[tt * 128:(tt + 1) * 128, :], xb_[:], reads=[xk], writes=["out"])
        p.release()
    p.barrier()
    stats = p.emit()
    nc_ctx.__exit__(None, None, None)
    return nc, cs, stats, dict(mod=mod, xres=xres, uT=uT, sgT=sgT, qT=qT, kT=kT, vS=vS, yT=yT)


def make_in_maps(inputs, cs):
    bf = ml_dtypes.bfloat16
    f = lambda a: np.ascontiguousarray(np.asarray(a, dtype=np.float32))
    shared = {
        "ada_w": f(inputs["ada_w"]), "ada_b": f(inputs["ada_b"]), "norm_g": f(inputs["norm_g"]),
        "w_in": f(inputs["w_in"]), "conv_w": f(inputs["conv_w"]), "conv_b": f(inputs["conv_b"]),
        "conv_ln_g": f(inputs["conv_ln_g"]), "conv_ln_b": f(inputs["conv_ln_b"]),
        "gqa_qn_g": f(inputs["gqa_qn_g"]), "gqa_kn_g": f(inputs["gqa_kn_g"]),
        "ret_dec": f(np.concatenate([inputs["ret_decay_fwd"], inputs["ret_decay_bwd"]], axis=1)),
        "ret_gn_g": f(inputs["ret_gn_g"]), "ret_gn_b": f(inputs["ret_gn_b"]),
        "na_bT": np.ascontiguousarray(np.transpose(np.asarray(inputs["na_bias"], np.float32)[:, :, ::-1, :], (0, 3, 1, 2))),
        "w_out": f(inputs["w_out"]), "final_g": f(inputs["final_g"]),
    }
    for k, v in cs.items():
        shared["k_" + k] = v
    maps = []
    for core in range(8):
        b = core // 2
        m = dict(shared)
        m["x"] = f(inputs["x"][b])
        m["ctx"] = f(inputs["ctx"][b])
        m["c2"] = f(np.stack([inputs["c"][b], inputs["c_ctx"]], axis=0))
        maps.append(m)
    return maps


_CACHE = {}


def kernel(**inputs):
    if "prog" not in _CACHE:
        _CACHE["prog"] = build_program()
    nc, cs, stats, _ = _CACHE["prog"]
    maps = make_in_maps(inputs, cs)
    res = run_bass_kernel_spmd(nc, maps, core_ids=list(range(8)))
    outs = [res.results[2 * b]["out"] for b in range(4)]
    return np.stack(outs, axis=0).astype(np.float32)
```
